# Optimizing a Trainium2 kernel written in Bass

```python
import math
import jax, jax.numpy as jnp
from jax import lax
import numpy as np


D_MODEL = 1024
BATCH = 16
SEQ = 4096
DEPTH = 4
DEC_BATCH = 16
DEC_SEQ = 2048
PAST_LEN = 128

N_META = 16
HEAD_DIM = 64
BLOCK = 128
WINDOW = 128
ROPE_THETA = 10000.0
EPS = 1e-6
NEG_INF = -1e30

A_WIDTH = D_MODEL // 2
A_HEADS = A_WIDTH // HEAD_DIM
A_KV_HEADS = A_HEADS // 4
A_GROUP = A_HEADS // A_KV_HEADS
B_WIDTH = D_MODEL - A_WIDTH
B_VDIM = 2 * HEAD_DIM
B_HEADS = B_WIDTH // B_VDIM
MIX_WIDTH = A_WIDTH + B_WIDTH

A_Q = A_HEADS * HEAD_DIM
A_KV = A_KV_HEADS * HEAD_DIM
B_QK = 2 * B_HEADS * HEAD_DIM
IN_SIZES = (A_Q, A_KV, A_KV, A_WIDTH, B_QK, B_QK, B_WIDTH, B_WIDTH)
IN_WIDTH = A_Q + 2 * A_KV + A_WIDTH + 2 * B_QK + 2 * B_WIDTH

kernel_name = 'hymba_style_window_gqa_diff_attn_encoder'


def rmsnorm(x, g):
    xf = x.astype(jnp.float32)
    y = xf * lax.rsqrt(jnp.mean(xf * xf, axis=-1, keepdims=True) + EPS) * g.astype(jnp.float32)
    return y.astype(x.dtype)


def rope_tables(length):
    inv_freq = 1.0 / (ROPE_THETA ** (jnp.arange(0, HEAD_DIM, 2, dtype=jnp.float32) / HEAD_DIM))
    ang = jnp.arange(length, dtype=jnp.float32)[:, None] * inv_freq[None, :]
    ang = jnp.concatenate([ang, ang], axis=-1)
    return jnp.cos(ang), jnp.sin(ang)


def apply_rope(x, cos, sin):
    shp = (x.shape[1],) + (1,) * (x.ndim - 3) + (HEAD_DIM,)
    c, s = cos.reshape(shp), sin.reshape(shp)
    xf = x.astype(jnp.float32)
    half = HEAD_DIM // 2
    rot = jnp.concatenate([-xf[..., half:], xf[..., :half]], axis=-1)
    return (xf * c + rot * s).astype(x.dtype)


def window_gqa_attention(q, k, v, sink):
    bsz, length = q.shape[0], q.shape[1]
    s = length - N_META
    nb = s // BLOCK
    scale = HEAD_DIM ** -0.5
    q = q.reshape(bsz, length, A_KV_HEADS, A_GROUP, HEAD_DIM)
    qm, qr = q[:, :N_META], q[:, N_META:]
    km, kr = k[:, :N_META], k[:, N_META:]
    vm, vr = v[:, :N_META], v[:, N_META:]
    sink_f = sink.astype(jnp.float32).reshape(A_KV_HEADS, A_GROUP)

    qb = qr.reshape(bsz, nb, BLOCK, A_KV_HEADS, A_GROUP, HEAD_DIM)

    def band(t):
        tp = jnp.pad(t, ((0, 0), (BLOCK, BLOCK), (0, 0), (0, 0)))
        tp = tp.reshape(bsz, nb + 2, BLOCK, A_KV_HEADS, HEAD_DIM)
        return jnp.concatenate([tp[:, :-2], tp[:, 1:-1], tp[:, 2:]], axis=2)

    kb, vb = band(kr), band(vr)
    qi = jnp.arange(nb)[:, None] * BLOCK + jnp.arange(BLOCK)[None, :]
    kj = (jnp.arange(nb)[:, None] - 1) * BLOCK + jnp.arange(3 * BLOCK)[None, :]
    rel = kj[:, None, :] - qi[:, :, None]
    valid = (jnp.abs(rel) <= WINDOW) & (kj[:, None, :] >= 0) & (kj[:, None, :] < s)

    s_meta = jnp.einsum('bnqkgd,bmkd->bnkgqm', qb, km, preferred_element_type=jnp.float32) * scale
    s_band = jnp.einsum('bnqkgd,bnukd->bnkgqu', qb, kb, preferred_element_type=jnp.float32) * scale
    s_band = jnp.where(valid[None, :, None, None], s_band, NEG_INF)
    s_sink = jnp.broadcast_to(sink_f[None, None, :, :, None, None], s_meta.shape[:-1] + (1,))
    p = jax.nn.softmax(jnp.concatenate([s_meta, s_band, s_sink], axis=-1), axis=-1).astype(v.dtype)
    o = (jnp.einsum('bnkgqm,bmkd->bnqkgd', p[..., :N_META], vm)
         + jnp.einsum('bnkgqu,bnukd->bnqkgd', p[..., N_META:N_META + 3 * BLOCK], vb))
    o_real = o.reshape(bsz, s, A_WIDTH)

    kr0, vr0 = kr[:, :BLOCK], vr[:, :BLOCK]
    mpos = jnp.arange(N_META)
    rpos = N_META + jnp.arange(BLOCK)
    mvalid = (rpos[None, :] - mpos[:, None]) <= WINDOW
    sm_meta = jnp.einsum('bmkgd,bnkd->bkgmn', qm, km, preferred_element_type=jnp.float32) * scale
    sm_real = jnp.einsum('bmkgd,bukd->bkgmu', qm, kr0, preferred_element_type=jnp.float32) * scale
    sm_real = jnp.where(mvalid[None, None, None], sm_real, NEG_INF)
    sm_sink = jnp.broadcast_to(sink_f[None, :, :, None, None], sm_meta.shape[:-1] + (1,))
    pm = jax.nn.softmax(jnp.concatenate([sm_meta, sm_real, sm_sink], axis=-1), axis=-1).astype(v.dtype)
    om = (jnp.einsum('bkgmn,bnkd->bmkgd', pm[..., :N_META], vm)
          + jnp.einsum('bkgmu,bukd->bmkgd', pm[..., N_META:N_META + BLOCK], vr0))
    o_meta = om.reshape(bsz, N_META, A_WIDTH)
    return jnp.concatenate([o_meta, o_real], axis=1)


def diff_attention(q, k, v, lam, lambda_init, subln_g):
    bsz, length = q.shape[0], q.shape[1]
    s = length - N_META
    nb = s // BLOCK
    scale = HEAD_DIM ** -0.5

    def attend(qblk):
        sc = jnp.einsum('bqhcd,bkhcd->bhcqk', qblk, k, preferred_element_type=jnp.float32) * scale
        p = jax.nn.softmax(sc, axis=-1)
        a = p[:, :, 0] - lam * p[:, :, 1]
        return jnp.einsum('bhqk,bkhe->bqhe', a.astype(v.dtype), v)

    o_meta = attend(q[:, :N_META])
    qb = q[:, N_META:].reshape(bsz, nb, BLOCK, B_HEADS, 2, HEAD_DIM).swapaxes(0, 1)
    o_real = lax.map(attend, qb).swapaxes(0, 1).reshape(bsz, s, B_HEADS, B_VDIM)
    o = jnp.concatenate([o_meta, o_real], axis=1)
    o = rmsnorm(o, subln_g) * (1.0 - lambda_init)
    return o.reshape(bsz, length, B_WIDTH)


def layer(x, cos, sin, w_in, w_out, pre_g, post_g, sink, lq1, lk1, lq2, lk2, subln_g, lambda_init):
    bsz, length, _ = x.shape
    h = rmsnorm(x, pre_g)
    proj = jnp.einsum('bld,de->ble', h, w_in)
    parts = []
    off = 0
    for size in IN_SIZES:
        parts.append(proj[..., off:off + size])
        off += size
    aq, ak, av, ag, bq, bk, bv, bg = parts

    aq = apply_rope(aq.reshape(bsz, length, A_HEADS, HEAD_DIM), cos, sin)
    ak = apply_rope(ak.reshape(bsz, length, A_KV_HEADS, HEAD_DIM), cos, sin)
    av = av.reshape(bsz, length, A_KV_HEADS, HEAD_DIM)
    oa = window_gqa_attention(aq, ak, av, sink) * jax.nn.silu(ag)

    f32 = jnp.float32
    lam = (jnp.exp(jnp.sum(lq1.astype(f32) * lk1.astype(f32)))
           - jnp.exp(jnp.sum(lq2.astype(f32) * lk2.astype(f32))) + lambda_init)
    bq = apply_rope(bq.reshape(bsz, length, B_HEADS, 2, HEAD_DIM), cos, sin)
    bk = apply_rope(bk.reshape(bsz, length, B_HEADS, 2, HEAD_DIM), cos, sin)
    bv = bv.reshape(bsz, length, B_HEADS, B_VDIM)
    ob = diff_attention(bq, bk, bv, lam, lambda_init, subln_g) * jax.nn.silu(bg)

    o = jnp.einsum('ble,ed->bld', jnp.concatenate([oa, ob], axis=-1), w_out)
    return x + rmsnorm(o, post_g)


def encode(x, meta_tokens, w_in, w_out, pre_norm_g, post_norm_g, sink_logits,
           lambda_q1, lambda_k1, lambda_q2, lambda_k2, subln_g):
    bsz, s, _ = x.shape
    meta = jnp.broadcast_to(meta_tokens[None].astype(x.dtype), (bsz, N_META, D_MODEL))
    h = jnp.concatenate([meta, x], axis=1)
    cos, sin = rope_tables(N_META + s)
    for l in range(DEPTH):
        lambda_init = 0.8 - 0.6 * math.exp(-0.3 * l)
        h = layer(h, cos, sin, w_in[l], w_out[l], pre_norm_g[l], post_norm_g[l], sink_logits[l],
                  lambda_q1[l], lambda_k1[l], lambda_q2[l], lambda_k2[l], subln_g[l], lambda_init)
    return h[:, N_META:]


def setup_inputs(seed: int = 0) -> dict:
    key = jax.random.key(seed)
    ks = jax.random.split(key, 14)
    f32 = jnp.float32
    return {
        'x_prompt': jax.random.normal(ks[0], (BATCH, SEQ, D_MODEL), f32),
        'x_sample': jax.random.normal(ks[1], (DEC_BATCH, DEC_SEQ, D_MODEL), f32),
        'meta_tokens': jax.random.normal(ks[2], (N_META, D_MODEL), f32),
        'w_in': jax.random.normal(ks[3], (DEPTH, D_MODEL, IN_WIDTH), f32) * D_MODEL ** -0.5,
        'w_out': jax.random.normal(ks[4], (DEPTH, MIX_WIDTH, D_MODEL), f32) * MIX_WIDTH ** -0.5,
        'pre_norm_g': 1.0 + 0.02 * jax.random.normal(ks[5], (DEPTH, D_MODEL), f32),
        'post_norm_g': 1.0 + 0.02 * jax.random.normal(ks[6], (DEPTH, D_MODEL), f32),
        'sink_logits': 0.5 * jax.random.normal(ks[7], (DEPTH, A_HEADS), f32),
        'lambda_q1': 0.1 * jax.random.normal(ks[8], (DEPTH, HEAD_DIM), f32),
        'lambda_k1': 0.1 * jax.random.normal(ks[9], (DEPTH, HEAD_DIM), f32),
        'lambda_q2': 0.1 * jax.random.normal(ks[10], (DEPTH, HEAD_DIM), f32),
        'lambda_k2': 0.1 * jax.random.normal(ks[11], (DEPTH, HEAD_DIM), f32),
        'subln_g': 1.0 + 0.02 * jax.random.normal(ks[12], (DEPTH, B_VDIM), f32),
    }


def reference(x_prompt, x_sample, meta_tokens, w_in, w_out, pre_norm_g, post_norm_g, sink_logits,
              lambda_q1, lambda_k1, lambda_q2, lambda_k2, subln_g):
    y_prompt = encode(x_prompt, meta_tokens, w_in, w_out, pre_norm_g, post_norm_g, sink_logits,
                      lambda_q1, lambda_k1, lambda_q2, lambda_k2, subln_g)
    y_sample = encode(x_sample, meta_tokens, w_in, w_out, pre_norm_g, post_norm_g, sink_logits,
                      lambda_q1, lambda_k1, lambda_q2, lambda_k2, subln_g)
    return (y_prompt, y_sample)
```

```python
import math
from contextlib import ExitStack

import numpy as np
import concourse.bass as bass
import concourse.mybir as mybir
from concourse.bass_utils import run_bass_kernel_spmd

F32 = mybir.dt.float32
BF16 = mybir.dt.bfloat16
AF = mybir.ActivationFunctionType
ALU = mybir.AluOpType

D = 1024
NMETA = 16
EPS = 1e-6
SCALE = 0.125
ROPE_THETA = 10000.0
N_CORES = 8


class Op:
    __slots__ = ("eng", "fn", "deps", "sig", "cnt", "dma", "vsem", "vval")

    def __init__(self, eng, fn, dma):
        self.eng = eng
        self.fn = fn
        self.deps = ()
        self.sig = False
        self.cnt = 0
        self.dma = dma
        self.vsem = None
        self.vval = 0


class Rec:
    ENGS = ("pe", "act", "dve", "pool", "sp")

    def __init__(self, n_dma_sems):
        self.ops = {e: [] for e in self.ENGS}
        self.res_w = {}
        self.res_r = {}
        self.n_dma = n_dma_sems
        self.dma_rr = {e: 0 for e in n_dma_sems}
        self.dma_last = {}
        self.dma_val = {}

    def emit(self, eng, fn, reads=(), writes=(), dma=False):
        op = Op(eng, fn, dma)
        deps = []
        seen = set()
        res_w, res_r = self.res_w, self.res_r
        for r in reads:
            w = res_w.get(r)
            if w is not None and id(w) not in seen:
                seen.add(id(w)); deps.append(w)
        for r in writes:
            w = res_w.get(r)
            if w is not None and id(w) not in seen:
                seen.add(id(w)); deps.append(w)
            for rd in res_r.get(r, ()):
                if id(rd) not in seen:
                    seen.add(id(rd)); deps.append(rd)
        if dma:
            k = (eng, self.dma_rr[eng])
            self.dma_rr[eng] = (self.dma_rr[eng] + 1) % self.n_dma[eng]
            prev = self.dma_last.get(k)
            if prev is not None and id(prev) not in seen:
                seen.add(id(prev)); deps.append(prev)
            self.dma_last[k] = op
            self.dma_val[k] = self.dma_val.get(k, 0) + 16
            op.vsem = k
            op.vval = self.dma_val[k]
        keep = []
        for d in deps:
            if d.dma or dma:
                keep.append(d)
            elif d.eng == eng and eng == "pe":
                continue
            else:
                keep.append(d)
        op.deps = keep
        for r in reads:
            res_r.setdefault(r, []).append(op)
        for r in writes:
            res_w[r] = op
            res_r[r] = []
        self.ops[eng].append(op)
        return op

    def finalize(self):
        for e in self.ENGS:
            for op in self.ops[e]:
                for d in op.deps:
                    if not d.dma:
                        d.sig = True
        for e in self.ENGS:
            c = 0
            for op in self.ops[e]:
                if op.sig and not op.dma:
                    c += 1
                    op.cnt = c

    def play(self, eng, handle, esems, dsems):
        waited = {}
        for op in self.ops[eng]:
            for d in op.deps:
                if d.dma:
                    key = ("d",) + d.vsem
                    sem = dsems[d.vsem]
                    val = d.vval
                else:
                    key = ("e", d.eng)
                    sem = esems[d.eng]
                    val = d.cnt
                if waited.get(key, 0) < val:
                    handle.wait_ge(sem, val)
                    waited[key] = val
            ins = op.fn(handle)
            if op.dma:
                ins.then_inc(dsems[op.vsem], 16)
            elif op.sig:
                ins.then_inc(esems[eng], 1)
        if eng == "sp":
            for k, v in self.dma_val.items():
                handle.wait_ge(dsems[k], v)


class Cfg:
    def __init__(self, n_layers=4, n_prompt=2, s_prompt=4096, n_sample=2, s_sample=2048, lambda_layers=None):
        self.n_layers = n_layers
        self.n_prompt = n_prompt
        self.s_prompt = s_prompt
        self.n_sample = n_sample
        self.s_sample = s_sample


def build_program(cfg):
    nc = bass.Bass("TRN2", target_bir_lowering=False)
    NL = cfg.n_layers
    SP_, SS_ = cfg.s_prompt, cfg.s_sample
    nP, nS = cfg.n_prompt, cfg.n_sample
    LPP, LPS = SP_ + NMETA, SS_ + NMETA
    LPM = max(LPP, LPS)
    NTM = max(SP_, SS_) // 128 + 1

    def din(name, shape, dt=F32):
        return nc.dram_tensor(name, list(shape), dt, kind="ExternalInput").ap()

    xp = din("xp", [nP, SP_, D])
    xs = din("xs", [nS, SS_, D])
    meta = din("meta", [NMETA, D])
    w_in = din("w_in", [NL, D, 3328])
    w_out = din("w_out", [NL, D, D])
    pre_g = din("pre_g", [NL, D])
    post_g = din("post_g", [NL, D])
    sink = din("sink", [NL, 8])
    lq1 = din("lq1", [NL, 64]); lk1 = din("lk1", [NL, 64])
    lq2 = din("lq2", [NL, 64]); lk2 = din("lk2", [NL, 64])
    subln = din("subln", [NL, 128])
    ropeP = din("ropeP", [128, 2, LPP])
    ropeS = din("ropeS", [128, 2, LPS])
    cident = din("cident", [128, 128])
    cmasks = din("cmasks", [128, 272])
    DBG = getattr(cfg, "debug", False)
    if DBG:
        dbg = {n: nc.dram_tensor("dbg_" + n, sh, dt, kind="ExternalOutput").ap() for n, sh, dt in (
            ("KTB", [128, 4, LPM], BF16), ("KTA", [128, LPM], BF16), ("VB", [128, NTM, 512], BF16), ("VA", [128, NTM, 128], BF16),
            ("oT", [128, 8, 256], BF16))}
    yp = nc.dram_tensor("yp", [nP, SP_, D], F32, kind="ExternalOutput").ap()
    ys = nc.dram_tensor("ys", [nS, SS_, D], F32, kind="ExternalOutput").ap()

    seqs = []
    row = 0
    for i in range(max(nP, nS)):
        if i < nP:
            seqs.append(dict(kind="p", idx=i, S=SP_, base=row)); row += LPP
        if i < nS:
            seqs.append(dict(kind="s", idx=i, S=SS_, base=row)); row += LPS
    tot_rows = row
    xscr = [nc.dram_tensor("xscrA", [tot_rows, D], F32).ap(), nc.dram_tensor("xscrB", [tot_rows, D], F32).ap()]

    R = Rec({"sp": 12, "pool": 6})
    es = ExitStack()

    def sb(name, shape, dt):
        return es.enter_context(nc.sbuf_tensor(name, list(shape), dt))

    def ps(name, shape, dt):
        return es.enter_context(nc.psum_tensor(name, list(shape), dt))

    KTB = sb("KTB", [128, 4, LPM], BF16)
    VB = sb("VB", [128, NTM, 512], BF16)
    KTA = sb("KTA", [128, LPM], BF16)
    VA = sb("VA", [128, NTM, 128], BF16)
    Wkv = sb("Wkv", [128, 8, 1280], BF16)
    Wq = sb("Wq", [128, 8, 2048], BF16)
    Wo = sb("Wo", [128, 8, 1024], BF16)
    gcol = sb("gcol", [128, 8], F32)
    gpost = sb("gpost", [128, D], F32)
    ident = sb("ident", [128, 128], BF16)
    masks = sb("masks", [128, 272], BF16)
    onesg = sb("onesg", [128, 2, 128], BF16)
    onesb = sb("onesb", [128, 128], BF16)
    onesf = sb("onesf", [128, 128], F32)
    Fp = [sb(f"F{i}", [128, D], F32) for i in range(4)]
    Bp = [sb(f"B{i}", [128, D], BF16) for i in range(4)]
    hT = sb("hT", [128, 8, 256], BF16)
    tabs = sb("tabs", [128, 2, 256], F32)
    QTA = sb("QTA", [128, 2, 4, 256], BF16)
    QTB = sb("QTB", [128, 2, 4, 2, 256], BF16)
    GA = sb("GA", [128, 4, 256], BF16)
    GB = sb("GB", [128, 2, 4, 256], BF16)
    oT = sb("oT", [128, 8, 256], BF16)
    small = sb("small", [128, 48], F32)
    sinkbc = sb("sinkbc", [128, 4], F32)

    C_EPS, C_SS0, C_SS1, C_LN0, C_LN1, C_RS0, C_RS1 = 0, 1, 2, 3, 4, 5, 6
    C_SS2, C_LN2, C_RS2 = 8, 10, 12
    C_D1, C_D2, C_E1, C_E2, C_T, C_NLAM, C_SG, C_SGR, C_NEG1 = 20, 21, 22, 23, 24, 25, 26, 27, 28
    gcolkv = small[:, 32:40]

    psS = ps("psS", [128, 1024], F32)
    psOO = ps("psOO", [128, 1024], F32)
    psSS = ps("psSS", [128, 1024], F32)
    psP = ps("psP", [128, 1024], F32)
    psOb = [psOO[:, 0:512], psOO[:, 512:1024]]
    psSb = [psSS[:, 0:512], psSS[:, 512:1024]]
    psTb = [psOO[:, 0:512].bitcast(BF16).rearrange("p (k r) -> p k r", k=8),
            psOO[:, 512:1024].bitcast(BF16).rearrange("p (k r) -> p k r", k=8)]

    emit = R.emit

    def MM(out, lhsT, rhs, start, stop, reads, writes):
        emit("pe", lambda e: e.matmul(out, lhsT=lhsT, rhs=rhs, start=start, stop=stop), reads, writes)

    def TR(out, in_, idn, reads, writes):
        emit("pe", lambda e: e.transpose(out=out, in_=in_, identity=idn), reads, writes)

    def ACTV(out, in_, func, reads, writes, scale=1.0, bias=None, accum_out=None):
        kw = {}
        if bias is not None:
            kw["bias"] = bias
        if accum_out is not None:
            kw["accum_out"] = accum_out
        emit("act", lambda e: e.activation(out=out, in_=in_, func=func, scale=scale, **kw), reads, writes)

    def ACOPY(out, in_, reads, writes):
        emit("act", lambda e: e.copy(out=out, in_=in_), reads, writes)

    def TT(eng, out, in0, in1, op, reads, writes):
        emit(eng, lambda e: e.tensor_tensor(out=out, in0=in0, in1=in1, op=op), reads, writes)

    def STT(eng, out, in0, scalar, in1, op0, op1, reads, writes, accum_out=None):
        if accum_out is None:
            emit(eng, lambda e: e.scalar_tensor_tensor(out=out, in0=in0, scalar=scalar, in1=in1, op0=op0, op1=op1), reads, writes)
        else:
            emit(eng, lambda e: e.scalar_tensor_tensor(out=out, in0=in0, scalar=scalar, in1=in1, op0=op0, op1=op1,
                                                      accum_out=accum_out), reads, writes)

    def TS(eng, out, in0, s1, op0, reads, writes):
        emit(eng, lambda e: e.tensor_scalar(out=out, in0=in0, scalar1=s1, scalar2=None, op0=op0), reads, writes)

    def RECIP(out, in_, reads, writes):
        emit("dve", lambda e: e.reciprocal(out=out, in_=in_), reads, writes)

    def VCOPY(out, in_, reads, writes):
        emit("dve", lambda e: e.tensor_copy(out=out, in_=in_), reads, writes)

    def MEMSET(eng, ap, val, reads, writes):
        emit(eng, lambda e: e.memset(ap, val), reads, writes)

    def DMA(eng, out, in_, reads, writes):
        emit(eng, lambda e: e.dma_start(out=out, in_=in_), reads, writes, dma=True)

    def col(c, Rr=128):
        return small[0:Rr, c:c + 1]

    DMA("pool", ident[:], cident, [], ["ident"])
    DMA("pool", masks[:], cmasks, [], ["masks"])
    MEMSET("pool", onesb[:], 1.0, [], ["onesb"])
    MEMSET("pool", onesf[:], 1.0, [], ["onesf"])
    MEMSET("pool", small[:], 0.0, [], ["small"])
    MEMSET("pool", col(C_EPS), EPS, [], ["small"])
    MEMSET("pool", col(C_NEG1), -1.0, [], ["small", "small_init"])
    MEMSET("pool", QTA[:], 0.0, [], [f"QTA{j}" for j in range(4)])
    MEMSET("pool", QTB[:], 0.0, [], [f"QTB{sl}_{h}" for sl in range(2) for h in range(4)])
    MEMSET("pool", onesg[:], 0.0, [], ["onesg"])
    MEMSET("pool", onesg[:, 0, 0:64], 1.0, [], ["onesg"])
    MEMSET("pool", onesg[:, 1, 64:128], 1.0, [], ["onesg"])

    def load_wkv(l):
        src = w_in[l].rearrange("(c p) n -> p c n", p=128)
        for (d0, s0, n) in ((0, 1792, 512), (512, 512, 128), (640, 2304, 512), (1152, 640, 128)):
            DMA("pool", Wkv[:, :, d0:d0 + n], src[:, :, s0:s0 + n], [], ["Wkv"])
        for kc in range(8):
            DMA("pool", gcolkv[:, kc:kc + 1], pre_g[l:l + 1, kc * 128:(kc + 1) * 128].rearrange("o p -> p o"), ["small_init"], ["gcolkv"])
        for kc in range(8):
            TS("dve", Wkv[:, kc, :], Wkv[:, kc, :], gcolkv[:, kc:kc + 1], ALU.mult, ["Wkv", "gcolkv"], ["Wkv"])

    def load_wq_wo(l):
        src = w_in[l].rearrange("(c p) n -> p c n", p=128)
        for (dbase, sbase) in ((0, 0), (1024, 768)):
            for j in range(4):
                for g in range(2):
                    d0 = dbase + j * 128 + g * 64
                    s0 = sbase + (g * 4 + j) * 64
                    DMA("pool", Wq[:, :, d0:d0 + 64], src[:, :, s0:s0 + 64], [], ["Wq"])
        DMA("pool", Wq[:, :, 512:1024], src[:, :, 1280:1792], [], ["Wq"])
        DMA("pool", Wq[:, :, 1536:2048], src[:, :, 2816:3328], [], ["Wq"])
        for kc in range(8):
            DMA("pool", gcol[:, kc:kc + 1], pre_g[l:l + 1, kc * 128:(kc + 1) * 128].rearrange("o p -> p o"), [], ["gcol"])
        for kc in range(8):
            TS("dve", Wq[:, kc, :], Wq[:, kc, :], gcol[:, kc:kc + 1], ALU.mult, ["Wq", "gcol"], ["Wq"])
        wo = w_out[l]
        for j in range(4):
            for g in range(2):
                r0 = (g * 4 + j) * 64
                DMA("pool", Wo[g * 64:(g + 1) * 64, j, :], wo[r0:r0 + 64, :], [], ["Wo"])
        DMA("pool", Wo[:, 4:8, :], wo[512:1024, :].rearrange("(c p) n -> p c n", p=128), [], ["Wo"])

    def load_params(l):
        lambda_init = 0.8 - 0.6 * math.exp(-0.3 * l)
        DMA("sp", gpost[:], post_g[l:l + 1, :].broadcast_to([128, D]), [], ["gpost"])
        lamt = Fp[3][:, 512:768].rearrange("p (a b) -> p a b", a=4)
        for i, t in enumerate((lq1, lk1, lq2, lk2)):
            DMA("sp", lamt[:, i, :], t[l:l + 1, :].broadcast_to([128, 64]), [], ["F3b"])
        for g in range(2):
            DMA("sp", sinkbc[g * 64:(g + 1) * 64, :], sink[l:l + 1, g * 4:(g + 1) * 4].broadcast_to([64, 4]), [], ["sinkbc"])
        DMA("sp", col(C_SG), subln[l:l + 1, :].rearrange("o p -> p o"), ["small_init"], ["sg_raw"])
        junk = Fp[3]
        MEMSET("dve", small[:, C_D1:C_D1 + 2], 0.0, ["small_init"], ["d1", "d2"])
        STT("dve", junk[:, 0:64], lamt[:, 0, :], 1.0, lamt[:, 1, :], ALU.mult, ALU.mult, ["F3b", "d1"], ["F3a", "d1"],
            accum_out=col(C_D1))
        STT("dve", junk[:, 64:128], lamt[:, 2, :], 1.0, lamt[:, 3, :], ALU.mult, ALU.mult, ["F3b", "d2"], ["F3a", "d2"],
            accum_out=col(C_D2))
        ACTV(small[:, C_E1:C_E1 + 2], small[:, C_D1:C_D1 + 2], AF.Exp, ["d1", "d2"], ["e12"])
        TT("dve", col(C_T), col(C_E2), col(C_E1), ALU.subtract, ["e12"], ["lt"])
        TS("dve", col(C_NLAM), col(C_T), -lambda_init, ALU.add, ["lt"], ["nlam"])
        TS("dve", col(C_SGR), col(C_SG), 1.0 - lambda_init, ALU.mult, ["sg_raw"], ["sg"])
        ACTV(sinkbc[:], sinkbc[:], AF.Exp, ["sinkbc"], ["sinkbc"])

    def x_src(seq, l, tile_idx, Rr):
        S = seq["S"]
        nb = S // 128
        if l == 0:
            if tile_idx == nb:
                return meta[0:Rr, :], None
            src = xp if seq["kind"] == "p" else xs
            return src[seq["idx"], tile_idx * 128:tile_idx * 128 + Rr, :], None
        buf = (l - 1) % 2
        r0 = seq["base"] + tile_idx * 128
        return xscr[buf][r0:r0 + Rr, :], f"xs{buf}_{seq['base']}_{tile_idx}"

    def x_dst(seq, l, tile_idx, Rr):
        if l == NL - 1:
            dst = yp if seq["kind"] == "p" else ys
            return dst[seq["idx"], tile_idx * 128:tile_idx * 128 + Rr, :], None
        buf = l % 2
        r0 = seq["base"] + tile_idx * 128
        return xscr[buf][r0:r0 + Rr, :], f"xs{buf}_{seq['base']}_{tile_idx}"

    def rstd_chain(ss_col, ln_col, rs_col, Rr, scale, tag):
        ACTV(col(ln_col, Rr), col(ss_col, Rr), AF.Ln, [f"ss{tag}", "small_init"], [f"ln{tag}"], scale=scale, bias=col(C_EPS, Rr))
        ACTV(col(rs_col, Rr), col(ln_col, Rr), AF.Exp, [f"ln{tag}"], [f"rs{tag}"], scale=-0.5)

    RF2A, RF2B = ["F2a0", "F2a1"], ["F2b0", "F2b1"]
    RF3A, RF3B = ["F3a"], ["F3b"]

    def tmp_set(i):
        if i < 2:
            return Fp[2][:, i * 256:(i + 1) * 256], Fp[2][:, 512 + i * 256:512 + (i + 1) * 256], [f"F2a{i}"], [f"F2b{i}"]
        j = i - 2
        return Fp[3][:, j * 256:(j + 1) * 256], Fp[3][:, 512 + j * 256:512 + (j + 1) * 256], RF3A, RF3B

    FRES = [["F0s0", "F0s1"], ["F1s0", "F1s1"], ["F2a0", "F2a1", "F2b0", "F2b1"], ["F3a", "F3b"]]
    BRES = [["B0"], ["B1"], ["B2a", "B2b"], ["B3a", "B3b"]]
    HT_MAIN = (hT, ["hT0", "hT1"])
    HT_ALT = (oT, ["oTA", "oTB0", "oTB1", "oTB2", "oTB3"])

    def h_loads(seq, l, tiles, fo=0):
        for ti, (tile_idx, Rr) in enumerate(tiles):
            src, sres = x_src(seq, l, tile_idx, Rr)
            DMA("sp", Fp[fo + ti][0:Rr, :], src, [sres] if sres else [], FRES[fo + ti])

    def h_compute(tiles, fo=0, on_act=False, scale_eng=None):
        for ti, (tile_idx, Rr) in enumerate(tiles):
            xin, xb = Fp[fo + ti], Bp[fo + ti]
            ssc = C_SS0 + ti
            MEMSET("pool" if on_act else "dve", col(ssc, Rr), 0.0, ["small_init"], [f"ssh{ti}"])
            if on_act:
                ACTV(xb[0:Rr, :], xin[0:Rr, :], AF.Square, FRES[fo + ti] + [f"ssh{ti}"], BRES[fo + ti] + [f"ssh{ti}"], accum_out=col(ssc, Rr))
            else:
                STT("dve", xb[0:Rr, :], xin[0:Rr, :], 1.0, xin[0:Rr, :], ALU.mult, ALU.mult, FRES[fo + ti] + [f"ssh{ti}"],
                    BRES[fo + ti] + [f"ssh{ti}"], accum_out=col(ssc, Rr))
        for ti, (tile_idx, Rr) in enumerate(tiles):
            rstd_chain(C_SS0 + ti, C_LN0 + ti, C_RS0 + ti, Rr, 1.0 / D, f"h{ti}")
        for ti, (tile_idx, Rr) in enumerate(tiles):
            if on_act and scale_eng is None:
                ACTV(Bp[fo + ti][0:Rr, :], Fp[fo + ti][0:Rr, :], AF.Identity, FRES[fo + ti] + [f"rsh{ti}"], BRES[fo + ti],
                     scale=col(C_RS0 + ti, Rr))
            else:
                TS(scale_eng or "dve", Bp[fo + ti][0:Rr, :], Fp[fo + ti][0:Rr, :], col(C_RS0 + ti, Rr), ALU.mult,
                   FRES[fo + ti] + [f"rsh{ti}"], BRES[fo + ti])

    def h_transposes(tiles, pbank, fo=0, hbuf=None, on_act=False):
        hb, hres = hbuf if hbuf is not None else HT_MAIN
        for ti, (tile_idx, Rr) in enumerate(tiles):
            xb = Bp[fo + ti]
            pT, pres = pbank[ti]
            for kc in range(8):
                TR(pT[:, kc, 0:Rr], xb[0:Rr, kc * 128:(kc + 1) * 128], ident[0:Rr, 0:Rr], BRES[fo + ti] + ["ident"], [pres])
            (ACOPY if on_act else VCOPY)(hb[:, :, ti * 128:ti * 128 + Rr], pT[:, :, 0:Rr], [pres], [hres[ti]] if hbuf is None else hres)

    pp = [0]
    ff = [0]

    def next_tmp(in_diff):
        k = ff[0]
        ff[0] = (ff[0] + 1) % 4
        Fb = Fp[k // 2]
        c0 = (k % 2) * 512
        rn = [f"F{k // 2}s{k % 2}"]
        return Fb[:, c0:c0 + 256], Fb[:, c0 + 256:c0 + 512], rn, rn

    def proj_fm(Wt, wname, col0, NQ, ntiles, hbuf=None):
        half = pp[0]; pp[0] ^= 1
        out = psP[:, half * 512:half * 512 + NQ]
        hb, hres = hbuf if hbuf is not None else (hT, [f"hT{t}" for t in range(ntiles)])
        for kc in range(8):
            MM(out, Wt[:, kc, col0:col0 + 128], hb[:, kc, 0:NQ], kc == 0, kc == 7, hres + [wname], [f"psP{half}"])
        return out, f"psP{half}"

    GBf = GB[:].rearrange("p a b c -> p (a b c)").bitcast(F32)
    p1s = [0]

    def p1_tmp():
        k = p1s[0]; p1s[0] ^= 1
        return (GBf[:, k * 512:k * 512 + 256], GBf[:, k * 512 + 256:k * 512 + 512], [f"GB{k}_0", f"GB{k}_1"], [f"GB{k}_2", f"GB{k}_3"])

    def rope_to(psrc, pres, NQ, pieces, dres, in_diff=False, tmp=None):
        t1, t2, r1, r2 = tmp if tmp is not None else next_tmp(in_diff)
        TT("dve", t1[:, 0:NQ], psrc, tabs[:, 0, 0:NQ], ALU.mult, [pres, "tabs"], r1)
        for (o0, i0) in ((0, 32), (32, 0), (64, 96), (96, 64)):
            TT("dve", t2[o0:o0 + 32, 0:NQ], psrc[i0:i0 + 32, :], tabs[i0:i0 + 32, 1, 0:NQ], ALU.mult, [pres, "tabs"], r2)
        for (p0, p1, dd) in pieces:
            TT("pool", dd, t1[p0:p1, 0:NQ], t2[p0:p1, 0:NQ], ALU.add, r1 + r2, [dres])

    def gate_tail(psrc, pres, NQ, dest, dres, in_diff):
        t1, t2, r1, r2 = next_tmp(in_diff)
        ee, rr = t1[:, 0:NQ], t2[:, 0:NQ]
        ACTV(ee, psrc, AF.Exp, [pres], r1, scale=-1.0)
        TS("dve", ee, ee, 1.0, ALU.add, r1, r1)
        RECIP(rr, ee, r1, r2)
        TT("dve", dest, psrc, rr, ALU.mult, [pres] + r2, [dres])

    def gate_to(col0, NQ, ntiles, dest, dres, in_diff=False):
        psrc, pres = proj_fm(Wq, "Wq", col0, NQ, ntiles)
        t1, t2, r1, r2 = next_tmp(in_diff)
        ee, rr = t1[:, 0:NQ], t2[:, 0:NQ]
        ACTV(ee, psrc, AF.Exp, [pres], r1, scale=-1.0)
        TS("dve", ee, ee, 1.0, ALU.add, r1, r1)
        RECIP(rr, ee, r1, r2)
        TT("dve", dest, psrc, rr, ALU.mult, [pres] + r2, [dres])

    def load_tabs(seq, col0, NQ):
        src = ropeP if seq["kind"] == "p" else ropeS
        DMA("sp", tabs[:, :, 0:NQ], src[:, :, col0:col0 + NQ], [], ["tabs"])

    sS = [0]
    wS = [0]
    bE = [0]

    def next_E():
        k = bE[0]; bE[0] = (bE[0] + 1) % 4
        return Bp[2 + k // 2][:, (k % 2) * 512:(k % 2) * 512 + 512], f"B{2 + k // 2}{'ab'[k % 2]}"

    PB_O = [(psTb[0], "psO0"), (psTb[1], "psO1")]
    psPT = [psP[:, 0:512].bitcast(BF16).rearrange("p (k r) -> p k r", k=8), psP[:, 512:1024].bitcast(BF16).rearrange("p (k r) -> p k r", k=8)]
    PB_P = [(psPT[0], "psP0"), (psPT[1], "psP1")]

    def phase1(seq, l):
        nb = seq["S"] // 128
        chunks = [[(t, 128), (t + 1, 128)] for t in range(0, nb, 2)] + [[(nb, NMETA)]]

        def pre_a(ci):
            fo = 2 * (ci % 2)
            h_loads(seq, l, chunks[ci], fo)
            h_compute(chunks[ci], fo, on_act=True)

        def pre_b(ci):
            fo = 2 * (ci % 2)
            h_transposes(chunks[ci], PB_O, fo, HT_MAIN_P1 if ci % 2 == 0 else HT_ALT, on_act=True)

        HT_MAIN_P1 = (hT, ["hT0", "hT1"])
        load_tabs(seq, chunks[0][0][0] * 128, sum(r for _, r in chunks[0]))
        pre_a(0)
        pre_b(0)
        for ci, tiles in enumerate(chunks):
            hbuf = HT_MAIN_P1 if ci % 2 == 0 else HT_ALT
            hb, hres = hbuf
            col0 = tiles[0][0] * 128
            NQ = sum(r for _, r in tiles)
            nt = len(tiles)
            cid = tiles[0][0]
            if ci + 1 < len(chunks):
                pre_a(ci + 1)
            for h in range(4):
                psrc, pres = proj_fm(Wkv, "Wkv", h * 128, NQ, nt, hbuf)
                rope_to(psrc, pres, NQ, [(0, 128, KTB[:, h, col0:col0 + NQ])], f"KTB{h}_{cid}", tmp=p1_tmp())
            psrc, pres = proj_fm(Wkv, "Wkv", 512, NQ, nt, hbuf)
            rope_to(psrc, pres, NQ, [(0, 128, KTA[:, col0:col0 + NQ])], f"KTA_{cid}", tmp=p1_tmp())
            if ci + 1 < len(chunks):
                pre_b(ci + 1)
            for ti, (tile_idx, Rr) in enumerate(tiles):
                half = pp[0]; pp[0] ^= 1
                out = psP[0:Rr, half * 512:(half + 1) * 512]
                for kc in range(8):
                    MM(out, hb[:, kc, ti * 128:ti * 128 + Rr], Wkv[:, kc, 640:1152], kc == 0, kc == 7, hres + ["Wkv"], [f"psP{half}"])
                ACOPY(VB[0:Rr, tile_idx, :], out, [f"psP{half}"], [f"VB_{tile_idx}"])
                half = pp[0]; pp[0] ^= 1
                out2 = psP[0:Rr, half * 512:half * 512 + 128]
                for kc in range(8):
                    MM(out2, hb[:, kc, ti * 128:ti * 128 + Rr], Wkv[:, kc, 1152:1280], kc == 0, kc == 7, hres + ["Wkv"], [f"psP{half}"])
                ACOPY(VA[0:Rr, tile_idx, :], out2, [f"psP{half}"], [f"VA_{tile_idx}"])
            if ci + 1 < len(chunks):
                nxt = chunks[ci + 1]
                load_tabs(seq, nxt[0][0] * 128, sum(r for _, r in nxt))

    def front_units(seq, l, tiles, slot, in_diff):
        col0 = tiles[0][0] * 128
        NQ = sum(r for _, r in tiles)
        nt = len(tiles)
        units = []
        units.append(lambda: (load_tabs(seq, col0, NQ), h_loads(seq, l, tiles)))
        units.append(lambda: h_compute(tiles, 0, on_act=True, scale_eng="dve"))
        units.append(lambda: h_transposes(tiles, PB_P if in_diff else PB_O))

        def qa(j):
            psrc, pres = proj_fm(Wq, "Wq", j * 128, NQ, nt)
            return lambda: rope_to(psrc, pres, NQ, [(0, 64, QTA[0:64, 0, j, 0:NQ]), (64, 128, QTA[64:128, 1, j, 0:NQ])],
                                   f"QTA{j}", in_diff)

        def qb(h):
            psrc, pres = proj_fm(Wq, "Wq", 512 + h * 128, NQ, nt)
            return lambda: rope_to(psrc, pres, NQ, [(0, 64, QTB[0:64, slot, h, 0, 0:NQ]), (64, 128, QTB[64:128, slot, h, 1, 0:NQ])],
                                   f"QTB{slot}_{h}", in_diff)

        def gt(col0, dest, dres):
            psrc, pres = proj_fm(Wq, "Wq", col0, NQ, nt)
            return lambda: gate_tail(psrc, pres, NQ, dest, dres, in_diff)
        for j in range(4):
            units.append(lambda j=j: qa(j))
        for j in range(4):
            units.append(lambda j=j: gt(1024 + j * 128, GA[:, j, 0:NQ], f"GA{j}"))
        for h in range(4):
            units.append(lambda h=h: qb(h))
        for h in range(4):
            units.append(lambda h=h: gt(1536 + h * 128, GB[:, slot, h, 0:NQ], f"GB{slot}_{h}"))
        return units

    def window_block(nb, S, qc0, QB, kl):
        W4 = 4 * QB
        items = [(g, ki) + tuple(kl[ki]) for g in range(2) for ki in range(len(kl))]
        st = {}

        def issue_S(i):
            g, ki, kt, M, kc0, mk = items[i]
            half = wS[0]; wS[0] ^= 1
            sout = psS[0:M, half * 512:half * 512 + W4]
            kres = f"KTA_{(kt // 2) * 2 if kt < nb else nb}"
            MM(sout, KTA[:, kc0:kc0 + M], QTA[:, g, :, qc0:qc0 + QB], True, mk is None, [kres] + [f"QTA{j}" for j in range(4)], [f"psS{half}"])
            if mk is not None:
                MM(sout, ident[0:M, 0:M], masks[0:M, mk:mk + QB].unsqueeze(1).broadcast_to([M, 4, QB]), False, True,
                   ["ident", "masks"], [f"psS{half}"])
            E, eres = next_E()
            ACTV(E[0:M, 0:W4], sout, AF.Exp, [f"psS{half}"], [eres], scale=SCALE)
            st[i] = (E, eres)

        def issue_AV(i):
            g, ki, kt, M, kc0, mk = items[i]
            E, eres = st.pop(i)
            first, last = ki == 0, ki == len(kl) - 1
            MM(psOb[g][:, 0:W4], VA[0:M, kt, :], E[0:M, 0:W4], first, last, [eres, f"VA_{kt}"], [f"psO{g}"])
            MM(psSb[0][:, 0:W4], onesg[0:M, g, :], E[0:M, 0:W4], first and g == 0, last and g == 1, [eres, "onesg"], ["psSum0"])

        n = len(items)
        issue_S(0)
        if n > 1:
            issue_S(1)
        for i in range(n):
            if i + 2 < n:
                issue_S(i + 2)
            issue_AV(i)
        def finalize():
            F3 = Fp[3]
            den = F3[:, 0:W4].rearrange("p (a b) -> p a b", a=4)
            TT("dve", den, psSb[0][:, 0:W4].rearrange("p (a b) -> p a b", a=4), sinkbc[:, :].unsqueeze(2).broadcast_to([128, 4, QB]),
               ALU.add, ["psSum0", "sinkbc"], RF3A)
            RECIP(F3[:, 0:W4], F3[:, 0:W4], RF3A, RF3A)
            for g in range(2):
                TT("dve", F3[g * 64:(g + 1) * 64, 512:512 + W4], psOb[g][g * 64:(g + 1) * 64, 0:W4], F3[g * 64:(g + 1) * 64, 0:W4],
                   ALU.mult, [f"psO{g}"] + RF3A, RF3B)
            TT("pool", oT[:, 0:4, qc0:qc0 + QB], F3[:, 512:512 + W4].rearrange("p (a b) -> p a b", a=4), GA[:, :, qc0:qc0 + QB], ALU.mult,
               RF3B + [f"GA{j}" for j in range(4)], ["oTA"])
        return finalize

    def window_blocks(seq, tiles):
        S = seq["S"]; nb = S // 128
        if tiles[0][0] == nb:
            return [lambda: window_block(nb, S, 0, NMETA, [(nb, NMETA, S, None), (0, 128, 0, 256)])]
        out = []
        for bi, (t, _) in enumerate(tiles):
            kl = []
            if t >= 1:
                kl.append((t - 1, 128, (t - 1) * 128, 0))
            kl.append((t, 128, t * 128, None))
            if t + 1 < nb:
                kl.append((t + 1, 128, (t + 1) * 128, 128))
            kl.append((nb, NMETA, S, None))
            out.append(lambda bi=bi, kl=kl: window_block(nb, S, bi * 128, 128, kl))
        return out

    def window_all(seq, tiles):
        for th in window_blocks(seq, tiles):
            th()()

    post2_done = [False]
    last_post2 = [None]

    def diff_all(nb, NQ, ktiles, slot, units, reload_x):
        W2 = 2 * NQ
        n = len(ktiles)
        pending = [None]
        tails = []
        reloaded = [False]
        units = list(units)
        total_steps = 4 * n
        stride = max(1, (total_steps - 2) // (len(units) + 1)) if units else 0
        step = [0]

        def post1(h):
            p = h % 2
            psO, psSum = psOb[p], psSb[p]
            F2, F3 = Fp[2], Fp[3]
            RECIP(F2[:, 0:W2], psSum[:, 0:W2], [f"psSum{p}"], RF2A)
            TT("dve", F2[:, 512:512 + W2], psO[:, 0:W2], F2[:, 0:W2], ALU.mult, [f"psO{p}"] + RF2A, RF2B)
            STT("dve", F3[:, 0:NQ], F2[:, 512 + NQ:512 + W2], small[:, C_NLAM:C_NLAM + 1], F2[:, 512:512 + NQ], ALU.mult, ALU.add,
                RF2B + ["nlam"], RF3A)
            TT("pool", F3[:, 256:256 + NQ], F3[:, 0:NQ], F3[:, 0:NQ], ALU.mult, RF3A, RF3A)

        def post2(h):
            p = h % 2
            psSum = psSb[p]
            F3 = Fp[3]
            post2_done[0] = True
            MM(psSum[:, 0:NQ], onesf[:, :], F3[:, 256:256 + NQ], True, True, RF3A + ["onesf"], [f"psSum{p}"])
            ACTV(F3[:, 512:512 + NQ], psSum[:, 0:NQ], AF.Ln, [f"psSum{p}", "small_init"], RF3B, scale=1.0 / 128,
                 bias=small[:, C_EPS:C_EPS + 1])
            ACTV(F3[:, 512:512 + NQ], F3[:, 512:512 + NQ], AF.Exp, RF3B, RF3B, scale=-0.5)
            STT("dve", F3[:, 768:768 + NQ], F3[:, 0:NQ], small[:, C_SGR:C_SGR + 1], F3[:, 512:512 + NQ], ALU.mult, ALU.mult,
                RF3A + RF3B + ["sg"], RF3B)
            TT("pool", oT[:, 4 + h, 0:NQ], F3[:, 768:768 + NQ], GB[:, slot, h, 0:NQ], ALU.mult, RF3B + [f"GB{slot}_{h}"], [f"oTB{h}"])

        K0 = min(max(5, n // 2), n - 1)
        steps = [(h, ki) for h in range(4) for ki in range(n)]
        st = {}

        def issue_S(i):
            h, ki = steps[i]
            kt, M, kc0 = ktiles[ki]
            half = sS[0]; sS[0] ^= 1
            sres = f"psS{half}"
            kres = f"KTB{h}_{(kt // 2) * 2 if kt < nb else nb}"
            MM(psS[0:M, half * 512:half * 512 + W2], KTB[:, h, kc0:kc0 + M], QTB[:, slot, h, :, 0:NQ], True, True,
               [kres, f"QTB{slot}_{h}"], [sres])
            E, eres = next_E()
            ACTV(E[0:M, 0:W2], psS[0:M, half * 512:half * 512 + W2], AF.Exp, [sres], [eres], scale=SCALE)
            st[i] = (E, eres)

        def issue_AV(i):
            h, ki = steps[i]
            p = h % 2
            kt, M, kc0 = ktiles[ki]
            E, eres = st.pop(i)
            first, last = ki == 0, ki == n - 1
            MM(psOb[p][:, 0:W2], VB[0:M, kt, h * 128:(h + 1) * 128], E[0:M, 0:W2], first, last, [eres, f"VB_{kt}"], [f"psO{p}"])
            MM(psSb[p][:, 0:W2], onesb[0:M, :], E[0:M, 0:W2], first, last, [eres, "onesb"], [f"psSum{p}"])

        ns = len(steps)
        issue_S(0)
        if ns > 1:
            issue_S(1)
        for i in range(ns):
            h, ki = steps[i]
            if i + 2 < ns:
                issue_S(i + 2)
            issue_AV(i)
            if ki == K0 and pending[0] is not None:
                post2(pending[0]); pending[0] = None
            step[0] += 1
            for tl in [t for t in tails if t[0] <= step[0]]:
                tails.remove(tl)
                tl[1]()
            if units and step[0] % stride == 0:
                tail = units.pop(0)()
                if callable(tail):
                    tails.append((step[0] + 2, tail))
            elif not units and not tails and not reloaded[0]:
                reload_x(); reloaded[0] = True
            if ki == n - 1:
                post1(h)
                pending[0] = h
        for tl in tails:
            tl[1]()
        tails.clear()
        while units:
            tail = units.pop(0)()
            if callable(tail):
                tail()
        if not reloaded[0]:
            reload_x()
        last_post2[0] = (lambda hh=pending[0]: post2(hh))

    def outproj_mm(ti, Rr, f_list):
        c0 = ti * 128
        ores = ["oTA", "oTA", "oTA", "oTA", "oTB0", "oTB1", "oTB2", "oTB3"]
        for f in f_list:
            for hf in range(2):
                MM(psP[0:Rr, hf * 512:(hf + 1) * 512], oT[:, f, c0:c0 + Rr], Wo[:, f, hf * 512:(hf + 1) * 512], f == 0, f == 7,
                   [ores[f], "Wo"], [f"psP{hf}"])

    def outproj_post(seq, l, ti, tile_idx, Rr):
        Fy = Fp[2 + ti]
        fy = (RF2A + RF2B) if ti == 0 else (RF3A + RF3B)
        ssc = C_SS2 + ti
        MEMSET("dve", col(ssc, Rr), 0.0, ["small_init"], [f"ssy{ti}"])
        ACTV(Fy[0:Rr, :], psP[0:Rr, :], AF.Square, ["psP0", "psP1", f"ssy{ti}"], fy + [f"ssy{ti}"], accum_out=col(ssc, Rr))
        rstd_chain(ssc, C_LN2 + ti, C_RS2 + ti, Rr, 1.0 / D, f"y{ti}")
        STT("dve", Fy[0:Rr, :], psP[0:Rr, :], col(C_RS2 + ti, Rr), gpost[0:Rr, :], ALU.mult, ALU.mult,
            ["psP0", "psP1", f"rsy{ti}", "gpost"], fy)
        TT("pool", Fy[0:Rr, :], Fy[0:Rr, :], Fp[ti][0:Rr, :], ALU.add, fy + FRES[ti], fy)
        dst, dres = x_dst(seq, l, tile_idx, Rr)
        DMA("pool", dst, Fy[0:Rr, :], fy, [dres] if dres else [])

    def phase2(seq, l):
        S = seq["S"]; nb = S // 128
        chunks = [[(t, 128), (t + 1, 128)] for t in range(0, nb, 2)]
        if l < NL - 1:
            chunks.append([(nb, NMETA)])
        ktiles = [(t, 128, t * 128) for t in range(nb)] + [(nb, NMETA, S)]
        for u in front_units(seq, l, chunks[0], 0, False):
            tail = u()
            if callable(tail):
                tail()
        window_all(seq, chunks[0])
        for ci, tiles in enumerate(chunks):
            slot = ci % 2
            NQ = sum(r for _, r in tiles)
            units = []
            if ci + 1 < len(chunks):
                units = front_units(seq, l, chunks[ci + 1], 1 - slot, True)
            diff_all(nb, NQ, ktiles, slot, units, lambda tiles=tiles: h_loads(seq, l, tiles))
            if DBG and l == 0 and seq is seqs[0] and ci == 0:
                for nme, t, rs in (("oT", oT, ["oTA"] + [f"oTB{j}" for j in range(4)]),
                                   ("KTB", KTB, [f"KTB{hh}_{c}" for hh in range(4) for c in list(range(0, nb, 2)) + [nb]]),
                                   ("KTA", KTA, [f"KTA_{c}" for c in list(range(0, nb, 2)) + [nb]]),
                                   ("VB", VB, [f"VB_{c}" for c in range(nb + 1)]), ("VA", VA, [f"VA_{c}" for c in range(nb + 1)])):
                    DMA("pool", dbg[nme], t[:], rs, [])
            wb = window_blocks(seq, chunks[ci + 1]) if ci + 1 < len(chunks) else []
            fin0 = wb[0]() if wb else None
            for ti, (tile_idx, Rr) in enumerate(tiles):
                outproj_mm(ti, Rr, range(7))
                if ti == 0:
                    last_post2[0]()
                outproj_mm(ti, Rr, [7])
                outproj_post(seq, l, ti, tile_idx, Rr)
            if fin0 is not None:
                fin0()
            for th in wb[1:]:
                th()()

    for l in range(NL):
        if l == 0:
            load_wkv(0)
        load_wq_wo(l)
        load_params(l)
        for si, seq in enumerate(seqs):
            phase1(seq, l)
            if si == len(seqs) - 1 and l + 1 < NL:
                load_wkv(l + 1)
            phase2(seq, l)

    R.finalize()
    with ExitStack() as ss_:
        esems = {e: ss_.enter_context(nc.semaphore(f"sem_{e}")) for e in Rec.ENGS}
        dsems = {}
        for q, n in R.n_dma.items():
            for i in range(n):
                dsems[(q, i)] = ss_.enter_context(nc.semaphore(f"dsem_{q}{i}"))
        block = ss_.enter_context(nc.Block())

        @block.tensor
        def _(t):
            R.play("pe", t, esems, dsems)

        @block.scalar
        def _(s):
            R.play("act", s, esems, dsems)

        @block.vector
        def _(v):
            R.play("dve", v, esems, dsems)

        @block.gpsimd
        def _(g):
            R.play("pool", g, esems, dsems)

        @block.sync
        def _(sy):
            R.play("sp", sy, esems, dsems)
    es.close()
    return nc


def rope_table(S):
    LP = S + NMETA
    pos = np.concatenate([np.arange(S, dtype=np.float32) + NMETA, np.arange(NMETA, dtype=np.float32)])
    inv_freq = (1.0 / (ROPE_THETA ** (np.arange(0, 64, 2, dtype=np.float32) / 64.0))).astype(np.float32)
    ang = (pos[None, :] * inv_freq[:, None]).astype(np.float32)
    cos = np.cos(ang).astype(np.float32)
    sin = np.sin(ang).astype(np.float32)
    tab = np.zeros((128, 2, LP), np.float32)
    for p in range(128):
        f = p % 32
        tab[p, 0] = cos[f]
        tab[p, 1] = -sin[f] if (p % 64) >= 32 else sin[f]
    return tab


def const_tables():
    ident = np.eye(128, dtype=np.float32)
    k = np.arange(128)[:, None]
    q = np.arange(128)[None, :]
    NEG = -30000.0
    masks = np.zeros((128, 272), np.float32)
    masks[:, 0:128] = np.where(k >= q, 0.0, NEG)
    masks[:, 128:256] = np.where(k <= q, 0.0, NEG)
    u = np.arange(128)[:, None]
    m = np.arange(NMETA)[None, :]
    masks[:, 256:272] = np.where(u <= 112 + m, 0.0, NEG)
    return ident, masks


_CACHE = {}


def kernel(x_prompt, x_sample, meta_tokens, w_in, w_out, pre_norm_g, post_norm_g, sink_logits,
           lambda_q1, lambda_k1, lambda_q2, lambda_k2, subln_g):
    f = lambda a: np.ascontiguousarray(np.asarray(a, dtype=np.float32))
    x_prompt, x_sample = f(x_prompt), f(x_sample)
    NL = w_in.shape[0]
    BP, SPn = x_prompt.shape[0], x_prompt.shape[1]
    BS, SSn = x_sample.shape[0], x_sample.shape[1]
    nP, nS = BP // N_CORES, BS // N_CORES
    cfg = Cfg(n_layers=NL, n_prompt=nP, s_prompt=SPn, n_sample=nS, s_sample=SSn)
    nc = build_program(cfg)
    ident, masks = const_tables()
    common = dict(meta=f(meta_tokens), w_in=f(w_in), w_out=f(w_out), pre_g=f(pre_norm_g), post_g=f(post_norm_g),
                  sink=f(sink_logits), lq1=f(lambda_q1), lk1=f(lambda_k1), lq2=f(lambda_q2), lk2=f(lambda_k2),
                  subln=f(subln_g), ropeP=rope_table(SPn), ropeS=rope_table(SSn), cident=ident, cmasks=masks)
    in_maps = []
    for c in range(N_CORES):
        m = dict(common)
        m["xp"] = x_prompt[c * nP:(c + 1) * nP]
        m["xs"] = x_sample[c * nS:(c + 1) * nS]
        in_maps.append(m)
    res = run_bass_kernel_spmd(nc, in_maps, core_ids=list(range(N_CORES)))
    ypo = np.concatenate([np.asarray(r["yp"]) for r in res.results], axis=0).astype(np.float32)
    yso = np.concatenate([np.asarray(r["ys"]) for r in res.results], axis=0).astype(np.float32)
    return (ypo, yso)
```

```python
import math
from contextlib import ExitStack

import numpy as np
import concourse.bass as bass
import concourse.mybir as mybir
from concourse.bass_utils import run_bass_kernel_spmd

F32 = mybir.dt.float32
BF16 = mybir.dt.bfloat16
AF = mybir.ActivationFunctionType
ALU = mybir.AluOpType

D = 1024
NMETA = 16
EPS = 1e-6
SCALE = 0.125
ROPE_THETA = 10000.0
N_CORES = 8


class Op:
    __slots__ = ("eng", "fn", "deps", "sig", "cnt", "dma", "vsem", "vval")

    def __init__(self, eng, fn, dma):
        self.eng = eng
        self.fn = fn
        self.deps = ()
        self.sig = False
        self.cnt = 0
        self.dma = dma
        self.vsem = None
        self.vval = 0


class Rec:
    ENGS = ("pe", "act", "dve", "pool", "sp")

    def __init__(self, n_dma_sems):
        self.ops = {e: [] for e in self.ENGS}
        self.res_w = {}
        self.res_r = {}
        self.n_dma = n_dma_sems
        self.dma_rr = {e: 0 for e in n_dma_sems}
        self.dma_last = {}
        self.dma_val = {}

    def emit(self, eng, fn, reads=(), writes=(), dma=False):
        op = Op(eng, fn, dma)
        deps = []
        seen = set()
        res_w, res_r = self.res_w, self.res_r
        for r in reads:
            w = res_w.get(r)
            if w is not None and id(w) not in seen:
                seen.add(id(w)); deps.append(w)
        for r in writes:
            w = res_w.get(r)
            if w is not None and id(w) not in seen:
                seen.add(id(w)); deps.append(w)
            for rd in res_r.get(r, ()):
                if id(rd) not in seen:
                    seen.add(id(rd)); deps.append(rd)
        if dma:
            k = (eng, self.dma_rr[eng])
            self.dma_rr[eng] = (self.dma_rr[eng] + 1) % self.n_dma[eng]
            prev = self.dma_last.get(k)
            if prev is not None and id(prev) not in seen:
                seen.add(id(prev)); deps.append(prev)
            self.dma_last[k] = op
            self.dma_val[k] = self.dma_val.get(k, 0) + 16
            op.vsem = k
            op.vval = self.dma_val[k]
        keep = []
        for d in deps:
            if d.dma or dma:
                keep.append(d)
            elif d.eng == eng and eng == "pe":
                continue
            else:
                keep.append(d)
        op.deps = keep
        for r in reads:
            res_r.setdefault(r, []).append(op)
        for r in writes:
            res_w[r] = op
            res_r[r] = []
        self.ops[eng].append(op)
        return op

    def finalize(self):
        for e in self.ENGS:
            for op in self.ops[e]:
                for d in op.deps:
                    if not d.dma:
                        d.sig = True
        for e in self.ENGS:
            c = 0
            for op in self.ops[e]:
                if op.sig and not op.dma:
                    c += 1
                    op.cnt = c

    def play(self, eng, handle, esems, dsems):
        waited = {}
        for op in self.ops[eng]:
            for d in op.deps:
                if d.dma:
                    key = ("d",) + d.vsem
                    sem = dsems[d.vsem]
                    val = d.vval
                else:
                    key = ("e", d.eng)
                    sem = esems[d.eng]
                    val = d.cnt
                if waited.get(key, 0) < val:
                    handle.wait_ge(sem, val)
                    waited[key] = val
            ins = op.fn(handle)
            if op.dma:
                ins.then_inc(dsems[op.vsem], 16)
            elif op.sig:
                ins.then_inc(esems[eng], 1)
        if eng == "sp":
            for k, v in self.dma_val.items():
                handle.wait_ge(dsems[k], v)


class Cfg:
    def __init__(self, n_layers=4, n_prompt=2, s_prompt=4096, n_sample=2, s_sample=2048, lambda_layers=None):
        self.n_layers = n_layers
        self.n_prompt = n_prompt
        self.s_prompt = s_prompt
        self.n_sample = n_sample
        self.s_sample = s_sample


def build_program(cfg):
    nc = bass.Bass("TRN2", target_bir_lowering=False)
    NL = cfg.n_layers
    SP_, SS_ = cfg.s_prompt, cfg.s_sample
    nP, nS = cfg.n_prompt, cfg.n_sample
    LPP, LPS = SP_ + NMETA, SS_ + NMETA
    LPM = max(LPP, LPS)
    NTM = max(SP_, SS_) // 128 + 1

    def din(name, shape, dt=F32):
        return nc.dram_tensor(name, list(shape), dt, kind="ExternalInput").ap()

    xp = din("xp", [nP, SP_, D])
    xs = din("xs", [nS, SS_, D])
    meta = din("meta", [NMETA, D])
    w_in = din("w_in", [NL, D, 3328])
    w_out = din("w_out", [NL, D, D])
    pre_g = din("pre_g", [NL, D])
    post_g = din("post_g", [NL, D])
    sink = din("sink", [NL, 8])
    lq1 = din("lq1", [NL, 64]); lk1 = din("lk1", [NL, 64])
    lq2 = din("lq2", [NL, 64]); lk2 = din("lk2", [NL, 64])
    subln = din("subln", [NL, 128])
    ropeP = din("ropeP", [128, 2, LPP])
    ropeS = din("ropeS", [128, 2, LPS])
    cident = din("cident", [128, 128])
    cmasks = din("cmasks", [128, 272])
    DBG = getattr(cfg, "debug", False)
    if DBG:
        dbg = {n: nc.dram_tensor("dbg_" + n, sh, dt, kind="ExternalOutput").ap() for n, sh, dt in (
            ("KTB", [128, 4, LPM], BF16), ("KTA", [128, LPM], BF16), ("VB", [128, NTM, 512], BF16), ("VA", [128, NTM, 128], BF16),
            ("oT", [128, 8, 256], BF16))}
    yp = nc.dram_tensor("yp", [nP, SP_, D], F32, kind="ExternalOutput").ap()
    ys = nc.dram_tensor("ys", [nS, SS_, D], F32, kind="ExternalOutput").ap()

    seqs = []
    row = 0
    for i in range(max(nP, nS)):
        if i < nP:
            seqs.append(dict(kind="p", idx=i, S=SP_, base=row)); row += LPP
        if i < nS:
            seqs.append(dict(kind="s", idx=i, S=SS_, base=row)); row += LPS
    tot_rows = row
    xscr = [nc.dram_tensor("xscrA", [tot_rows, D], F32).ap(), nc.dram_tensor("xscrB", [tot_rows, D], F32).ap()]

    R = Rec({"sp": 12, "pool": 6})
    es = ExitStack()

    def sb(name, shape, dt):
        return es.enter_context(nc.sbuf_tensor(name, list(shape), dt))

    def ps(name, shape, dt):
        return es.enter_context(nc.psum_tensor(name, list(shape), dt))

    KTB = sb("KTB", [128, 4, LPM], BF16)
    VB = sb("VB", [128, NTM, 512], BF16)
    KTA = sb("KTA", [128, LPM], BF16)
    VA = sb("VA", [128, NTM, 128], BF16)
    Wkv = sb("Wkv", [128, 8, 1280], BF16)
    Wq = sb("Wq", [128, 8, 2048], BF16)
    Wo = sb("Wo", [128, 8, 1024], BF16)
    gcol = sb("gcol", [128, 8], F32)
    gpost = sb("gpost", [128, D], F32)
    ident = sb("ident", [128, 128], BF16)
    masks = sb("masks", [128, 272], BF16)
    onesg = sb("onesg", [128, 2, 128], BF16)
    onesb = sb("onesb", [128, 128], BF16)
    onesf = sb("onesf", [128, 128], F32)
    Fp = [sb(f"F{i}", [128, D], F32) for i in range(4)]
    Bp = [sb(f"B{i}", [128, D], BF16) for i in range(4)]
    hT = sb("hT", [128, 8, 256], BF16)
    tabs = sb("tabs", [128, 2, 256], F32)
    QTA = sb("QTA", [128, 2, 4, 256], BF16)
    QTB = sb("QTB", [128, 2, 4, 2, 256], BF16)
    GA = sb("GA", [128, 4, 256], BF16)
    GB = sb("GB", [128, 2, 4, 256], BF16)
    oT = sb("oT", [128, 8, 256], BF16)
    small = sb("small", [128, 48], F32)
    sinkbc = sb("sinkbc", [128, 4], F32)

    C_EPS, C_SS0, C_SS1, C_LN0, C_LN1, C_RS0, C_RS1 = 0, 1, 2, 3, 4, 5, 6
    C_SS2, C_LN2, C_RS2 = 8, 10, 12
    C_D1, C_D2, C_E1, C_E2, C_T, C_NLAM, C_SG, C_SGR, C_NEG1 = 20, 21, 22, 23, 24, 25, 26, 27, 28
    gcolkv = small[:, 32:40]

    psS = ps("psS", [128, 1024], F32)
    psOO = ps("psOO", [128, 1024], F32)
    psSS = ps("psSS", [128, 1024], F32)
    psP = ps("psP", [128, 1024], F32)
    psOb = [psOO[:, 0:512], psOO[:, 512:1024]]
    psSb = [psSS[:, 0:512], psSS[:, 512:1024]]
    psTb = [psOO[:, 0:512].bitcast(BF16).rearrange("p (k r) -> p k r", k=8),
            psOO[:, 512:1024].bitcast(BF16).rearrange("p (k r) -> p k r", k=8)]

    emit = R.emit

    def MM(out, lhsT, rhs, start, stop, reads, writes):
        emit("pe", lambda e: e.matmul(out, lhsT=lhsT, rhs=rhs, start=start, stop=stop), reads, writes)

    def TR(out, in_, idn, reads, writes):
        emit("pe", lambda e: e.transpose(out=out, in_=in_, identity=idn), reads, writes)

    def ACTV(out, in_, func, reads, writes, scale=1.0, bias=None, accum_out=None):
        kw = {}
        if bias is not None:
            kw["bias"] = bias
        if accum_out is not None:
            kw["accum_out"] = accum_out
        emit("act", lambda e: e.activation(out=out, in_=in_, func=func, scale=scale, **kw), reads, writes)

    def ACOPY(out, in_, reads, writes):
        emit("act", lambda e: e.copy(out=out, in_=in_), reads, writes)

    def TT(eng, out, in0, in1, op, reads, writes):
        emit(eng, lambda e: e.tensor_tensor(out=out, in0=in0, in1=in1, op=op), reads, writes)

    def STT(eng, out, in0, scalar, in1, op0, op1, reads, writes, accum_out=None):
        if accum_out is None:
            emit(eng, lambda e: e.scalar_tensor_tensor(out=out, in0=in0, scalar=scalar, in1=in1, op0=op0, op1=op1), reads, writes)
        else:
            emit(eng, lambda e: e.scalar_tensor_tensor(out=out, in0=in0, scalar=scalar, in1=in1, op0=op0, op1=op1,
                                                      accum_out=accum_out), reads, writes)

    def TS(eng, out, in0, s1, op0, reads, writes):
        emit(eng, lambda e: e.tensor_scalar(out=out, in0=in0, scalar1=s1, scalar2=None, op0=op0), reads, writes)

    def RECIP(out, in_, reads, writes):
        emit("dve", lambda e: e.reciprocal(out=out, in_=in_), reads, writes)

    def VCOPY(out, in_, reads, writes):
        emit("dve", lambda e: e.tensor_copy(out=out, in_=in_), reads, writes)

    def MEMSET(eng, ap, val, reads, writes):
        emit(eng, lambda e: e.memset(ap, val), reads, writes)

    def DMA(eng, out, in_, reads, writes):
        emit(eng, lambda e: e.dma_start(out=out, in_=in_), reads, writes, dma=True)

    def col(c, Rr=128):
        return small[0:Rr, c:c + 1]

    DMA("pool", ident[:], cident, [], ["ident"])
    DMA("pool", masks[:], cmasks, [], ["masks"])
    MEMSET("pool", onesb[:], 1.0, [], ["onesb"])
    MEMSET("pool", onesf[:], 1.0, [], ["onesf"])
    MEMSET("pool", small[:], 0.0, [], ["small"])
    MEMSET("pool", col(C_EPS), EPS, [], ["small"])
    MEMSET("pool", col(C_NEG1), -1.0, [], ["small", "small_init"])
    MEMSET("pool", QTA[:], 0.0, [], [f"QTA{j}" for j in range(4)])
    MEMSET("pool", QTB[:], 0.0, [], [f"QTB{sl}_{h}" for sl in range(2) for h in range(4)])
    MEMSET("pool", onesg[:], 0.0, [], ["onesg"])
    MEMSET("pool", onesg[:, 0, 0:64], 1.0, [], ["onesg"])
    MEMSET("pool", onesg[:, 1, 64:128], 1.0, [], ["onesg"])

    def load_wkv(l):
        src = w_in[l].rearrange("(c p) n -> p c n", p=128)
        for (d0, s0, n) in ((0, 1792, 512), (512, 512, 128), (640, 2304, 512), (1152, 640, 128)):
            DMA("pool", Wkv[:, :, d0:d0 + n], src[:, :, s0:s0 + n], [], ["Wkv"])
        for kc in range(8):
            DMA("pool", gcolkv[:, kc:kc + 1], pre_g[l:l + 1, kc * 128:(kc + 1) * 128].rearrange("o p -> p o"), ["small_init"], ["gcolkv"])
        for kc in range(8):
            TS("dve", Wkv[:, kc, :], Wkv[:, kc, :], gcolkv[:, kc:kc + 1], ALU.mult, ["Wkv", "gcolkv"], ["Wkv"])

    def load_wq_wo(l):
        src = w_in[l].rearrange("(c p) n -> p c n", p=128)
        for (dbase, sbase) in ((0, 0), (1024, 768)):
            for j in range(4):
                for g in range(2):
                    d0 = dbase + j * 128 + g * 64
                    s0 = sbase + (g * 4 + j) * 64
                    DMA("pool", Wq[:, :, d0:d0 + 64], src[:, :, s0:s0 + 64], [], ["Wq"])
        DMA("pool", Wq[:, :, 512:1024], src[:, :, 1280:1792], [], ["Wq"])
        DMA("pool", Wq[:, :, 1536:2048], src[:, :, 2816:3328], [], ["Wq"])
        for kc in range(8):
            DMA("pool", gcol[:, kc:kc + 1], pre_g[l:l + 1, kc * 128:(kc + 1) * 128].rearrange("o p -> p o"), [], ["gcol"])
        for kc in range(8):
            TS("dve", Wq[:, kc, :], Wq[:, kc, :], gcol[:, kc:kc + 1], ALU.mult, ["Wq", "gcol"], ["Wq"])
        wo = w_out[l]
        for j in range(4):
            for g in range(2):
                r0 = (g * 4 + j) * 64
                DMA("pool", Wo[g * 64:(g + 1) * 64, j, :], wo[r0:r0 + 64, :], [], ["Wo"])
        DMA("pool", Wo[:, 4:8, :], wo[512:1024, :].rearrange("(c p) n -> p c n", p=128), [], ["Wo"])

    def load_params(l):
        lambda_init = 0.8 - 0.6 * math.exp(-0.3 * l)
        DMA("sp", gpost[:], post_g[l:l + 1, :].broadcast_to([128, D]), [], ["gpost"])
        lamt = Fp[3][:, 512:768].rearrange("p (a b) -> p a b", a=4)
        for i, t in enumerate((lq1, lk1, lq2, lk2)):
            DMA("sp", lamt[:, i, :], t[l:l + 1, :].broadcast_to([128, 64]), [], ["F3b"])
        for g in range(2):
            DMA("sp", sinkbc[g * 64:(g + 1) * 64, :], sink[l:l + 1, g * 4:(g + 1) * 4].broadcast_to([64, 4]), [], ["sinkbc"])
        DMA("sp", col(C_SG), subln[l:l + 1, :].rearrange("o p -> p o"), ["small_init"], ["sg_raw"])
        junk = Fp[3]
        MEMSET("dve", small[:, C_D1:C_D1 + 2], 0.0, ["small_init"], ["d1", "d2"])
        STT("dve", junk[:, 0:64], lamt[:, 0, :], 1.0, lamt[:, 1, :], ALU.mult, ALU.mult, ["F3b", "d1"], ["F3a", "d1"],
            accum_out=col(C_D1))
        STT("dve", junk[:, 64:128], lamt[:, 2, :], 1.0, lamt[:, 3, :], ALU.mult, ALU.mult, ["F3b", "d2"], ["F3a", "d2"],
            accum_out=col(C_D2))
        ACTV(small[:, C_E1:C_E1 + 2], small[:, C_D1:C_D1 + 2], AF.Exp, ["d1", "d2"], ["e12"])
        TT("dve", col(C_T), col(C_E2), col(C_E1), ALU.subtract, ["e12"], ["lt"])
        TS("dve", col(C_NLAM), col(C_T), -lambda_init, ALU.add, ["lt"], ["nlam"])
        TS("dve", col(C_SGR), col(C_SG), 1.0 - lambda_init, ALU.mult, ["sg_raw"], ["sg"])
        ACTV(sinkbc[:], sinkbc[:], AF.Exp, ["sinkbc"], ["sinkbc"])

    def x_src(seq, l, tile_idx, Rr):
        S = seq["S"]
        nb = S // 128
        if l == 0:
            if tile_idx == nb:
                return meta[0:Rr, :], None
            src = xp if seq["kind"] == "p" else xs
            return src[seq["idx"], tile_idx * 128:tile_idx * 128 + Rr, :], None
        buf = (l - 1) % 2
        r0 = seq["base"] + tile_idx * 128
        return xscr[buf][r0:r0 + Rr, :], f"xs{buf}_{seq['base']}_{tile_idx}"

    def x_dst(seq, l, tile_idx, Rr):
        if l == NL - 1:
            dst = yp if seq["kind"] == "p" else ys
            return dst[seq["idx"], tile_idx * 128:tile_idx * 128 + Rr, :], None
        buf = l % 2
        r0 = seq["base"] + tile_idx * 128
        return xscr[buf][r0:r0 + Rr, :], f"xs{buf}_{seq['base']}_{tile_idx}"

    def rstd_chain(ss_col, ln_col, rs_col, Rr, scale, tag):
        ACTV(col(ln_col, Rr), col(ss_col, Rr), AF.Ln, [f"ss{tag}", "small_init"], [f"ln{tag}"], scale=scale, bias=col(C_EPS, Rr))
        ACTV(col(rs_col, Rr), col(ln_col, Rr), AF.Exp, [f"ln{tag}"], [f"rs{tag}"], scale=-0.5)

    RF2A, RF2B = ["F2a0", "F2a1"], ["F2b0", "F2b1"]
    RF3A, RF3B = ["F3a"], ["F3b"]

    def tmp_set(i):
        if i < 2:
            return Fp[2][:, i * 256:(i + 1) * 256], Fp[2][:, 512 + i * 256:512 + (i + 1) * 256], [f"F2a{i}"], [f"F2b{i}"]
        j = i - 2
        return Fp[3][:, j * 256:(j + 1) * 256], Fp[3][:, 512 + j * 256:512 + (j + 1) * 256], RF3A, RF3B

    FRES = [["F0s0", "F0s1"], ["F1s0", "F1s1"], ["F2a0", "F2a1", "F2b0", "F2b1"], ["F3a", "F3b"]]
    BRES = [["B0"], ["B1"], ["B2a", "B2b"], ["B3a", "B3b"]]
    HT_MAIN = (hT, ["hT0", "hT1"])
    HT_ALT = (oT, ["oTA", "oTB0", "oTB1", "oTB2", "oTB3"])

    def h_loads(seq, l, tiles, fo=0):
        for ti, (tile_idx, Rr) in enumerate(tiles):
            src, sres = x_src(seq, l, tile_idx, Rr)
            DMA("sp", Fp[fo + ti][0:Rr, :], src, [sres] if sres else [], FRES[fo + ti])

    def h_compute(tiles, fo=0, on_act=False, scale_eng=None):
        for ti, (tile_idx, Rr) in enumerate(tiles):
            xin, xb = Fp[fo + ti], Bp[fo + ti]
            ssc = C_SS0 + ti
            MEMSET("pool" if on_act else "dve", col(ssc, Rr), 0.0, ["small_init"], [f"ssh{ti}"])
            if on_act:
                ACTV(xb[0:Rr, :], xin[0:Rr, :], AF.Square, FRES[fo + ti] + [f"ssh{ti}"], BRES[fo + ti] + [f"ssh{ti}"], accum_out=col(ssc, Rr))
            else:
                STT("dve", xb[0:Rr, :], xin[0:Rr, :], 1.0, xin[0:Rr, :], ALU.mult, ALU.mult, FRES[fo + ti] + [f"ssh{ti}"],
                    BRES[fo + ti] + [f"ssh{ti}"], accum_out=col(ssc, Rr))
        for ti, (tile_idx, Rr) in enumerate(tiles):
            rstd_chain(C_SS0 + ti, C_LN0 + ti, C_RS0 + ti, Rr, 1.0 / D, f"h{ti}")
        for ti, (tile_idx, Rr) in enumerate(tiles):
            if on_act and scale_eng is None:
                ACTV(Bp[fo + ti][0:Rr, :], Fp[fo + ti][0:Rr, :], AF.Identity, FRES[fo + ti] + [f"rsh{ti}"], BRES[fo + ti],
                     scale=col(C_RS0 + ti, Rr))
            else:
                TS(scale_eng or "dve", Bp[fo + ti][0:Rr, :], Fp[fo + ti][0:Rr, :], col(C_RS0 + ti, Rr), ALU.mult,
                   FRES[fo + ti] + [f"rsh{ti}"], BRES[fo + ti])

    def h_compute_staged(tiles):
        for ti, (tile_idx, Rr) in enumerate(tiles):
            xin, xb = Fp[ti], Bp[ti]
            ssc = C_SS0 + ti
            MEMSET("dve", col(ssc, Rr), 0.0, ["small_init"], [f"ssh{ti}"])
            STT("dve", xb[0:Rr, :], xin[0:Rr, :], 1.0, xin[0:Rr, :], ALU.mult, ALU.mult, FRES[ti] + [f"ssh{ti}"],
                BRES[ti] + [f"ssh{ti}"], accum_out=col(ssc, Rr))

        def stage_b():
            for ti, (tile_idx, Rr) in enumerate(tiles):
                rstd_chain(C_SS0 + ti, C_LN0 + ti, C_RS0 + ti, Rr, 1.0 / D, f"h{ti}")

            def stage_c():
                for ti, (tile_idx, Rr) in enumerate(tiles):
                    TS("dve", Bp[ti][0:Rr, :], Fp[ti][0:Rr, :], col(C_RS0 + ti, Rr), ALU.mult, FRES[ti] + [f"rsh{ti}"], BRES[ti])
            return stage_c
        return stage_b

    def h_transposes(tiles, pbank, fo=0, hbuf=None, on_act=False):
        hb, hres = hbuf if hbuf is not None else HT_MAIN
        for ti, (tile_idx, Rr) in enumerate(tiles):
            xb = Bp[fo + ti]
            pT, pres = pbank[ti]
            for kc in range(8):
                TR(pT[:, kc, 0:Rr], xb[0:Rr, kc * 128:(kc + 1) * 128], ident[0:Rr, 0:Rr], BRES[fo + ti] + ["ident"], [pres])
            (ACOPY if on_act else VCOPY)(hb[:, :, ti * 128:ti * 128 + Rr], pT[:, :, 0:Rr], [pres], [hres[ti]] if hbuf is None else hres)

    pp = [0]
    ff = [0]

    def next_tmp(in_diff):
        k = ff[0]
        ff[0] = (ff[0] + 1) % 4
        Fb = Fp[k // 2]
        c0 = (k % 2) * 512
        rn = [f"F{k // 2}s{k % 2}"]
        return Fb[:, c0:c0 + 256], Fb[:, c0 + 256:c0 + 512], rn, rn

    def proj_fm(Wt, wname, col0, NQ, ntiles, hbuf=None):
        half = pp[0]; pp[0] ^= 1
        out = psP[:, half * 512:half * 512 + NQ]
        hb, hres = hbuf if hbuf is not None else (hT, [f"hT{t}" for t in range(ntiles)])
        for kc in range(8):
            MM(out, Wt[:, kc, col0:col0 + 128], hb[:, kc, 0:NQ], kc == 0, kc == 7, hres + [wname], [f"psP{half}"])
        return out, f"psP{half}"

    GBf = GB[:].rearrange("p a b c -> p (a b c)").bitcast(F32)
    p1s = [0]

    def p1_tmp():
        k = p1s[0]; p1s[0] ^= 1
        return (GBf[:, k * 512:k * 512 + 256], GBf[:, k * 512 + 256:k * 512 + 512], [f"GB{k}_0", f"GB{k}_1"], [f"GB{k}_2", f"GB{k}_3"])

    def rope_to(psrc, pres, NQ, pieces, dres, in_diff=False, tmp=None):
        t1, t2, r1, r2 = tmp if tmp is not None else next_tmp(in_diff)
        TT("dve", t1[:, 0:NQ], psrc, tabs[:, 0, 0:NQ], ALU.mult, [pres, "tabs"], r1)
        for (o0, i0) in ((0, 32), (32, 0), (64, 96), (96, 64)):
            TT("dve", t2[o0:o0 + 32, 0:NQ], psrc[i0:i0 + 32, :], tabs[i0:i0 + 32, 1, 0:NQ], ALU.mult, [pres, "tabs"], r2)
        for (p0, p1, dd) in pieces:
            TT("pool", dd, t1[p0:p1, 0:NQ], t2[p0:p1, 0:NQ], ALU.add, r1 + r2, [dres])

    def gate_tail(psrc, pres, NQ, dest, dres, in_diff):
        t1, t2, r1, r2 = next_tmp(in_diff)
        ee, rr = t1[:, 0:NQ], t2[:, 0:NQ]
        ACTV(ee, psrc, AF.Exp, [pres], r1, scale=-1.0)
        TS("dve", ee, ee, 1.0, ALU.add, r1, r1)
        RECIP(rr, ee, r1, r2)
        TT("dve", dest, psrc, rr, ALU.mult, [pres] + r2, [dres])

    def gate_to(col0, NQ, ntiles, dest, dres, in_diff=False):
        psrc, pres = proj_fm(Wq, "Wq", col0, NQ, ntiles)
        t1, t2, r1, r2 = next_tmp(in_diff)
        ee, rr = t1[:, 0:NQ], t2[:, 0:NQ]
        ACTV(ee, psrc, AF.Exp, [pres], r1, scale=-1.0)
        TS("dve", ee, ee, 1.0, ALU.add, r1, r1)
        RECIP(rr, ee, r1, r2)
        TT("dve", dest, psrc, rr, ALU.mult, [pres] + r2, [dres])

    def load_tabs(seq, col0, NQ):
        src = ropeP if seq["kind"] == "p" else ropeS
        DMA("sp", tabs[:, :, 0:NQ], src[:, :, col0:col0 + NQ], [], ["tabs"])

    sS = [0]
    wS = [0]
    bE = [0]

    def next_E():
        k = bE[0]; bE[0] = (bE[0] + 1) % 4
        return Bp[2 + k // 2][:, (k % 2) * 512:(k % 2) * 512 + 512], f"B{2 + k // 2}{'ab'[k % 2]}"

    PB_O = [(psTb[0], "psO0"), (psTb[1], "psO1")]
    psPT = [psP[:, 0:512].bitcast(BF16).rearrange("p (k r) -> p k r", k=8), psP[:, 512:1024].bitcast(BF16).rearrange("p (k r) -> p k r", k=8)]
    PB_P = [(psPT[0], "psP0"), (psPT[1], "psP1")]

    def phase1(seq, l):
        nb = seq["S"] // 128
        chunks = [[(t, 128), (t + 1, 128)] for t in range(0, nb, 2)] + [[(nb, NMETA)]]

        def pre_a(ci):
            fo = 2 * (ci % 2)
            h_loads(seq, l, chunks[ci], fo)
            h_compute(chunks[ci], fo, on_act=True)

        def pre_b(ci):
            fo = 2 * (ci % 2)
            h_transposes(chunks[ci], PB_O, fo, HT_MAIN_P1 if ci % 2 == 0 else HT_ALT, on_act=True)

        HT_MAIN_P1 = (hT, ["hT0", "hT1"])
        load_tabs(seq, chunks[0][0][0] * 128, sum(r for _, r in chunks[0]))
        pre_a(0)
        pre_b(0)
        for ci, tiles in enumerate(chunks):
            hbuf = HT_MAIN_P1 if ci % 2 == 0 else HT_ALT
            hb, hres = hbuf
            col0 = tiles[0][0] * 128
            NQ = sum(r for _, r in tiles)
            nt = len(tiles)
            cid = tiles[0][0]
            if ci + 1 < len(chunks):
                pre_a(ci + 1)
            for h in range(4):
                psrc, pres = proj_fm(Wkv, "Wkv", h * 128, NQ, nt, hbuf)
                rope_to(psrc, pres, NQ, [(0, 128, KTB[:, h, col0:col0 + NQ])], f"KTB{h}_{cid}", tmp=p1_tmp())
            psrc, pres = proj_fm(Wkv, "Wkv", 512, NQ, nt, hbuf)
            rope_to(psrc, pres, NQ, [(0, 128, KTA[:, col0:col0 + NQ])], f"KTA_{cid}", tmp=p1_tmp())
            if ci + 1 < len(chunks):
                pre_b(ci + 1)
            for ti, (tile_idx, Rr) in enumerate(tiles):
                half = pp[0]; pp[0] ^= 1
                out = psP[0:Rr, half * 512:(half + 1) * 512]
                for kc in range(8):
                    MM(out, hb[:, kc, ti * 128:ti * 128 + Rr], Wkv[:, kc, 640:1152], kc == 0, kc == 7, hres + ["Wkv"], [f"psP{half}"])
                ACOPY(VB[0:Rr, tile_idx, :], out, [f"psP{half}"], [f"VB_{tile_idx}"])
                half = pp[0]; pp[0] ^= 1
                out2 = psP[0:Rr, half * 512:half * 512 + 128]
                for kc in range(8):
                    MM(out2, hb[:, kc, ti * 128:ti * 128 + Rr], Wkv[:, kc, 1152:1280], kc == 0, kc == 7, hres + ["Wkv"], [f"psP{half}"])
                ACOPY(VA[0:Rr, tile_idx, :], out2, [f"psP{half}"], [f"VA_{tile_idx}"])
            if ci + 1 < len(chunks):
                nxt = chunks[ci + 1]
                load_tabs(seq, nxt[0][0] * 128, sum(r for _, r in nxt))

    def front_units(seq, l, tiles, slot, in_diff):
        col0 = tiles[0][0] * 128
        NQ = sum(r for _, r in tiles)
        nt = len(tiles)
        units = []
        units.append(lambda: (load_tabs(seq, col0, NQ), h_loads(seq, l, tiles)))
        if in_diff:
            units.append(lambda: h_compute_staged(tiles))
            units.append(lambda: None)
        else:
            units.append(lambda: h_compute(tiles, 0, on_act=True, scale_eng="dve"))
        units.append(lambda: h_transposes(tiles, PB_P if in_diff else PB_O))

        def qa(j):
            psrc, pres = proj_fm(Wq, "Wq", j * 128, NQ, nt)
            return lambda: rope_to(psrc, pres, NQ, [(0, 64, QTA[0:64, 0, j, 0:NQ]), (64, 128, QTA[64:128, 1, j, 0:NQ])],
                                   f"QTA{j}", in_diff)

        def qb(h):
            psrc, pres = proj_fm(Wq, "Wq", 512 + h * 128, NQ, nt)
            return lambda: rope_to(psrc, pres, NQ, [(0, 64, QTB[0:64, slot, h, 0, 0:NQ]), (64, 128, QTB[64:128, slot, h, 1, 0:NQ])],
                                   f"QTB{slot}_{h}", in_diff)

        def gt(col0, dest, dres):
            psrc, pres = proj_fm(Wq, "Wq", col0, NQ, nt)
            return lambda: gate_tail(psrc, pres, NQ, dest, dres, in_diff)
        for j in range(4):
            units.append(lambda j=j: qa(j))
        for j in range(4):
            units.append(lambda j=j: gt(1024 + j * 128, GA[:, j, 0:NQ], f"GA{j}"))
        for h in range(4):
            units.append(lambda h=h: qb(h))
        for h in range(4):
            units.append(lambda h=h: gt(1536 + h * 128, GB[:, slot, h, 0:NQ], f"GB{slot}_{h}"))
        return units

    def window_block(nb, S, qc0, QB, kl):
        W4 = 4 * QB
        items = [(g, ki) + tuple(kl[ki]) for g in range(2) for ki in range(len(kl))]
        st = {}

        def issue_S(i):
            g, ki, kt, M, kc0, mk = items[i]
            half = wS[0]; wS[0] ^= 1
            sout = psS[0:M, half * 512:half * 512 + W4]
            kres = f"KTA_{(kt // 2) * 2 if kt < nb else nb}"
            MM(sout, KTA[:, kc0:kc0 + M], QTA[:, g, :, qc0:qc0 + QB], True, mk is None, [kres] + [f"QTA{j}" for j in range(4)], [f"psS{half}"])
            if mk is not None:
                MM(sout, ident[0:M, 0:M], masks[0:M, mk:mk + QB].unsqueeze(1).broadcast_to([M, 4, QB]), False, True,
                   ["ident", "masks"], [f"psS{half}"])
            E, eres = next_E()
            ACTV(E[0:M, 0:W4], sout, AF.Exp, [f"psS{half}"], [eres], scale=SCALE)
            st[i] = (E, eres)

        def issue_AV(i):
            g, ki, kt, M, kc0, mk = items[i]
            E, eres = st.pop(i)
            first, last = ki == 0, ki == len(kl) - 1
            MM(psOb[g][:, 0:W4], VA[0:M, kt, :], E[0:M, 0:W4], first, last, [eres, f"VA_{kt}"], [f"psO{g}"])
            MM(psSb[0][:, 0:W4], onesg[0:M, g, :], E[0:M, 0:W4], first and g == 0, last and g == 1, [eres, "onesg"], ["psSum0"])

        n = len(items)
        issue_S(0)
        if n > 1:
            issue_S(1)
        for i in range(n):
            if i + 2 < n:
                issue_S(i + 2)
            issue_AV(i)
        def finalize():
            F3 = Fp[3]
            den = F3[:, 0:W4].rearrange("p (a b) -> p a b", a=4)
            TT("dve", den, psSb[0][:, 0:W4].rearrange("p (a b) -> p a b", a=4), sinkbc[:, :].unsqueeze(2).broadcast_to([128, 4, QB]),
               ALU.add, ["psSum0", "sinkbc"], RF3A)
            RECIP(F3[:, 0:W4], F3[:, 0:W4], RF3A, RF3A)
            for g in range(2):
                TT("dve", F3[g * 64:(g + 1) * 64, 512:512 + W4], psOb[g][g * 64:(g + 1) * 64, 0:W4], F3[g * 64:(g + 1) * 64, 0:W4],
                   ALU.mult, [f"psO{g}"] + RF3A, RF3B)
            TT("pool", oT[:, 0:4, qc0:qc0 + QB], F3[:, 512:512 + W4].rearrange("p (a b) -> p a b", a=4), GA[:, :, qc0:qc0 + QB], ALU.mult,
               RF3B + [f"GA{j}" for j in range(4)], ["oTA"])
        return finalize

    def window_blocks(seq, tiles):
        S = seq["S"]; nb = S // 128
        if tiles[0][0] == nb:
            return [lambda: window_block(nb, S, 0, NMETA, [(nb, NMETA, S, None), (0, 128, 0, 256)])]
        out = []
        for bi, (t, _) in enumerate(tiles):
            kl = []
            if t >= 1:
                kl.append((t - 1, 128, (t - 1) * 128, 0))
            kl.append((t, 128, t * 128, None))
            if t + 1 < nb:
                kl.append((t + 1, 128, (t + 1) * 128, 128))
            kl.append((nb, NMETA, S, None))
            out.append(lambda bi=bi, kl=kl: window_block(nb, S, bi * 128, 128, kl))
        return out

    def window_all(seq, tiles):
        for th in window_blocks(seq, tiles):
            th()()

    post2_done = [False]
    last_post2 = [None]

    def diff_all(nb, NQ, ktiles, slot, units, reload_x):
        W2 = 2 * NQ
        n = len(ktiles)
        pending = [None]
        tails = []
        reloaded = [False]
        next_unit = [1]
        units = list(units)
        total_steps = 4 * n
        stride = max(1, (total_steps - 2) // (len(units) + 1)) if units else 0
        step = [0]

        def post1(h):
            p = h % 2
            psO, psSum = psOb[p], psSb[p]
            F2, F3 = Fp[2], Fp[3]
            RECIP(F2[:, 0:W2], psSum[:, 0:W2], [f"psSum{p}"], RF2A)
            TT("dve", F2[:, 512:512 + W2], psO[:, 0:W2], F2[:, 0:W2], ALU.mult, [f"psO{p}"] + RF2A, RF2B)
            STT("dve", F3[:, 0:NQ], F2[:, 512 + NQ:512 + W2], small[:, C_NLAM:C_NLAM + 1], F2[:, 512:512 + NQ], ALU.mult, ALU.add,
                RF2B + ["nlam"], RF3A)
            TT("pool", F3[:, 256:256 + NQ], F3[:, 0:NQ], F3[:, 0:NQ], ALU.mult, RF3A, RF3A)

        def post2(h):
            p = h % 2
            psSum = psSb[p]
            F3 = Fp[3]
            post2_done[0] = True
            MM(psSum[:, 0:NQ], onesf[:, :], F3[:, 256:256 + NQ], True, True, RF3A + ["onesf"], [f"psSum{p}"])
            ACTV(F3[:, 512:512 + NQ], psSum[:, 0:NQ], AF.Ln, [f"psSum{p}", "small_init"], RF3B, scale=1.0 / 128,
                 bias=small[:, C_EPS:C_EPS + 1])
            ACTV(F3[:, 512:512 + NQ], F3[:, 512:512 + NQ], AF.Exp, RF3B, RF3B, scale=-0.5)
            STT("dve", F3[:, 768:768 + NQ], F3[:, 0:NQ], small[:, C_SGR:C_SGR + 1], F3[:, 512:512 + NQ], ALU.mult, ALU.mult,
                RF3A + RF3B + ["sg"], RF3B)
            TT("pool", oT[:, 4 + h, 0:NQ], F3[:, 768:768 + NQ], GB[:, slot, h, 0:NQ], ALU.mult, RF3B + [f"GB{slot}_{h}"], [f"oTB{h}"])

        K0 = min(max(5, n // 2), n - 1)
        steps = [(h, ki) for h in range(4) for ki in range(n)]
        st = {}

        def issue_S(i):
            h, ki = steps[i]
            kt, M, kc0 = ktiles[ki]
            half = sS[0]; sS[0] ^= 1
            sres = f"psS{half}"
            kres = f"KTB{h}_{(kt // 2) * 2 if kt < nb else nb}"
            MM(psS[0:M, half * 512:half * 512 + W2], KTB[:, h, kc0:kc0 + M], QTB[:, slot, h, :, 0:NQ], True, True,
               [kres, f"QTB{slot}_{h}"], [sres])
            E, eres = next_E()
            ACTV(E[0:M, 0:W2], psS[0:M, half * 512:half * 512 + W2], AF.Exp, [sres], [eres], scale=SCALE)
            st[i] = (E, eres)

        def issue_AV(i):
            h, ki = steps[i]
            p = h % 2
            kt, M, kc0 = ktiles[ki]
            E, eres = st.pop(i)
            first, last = ki == 0, ki == n - 1
            MM(psOb[p][:, 0:W2], VB[0:M, kt, h * 128:(h + 1) * 128], E[0:M, 0:W2], first, last, [eres, f"VB_{kt}"], [f"psO{p}"])
            MM(psSb[p][:, 0:W2], onesb[0:M, :], E[0:M, 0:W2], first, last, [eres, "onesb"], [f"psSum{p}"])

        ns = len(steps)
        issue_S(0)
        if ns > 1:
            issue_S(1)
        for i in range(ns):
            h, ki = steps[i]
            if i + 2 < ns:
                issue_S(i + 2)
            issue_AV(i)
            if ki == K0 and pending[0] is not None:
                post2(pending[0]); pending[0] = None
            step[0] += 1
            for tl in [t for t in tails if t[0] <= step[0]]:
                tails.remove(tl)
                nxt = tl[1]()
                if callable(nxt):
                    tails.append((step[0] + 2, nxt))
            if units and not tails and step[0] >= next_unit[0]:
                tail = units.pop(0)()
                next_unit[0] = step[0] + stride
                if callable(tail):
                    tails.append((step[0] + 2, tail))
            elif not units and not tails and not reloaded[0]:
                reload_x(); reloaded[0] = True
            if ki == n - 1:
                post1(h)
                pending[0] = h
        while tails:
            nxt = tails.pop(0)[1]()
            if callable(nxt):
                tails.insert(0, (0, nxt))
        while units:
            tail = units.pop(0)()
            while callable(tail):
                tail = tail()
        if not reloaded[0]:
            reload_x()
        last_post2[0] = (lambda hh=pending[0]: post2(hh))

    def outproj_mm(ti, Rr, f_list):
        c0 = ti * 128
        ores = ["oTA", "oTA", "oTA", "oTA", "oTB0", "oTB1", "oTB2", "oTB3"]
        for f in f_list:
            for hf in range(2):
                MM(psP[0:Rr, hf * 512:(hf + 1) * 512], oT[:, f, c0:c0 + Rr], Wo[:, f, hf * 512:(hf + 1) * 512], f == 0, f == 7,
                   [ores[f], "Wo"], [f"psP{hf}"])

    def outproj_post(seq, l, ti, tile_idx, Rr):
        Fy = Fp[2 + ti]
        fy = (RF2A + RF2B) if ti == 0 else (RF3A + RF3B)
        ssc = C_SS2 + ti
        MEMSET("dve", col(ssc, Rr), 0.0, ["small_init"], [f"ssy{ti}"])
        ACTV(Fy[0:Rr, :], psP[0:Rr, :], AF.Square, ["psP0", "psP1", f"ssy{ti}"], fy + [f"ssy{ti}"], accum_out=col(ssc, Rr))
        rstd_chain(ssc, C_LN2 + ti, C_RS2 + ti, Rr, 1.0 / D, f"y{ti}")
        STT("dve", Fy[0:Rr, :], psP[0:Rr, :], col(C_RS2 + ti, Rr), gpost[0:Rr, :], ALU.mult, ALU.mult,
            ["psP0", "psP1", f"rsy{ti}", "gpost"], fy)
        TT("pool", Fy[0:Rr, :], Fy[0:Rr, :], Fp[ti][0:Rr, :], ALU.add, fy + FRES[ti], fy)
        dst, dres = x_dst(seq, l, tile_idx, Rr)
        DMA("pool", dst, Fy[0:Rr, :], fy, [dres] if dres else [])

    def phase2(seq, l):
        S = seq["S"]; nb = S // 128
        chunks = [[(t, 128), (t + 1, 128)] for t in range(0, nb, 2)]
        if l < NL - 1:
            chunks.append([(nb, NMETA)])
        ktiles = [(t, 128, t * 128) for t in range(nb)] + [(nb, NMETA, S)]
        for u in front_units(seq, l, chunks[0], 0, False):
            tail = u()
            while callable(tail):
                tail = tail()
        window_all(seq, chunks[0])
        for ci, tiles in enumerate(chunks):
            slot = ci % 2
            NQ = sum(r for _, r in tiles)
            units = []
            if ci + 1 < len(chunks):
                units = front_units(seq, l, chunks[ci + 1], 1 - slot, True)
            diff_all(nb, NQ, ktiles, slot, units, lambda tiles=tiles: h_loads(seq, l, tiles))
            if DBG and l == 0 and seq is seqs[0] and ci == 0:
                for nme, t, rs in (("oT", oT, ["oTA"] + [f"oTB{j}" for j in range(4)]),
                                   ("KTB", KTB, [f"KTB{hh}_{c}" for hh in range(4) for c in list(range(0, nb, 2)) + [nb]]),
                                   ("KTA", KTA, [f"KTA_{c}" for c in list(range(0, nb, 2)) + [nb]]),
                                   ("VB", VB, [f"VB_{c}" for c in range(nb + 1)]), ("VA", VA, [f"VA_{c}" for c in range(nb + 1)])):
                    DMA("pool", dbg[nme], t[:], rs, [])
            for ti, (tile_idx, Rr) in enumerate(tiles):
                outproj_mm(ti, Rr, range(7))
                if ti == 0:
                    last_post2[0]()
                outproj_mm(ti, Rr, [7])
                outproj_post(seq, l, ti, tile_idx, Rr)
            if ci + 1 < len(chunks):
                window_all(seq, chunks[ci + 1])

    for l in range(NL):
        if l == 0:
            load_wkv(0)
        load_wq_wo(l)
        load_params(l)
        for si, seq in enumerate(seqs):
            phase1(seq, l)
            if si == len(seqs) - 1 and l + 1 < NL:
                load_wkv(l + 1)
            phase2(seq, l)

    R.finalize()
    with ExitStack() as ss_:
        esems = {e: ss_.enter_context(nc.semaphore(f"sem_{e}")) for e in Rec.ENGS}
        dsems = {}
        for q, n in R.n_dma.items():
            for i in range(n):
                dsems[(q, i)] = ss_.enter_context(nc.semaphore(f"dsem_{q}{i}"))
        block = ss_.enter_context(nc.Block())

        @block.tensor
        def _(t):
            R.play("pe", t, esems, dsems)

        @block.scalar
        def _(s):
            R.play("act", s, esems, dsems)

        @block.vector
        def _(v):
            R.play("dve", v, esems, dsems)

        @block.gpsimd
        def _(g):
            R.play("pool", g, esems, dsems)

        @block.sync
        def _(sy):
            R.play("sp", sy, esems, dsems)
    es.close()
    return nc


def rope_table(S):
    LP = S + NMETA
    pos = np.concatenate([np.arange(S, dtype=np.float32) + NMETA, np.arange(NMETA, dtype=np.float32)])
    inv_freq = (1.0 / (ROPE_THETA ** (np.arange(0, 64, 2, dtype=np.float32) / 64.0))).astype(np.float32)
    ang = (pos[None, :] * inv_freq[:, None]).astype(np.float32)
    cos = np.cos(ang).astype(np.float32)
    sin = np.sin(ang).astype(np.float32)
    tab = np.zeros((128, 2, LP), np.float32)
    for p in range(128):
        f = p % 32
        tab[p, 0] = cos[f]
        tab[p, 1] = -sin[f] if (p % 64) >= 32 else sin[f]
    return tab


def const_tables():
    ident = np.eye(128, dtype=np.float32)
    k = np.arange(128)[:, None]
    q = np.arange(128)[None, :]
    NEG = -30000.0
    masks = np.zeros((128, 272), np.float32)
    masks[:, 0:128] = np.where(k >= q, 0.0, NEG)
    masks[:, 128:256] = np.where(k <= q, 0.0, NEG)
    u = np.arange(128)[:, None]
    m = np.arange(NMETA)[None, :]
    masks[:, 256:272] = np.where(u <= 112 + m, 0.0, NEG)
    return ident, masks


_CACHE = {}


def kernel(x_prompt, x_sample, meta_tokens, w_in, w_out, pre_norm_g, post_norm_g, sink_logits,
           lambda_q1, lambda_k1, lambda_q2, lambda_k2, subln_g):
    f = lambda a: np.ascontiguousarray(np.asarray(a, dtype=np.float32))
    x_prompt, x_sample = f(x_prompt), f(x_sample)
    NL = w_in.shape[0]
    BP, SPn = x_prompt.shape[0], x_prompt.shape[1]
    BS, SSn = x_sample.shape[0], x_sample.shape[1]
    nP, nS = BP // N_CORES, BS // N_CORES
    cfg = Cfg(n_layers=NL, n_prompt=nP, s_prompt=SPn, n_sample=nS, s_sample=SSn)
    nc = build_program(cfg)
    ident, masks = const_tables()
    common = dict(meta=f(meta_tokens), w_in=f(w_in), w_out=f(w_out), pre_g=f(pre_norm_g), post_g=f(post_norm_g),
                  sink=f(sink_logits), lq1=f(lambda_q1), lk1=f(lambda_k1), lq2=f(lambda_q2), lk2=f(lambda_k2),
                  subln=f(subln_g), ropeP=rope_table(SPn), ropeS=rope_table(SSn), cident=ident, cmasks=masks)
    in_maps = []
    for c in range(N_CORES):
        m = dict(common)
        m["xp"] = x_prompt[c * nP:(c + 1) * nP]
        m["xs"] = x_sample[c * nS:(c + 1) * nS]
        in_maps.append(m)
    res = run_bass_kernel_spmd(nc, in_maps, core_ids=list(range(N_CORES)))
    ypo = np.concatenate([np.asarray(r["yp"]) for r in res.results], axis=0).astype(np.float32)
    yso = np.concatenate([np.asarray(r["ys"]) for r in res.results], axis=0).astype(np.float32)
    return (ypo, yso)
```

```python
import math
from contextlib import ExitStack

import numpy as np
import concourse.bass as bass
import concourse.mybir as mybir
from concourse.bass_utils import run_bass_kernel_spmd

F32 = mybir.dt.float32
BF16 = mybir.dt.bfloat16
AF = mybir.ActivationFunctionType
ALU = mybir.AluOpType

D = 1024
NMETA = 16
EPS = 1e-6
SCALE = 0.125
ROPE_THETA = 10000.0
N_CORES = 8


class Op:
    __slots__ = ("eng", "fn", "deps", "sig", "cnt", "dma", "vsem", "vval")

    def __init__(self, eng, fn, dma):
        self.eng = eng
        self.fn = fn
        self.deps = ()
        self.sig = False
        self.cnt = 0
        self.dma = dma
        self.vsem = None
        self.vval = 0


class Rec:
    ENGS = ("pe", "act", "dve", "pool", "sp")

    def __init__(self, n_dma_sems):
        self.ops = {e: [] for e in self.ENGS}
        self.res_w = {}
        self.res_r = {}
        self.n_dma = n_dma_sems
        self.dma_rr = {e: 0 for e in n_dma_sems}
        self.dma_last = {}
        self.dma_val = {}

    def emit(self, eng, fn, reads=(), writes=(), dma=False):
        op = Op(eng, fn, dma)
        deps = []
        seen = set()
        res_w, res_r = self.res_w, self.res_r
        for r in reads:
            w = res_w.get(r)
            if w is not None and id(w) not in seen:
                seen.add(id(w)); deps.append(w)
        for r in writes:
            w = res_w.get(r)
            if w is not None and id(w) not in seen:
                seen.add(id(w)); deps.append(w)
            for rd in res_r.get(r, ()):
                if id(rd) not in seen:
                    seen.add(id(rd)); deps.append(rd)
        if dma:
            k = (eng, self.dma_rr[eng])
            self.dma_rr[eng] = (self.dma_rr[eng] + 1) % self.n_dma[eng]
            prev = self.dma_last.get(k)
            if prev is not None and id(prev) not in seen:
                seen.add(id(prev)); deps.append(prev)
            self.dma_last[k] = op
            self.dma_val[k] = self.dma_val.get(k, 0) + 16
            op.vsem = k
            op.vval = self.dma_val[k]
        keep = []
        for d in deps:
            if d.dma or dma:
                keep.append(d)
            elif d.eng == eng and eng == "pe":
                continue
            else:
                keep.append(d)
        op.deps = keep
        for r in reads:
            res_r.setdefault(r, []).append(op)
        for r in writes:
            res_w[r] = op
            res_r[r] = []
        self.ops[eng].append(op)
        return op

    def finalize(self):
        for e in self.ENGS:
            for op in self.ops[e]:
                for d in op.deps:
                    if not d.dma:
                        d.sig = True
        for e in self.ENGS:
            c = 0
            for op in self.ops[e]:
                if op.sig and not op.dma:
                    c += 1
                    op.cnt = c

    def play(self, eng, handle, esems, dsems):
        waited = {}
        for op in self.ops[eng]:
            for d in op.deps:
                if d.dma:
                    key = ("d",) + d.vsem
                    sem = dsems[d.vsem]
                    val = d.vval
                else:
                    key = ("e", d.eng)
                    sem = esems[d.eng]
                    val = d.cnt
                if waited.get(key, 0) < val:
                    handle.wait_ge(sem, val)
                    waited[key] = val
            ins = op.fn(handle)
            if op.dma:
                ins.then_inc(dsems[op.vsem], 16)
            elif op.sig:
                ins.then_inc(esems[eng], 1)
        if eng == "sp":
            for k, v in self.dma_val.items():
                handle.wait_ge(dsems[k], v)


class Cfg:
    def __init__(self, n_layers=4, n_prompt=2, s_prompt=4096, n_sample=2, s_sample=2048, lambda_layers=None):
        self.n_layers = n_layers
        self.n_prompt = n_prompt
        self.s_prompt = s_prompt
        self.n_sample = n_sample
        self.s_sample = s_sample


def build_program(cfg):
    nc = bass.Bass("TRN2", target_bir_lowering=False)
    NL = cfg.n_layers
    SP_, SS_ = cfg.s_prompt, cfg.s_sample
    nP, nS = cfg.n_prompt, cfg.n_sample
    LPP, LPS = SP_ + NMETA, SS_ + NMETA
    LPM = max(LPP, LPS)
    NTM = max(SP_, SS_) // 128 + 1

    def din(name, shape, dt=F32):
        return nc.dram_tensor(name, list(shape), dt, kind="ExternalInput").ap()

    xp = din("xp", [nP, SP_, D])
    xs = din("xs", [nS, SS_, D])
    meta = din("meta", [NMETA, D])
    w_in = din("w_in", [NL, D, 3328])
    w_out = din("w_out", [NL, D, D])
    pre_g = din("pre_g", [NL, D])
    post_g = din("post_g", [NL, D])
    sink = din("sink", [NL, 8])
    lq1 = din("lq1", [NL, 64]); lk1 = din("lk1", [NL, 64])
    lq2 = din("lq2", [NL, 64]); lk2 = din("lk2", [NL, 64])
    subln = din("subln", [NL, 128])
    ropeP = din("ropeP", [128, 2, LPP])
    ropeS = din("ropeS", [128, 2, LPS])
    cident = din("cident", [128, 128])
    cmasks = din("cmasks", [128, 272])
    DBG = getattr(cfg, "debug", False)
    if DBG:
        dbg = {n: nc.dram_tensor("dbg_" + n, sh, dt, kind="ExternalOutput").ap() for n, sh, dt in (
            ("KTB", [128, 4, LPM], BF16), ("KTA", [128, LPM], BF16), ("VB", [128, NTM, 512], BF16), ("VA", [128, NTM, 128], BF16),
            ("oT", [128, 8, 256], BF16))}
    yp = nc.dram_tensor("yp", [nP, SP_, D], F32, kind="ExternalOutput").ap()
    ys = nc.dram_tensor("ys", [nS, SS_, D], F32, kind="ExternalOutput").ap()

    seqs = []
    row = 0
    for i in range(max(nP, nS)):
        if i < nP:
            seqs.append(dict(kind="p", idx=i, S=SP_, base=row)); row += LPP
        if i < nS:
            seqs.append(dict(kind="s", idx=i, S=SS_, base=row)); row += LPS
    tot_rows = row
    xscr = [nc.dram_tensor("xscrA", [tot_rows, D], F32).ap(), nc.dram_tensor("xscrB", [tot_rows, D], F32).ap()]

    R = Rec({"sp": 12, "pool": 6})
    es = ExitStack()

    def sb(name, shape, dt):
        return es.enter_context(nc.sbuf_tensor(name, list(shape), dt))

    def ps(name, shape, dt):
        return es.enter_context(nc.psum_tensor(name, list(shape), dt))

    KTB = sb("KTB", [128, 4, LPM], BF16)
    VB = sb("VB", [128, NTM, 512], BF16)
    KTA = sb("KTA", [128, LPM], BF16)
    VA = sb("VA", [128, NTM, 128], BF16)
    Wkv = sb("Wkv", [128, 8, 1280], BF16)
    Wq = sb("Wq", [128, 8, 2048], BF16)
    Wo = sb("Wo", [128, 8, 1024], BF16)
    gcol = sb("gcol", [128, 8], F32)
    gpost = sb("gpost", [128, D], F32)
    ident = sb("ident", [128, 128], BF16)
    masks = sb("masks", [128, 272], BF16)
    onesg = sb("onesg", [128, 2, 128], BF16)
    onesb = sb("onesb", [128, 128], BF16)
    onesf = sb("onesf", [128, 128], F32)
    Fp = [sb(f"F{i}", [128, D], F32) for i in range(4)]
    Bp = [sb(f"B{i}", [128, D], BF16) for i in range(4)]
    hT = sb("hT", [128, 8, 256], BF16)
    tabs = sb("tabs", [128, 2, 256], F32)
    QTA = sb("QTA", [128, 2, 4, 256], BF16)
    QTB = sb("QTB", [128, 2, 4, 2, 256], BF16)
    GA = sb("GA", [128, 4, 256], BF16)
    GB = sb("GB", [128, 2, 4, 256], BF16)
    oT = sb("oT", [128, 8, 256], BF16)
    small = sb("small", [128, 48], F32)
    sinkbc = sb("sinkbc", [128, 4], F32)

    C_EPS, C_SS0, C_SS1, C_LN0, C_LN1, C_RS0, C_RS1 = 0, 1, 2, 3, 4, 5, 6
    C_SS2, C_LN2, C_RS2 = 8, 10, 12
    C_D1, C_D2, C_E1, C_E2, C_T, C_NLAM, C_SG, C_SGR, C_NEG1 = 20, 21, 22, 23, 24, 25, 26, 27, 28
    gcolkv = small[:, 32:40]

    psS = ps("psS", [128, 1024], F32)
    psOO = ps("psOO", [128, 1024], F32)
    psSS = ps("psSS", [128, 1024], F32)
    psP = ps("psP", [128, 1024], F32)
    psOb = [psOO[:, 0:512], psOO[:, 512:1024]]
    psSb = [psSS[:, 0:512], psSS[:, 512:1024]]
    psTb = [psOO[:, 0:512].bitcast(BF16).rearrange("p (k r) -> p k r", k=8),
            psOO[:, 512:1024].bitcast(BF16).rearrange("p (k r) -> p k r", k=8)]

    emit = R.emit

    def MM(out, lhsT, rhs, start, stop, reads, writes):
        emit("pe", lambda e: e.matmul(out, lhsT=lhsT, rhs=rhs, start=start, stop=stop), reads, writes)

    def TR(out, in_, idn, reads, writes):
        emit("pe", lambda e: e.transpose(out=out, in_=in_, identity=idn), reads, writes)

    def ACTV(out, in_, func, reads, writes, scale=1.0, bias=None, accum_out=None):
        kw = {}
        if bias is not None:
            kw["bias"] = bias
        if accum_out is not None:
            kw["accum_out"] = accum_out
        emit("act", lambda e: e.activation(out=out, in_=in_, func=func, scale=scale, **kw), reads, writes)

    def ACOPY(out, in_, reads, writes):
        emit("act", lambda e: e.copy(out=out, in_=in_), reads, writes)

    def TT(eng, out, in0, in1, op, reads, writes):
        emit(eng, lambda e: e.tensor_tensor(out=out, in0=in0, in1=in1, op=op), reads, writes)

    def STT(eng, out, in0, scalar, in1, op0, op1, reads, writes, accum_out=None):
        if accum_out is None:
            emit(eng, lambda e: e.scalar_tensor_tensor(out=out, in0=in0, scalar=scalar, in1=in1, op0=op0, op1=op1), reads, writes)
        else:
            emit(eng, lambda e: e.scalar_tensor_tensor(out=out, in0=in0, scalar=scalar, in1=in1, op0=op0, op1=op1,
                                                      accum_out=accum_out), reads, writes)

    def TS(eng, out, in0, s1, op0, reads, writes):
        emit(eng, lambda e: e.tensor_scalar(out=out, in0=in0, scalar1=s1, scalar2=None, op0=op0), reads, writes)

    def RECIP(out, in_, reads, writes):
        emit("dve", lambda e: e.reciprocal(out=out, in_=in_), reads, writes)

    def VCOPY(out, in_, reads, writes):
        emit("dve", lambda e: e.tensor_copy(out=out, in_=in_), reads, writes)

    def MEMSET(eng, ap, val, reads, writes):
        emit(eng, lambda e: e.memset(ap, val), reads, writes)

    def DMA(eng, out, in_, reads, writes):
        emit(eng, lambda e: e.dma_start(out=out, in_=in_), reads, writes, dma=True)

    def col(c, Rr=128):
        return small[0:Rr, c:c + 1]

    DMA("pool", ident[:], cident, [], ["ident"])
    DMA("pool", masks[:], cmasks, [], ["masks"])
    MEMSET("pool", onesb[:], 1.0, [], ["onesb"])
    MEMSET("pool", onesf[:], 1.0, [], ["onesf"])
    MEMSET("pool", small[:], 0.0, [], ["small"])
    MEMSET("pool", col(C_EPS), EPS, [], ["small"])
    MEMSET("pool", col(C_NEG1), -1.0, [], ["small", "small_init"])
    MEMSET("pool", QTA[:], 0.0, [], [f"QTA{j}" for j in range(4)])
    MEMSET("pool", QTB[:], 0.0, [], [f"QTB{sl}_{h}" for sl in range(2) for h in range(4)])
    MEMSET("pool", onesg[:], 0.0, [], ["onesg"])
    MEMSET("pool", onesg[:, 0, 0:64], 1.0, [], ["onesg"])
    MEMSET("pool", onesg[:, 1, 64:128], 1.0, [], ["onesg"])

    def load_wkv(l):
        src = w_in[l].rearrange("(c p) n -> p c n", p=128)
        for (d0, s0, n) in ((0, 1792, 512), (512, 512, 128), (640, 2304, 512), (1152, 640, 128)):
            DMA("pool", Wkv[:, :, d0:d0 + n], src[:, :, s0:s0 + n], [], ["Wkv"])
        for kc in range(8):
            DMA("pool", gcolkv[:, kc:kc + 1], pre_g[l:l + 1, kc * 128:(kc + 1) * 128].rearrange("o p -> p o"), ["small_init"], ["gcolkv"])
        for kc in range(8):
            TS("dve", Wkv[:, kc, :], Wkv[:, kc, :], gcolkv[:, kc:kc + 1], ALU.mult, ["Wkv", "gcolkv"], ["Wkv"])

    def load_wq_wo(l):
        src = w_in[l].rearrange("(c p) n -> p c n", p=128)
        for (dbase, sbase) in ((0, 0), (1024, 768)):
            for j in range(4):
                for g in range(2):
                    d0 = dbase + j * 128 + g * 64
                    s0 = sbase + (g * 4 + j) * 64
                    DMA("pool", Wq[:, :, d0:d0 + 64], src[:, :, s0:s0 + 64], [], ["Wq"])
        DMA("pool", Wq[:, :, 512:1024], src[:, :, 1280:1792], [], ["Wq"])
        DMA("pool", Wq[:, :, 1536:2048], src[:, :, 2816:3328], [], ["Wq"])
        for kc in range(8):
            DMA("pool", gcol[:, kc:kc + 1], pre_g[l:l + 1, kc * 128:(kc + 1) * 128].rearrange("o p -> p o"), [], ["gcol"])
        for kc in range(8):
            TS("dve", Wq[:, kc, :], Wq[:, kc, :], gcol[:, kc:kc + 1], ALU.mult, ["Wq", "gcol"], ["Wq"])
        wo = w_out[l]
        for j in range(4):
            for g in range(2):
                r0 = (g * 4 + j) * 64
                DMA("pool", Wo[g * 64:(g + 1) * 64, j, :], wo[r0:r0 + 64, :], [], ["Wo"])
        DMA("pool", Wo[:, 4:8, :], wo[512:1024, :].rearrange("(c p) n -> p c n", p=128), [], ["Wo"])

    def load_params(l):
        lambda_init = 0.8 - 0.6 * math.exp(-0.3 * l)
        DMA("sp", gpost[:], post_g[l:l + 1, :].broadcast_to([128, D]), [], ["gpost"])
        lamt = Fp[3][:, 512:768].rearrange("p (a b) -> p a b", a=4)
        for i, t in enumerate((lq1, lk1, lq2, lk2)):
            DMA("sp", lamt[:, i, :], t[l:l + 1, :].broadcast_to([128, 64]), [], ["F3b"])
        for g in range(2):
            DMA("sp", sinkbc[g * 64:(g + 1) * 64, :], sink[l:l + 1, g * 4:(g + 1) * 4].broadcast_to([64, 4]), [], ["sinkbc"])
        DMA("sp", col(C_SG), subln[l:l + 1, :].rearrange("o p -> p o"), ["small_init"], ["sg_raw"])
        junk = Fp[3]
        MEMSET("dve", small[:, C_D1:C_D1 + 2], 0.0, ["small_init"], ["d1", "d2"])
        STT("dve", junk[:, 0:64], lamt[:, 0, :], 1.0, lamt[:, 1, :], ALU.mult, ALU.mult, ["F3b", "d1"], ["F3a", "d1"],
            accum_out=col(C_D1))
        STT("dve", junk[:, 64:128], lamt[:, 2, :], 1.0, lamt[:, 3, :], ALU.mult, ALU.mult, ["F3b", "d2"], ["F3a", "d2"],
            accum_out=col(C_D2))
        ACTV(small[:, C_E1:C_E1 + 2], small[:, C_D1:C_D1 + 2], AF.Exp, ["d1", "d2"], ["e12"])
        TT("dve", col(C_T), col(C_E2), col(C_E1), ALU.subtract, ["e12"], ["lt"])
        TS("dve", col(C_NLAM), col(C_T), -lambda_init, ALU.add, ["lt"], ["nlam"])
        TS("dve", col(C_SGR), col(C_SG), 1.0 - lambda_init, ALU.mult, ["sg_raw"], ["sg"])
        ACTV(sinkbc[:], sinkbc[:], AF.Exp, ["sinkbc"], ["sinkbc"])

    def x_src(seq, l, tile_idx, Rr):
        S = seq["S"]
        nb = S // 128
        if l == 0:
            if tile_idx == nb:
                return meta[0:Rr, :], None
            src = xp if seq["kind"] == "p" else xs
            return src[seq["idx"], tile_idx * 128:tile_idx * 128 + Rr, :], None
        buf = (l - 1) % 2
        r0 = seq["base"] + tile_idx * 128
        return xscr[buf][r0:r0 + Rr, :], f"xs{buf}_{seq['base']}_{tile_idx}"

    def x_dst(seq, l, tile_idx, Rr):
        if l == NL - 1:
            dst = yp if seq["kind"] == "p" else ys
            return dst[seq["idx"], tile_idx * 128:tile_idx * 128 + Rr, :], None
        buf = l % 2
        r0 = seq["base"] + tile_idx * 128
        return xscr[buf][r0:r0 + Rr, :], f"xs{buf}_{seq['base']}_{tile_idx}"

    def rstd_chain(ss_col, ln_col, rs_col, Rr, scale, tag):
        ACTV(col(ln_col, Rr), col(ss_col, Rr), AF.Ln, [f"ss{tag}", "small_init"], [f"ln{tag}"], scale=scale, bias=col(C_EPS, Rr))
        ACTV(col(rs_col, Rr), col(ln_col, Rr), AF.Exp, [f"ln{tag}"], [f"rs{tag}"], scale=-0.5)

    RF2A, RF2B = ["F2a0", "F2a1"], ["F2b0", "F2b1"]
    RF3A, RF3B = ["F3a"], ["F3b"]

    def tmp_set(i):
        if i < 2:
            return Fp[2][:, i * 256:(i + 1) * 256], Fp[2][:, 512 + i * 256:512 + (i + 1) * 256], [f"F2a{i}"], [f"F2b{i}"]
        j = i - 2
        return Fp[3][:, j * 256:(j + 1) * 256], Fp[3][:, 512 + j * 256:512 + (j + 1) * 256], RF3A, RF3B

    FRES = [["F0s0", "F0s1"], ["F1s0", "F1s1"], ["F2a0", "F2a1", "F2b0", "F2b1"], ["F3a", "F3b"]]
    BRES = [["B0"], ["B1"], ["B2a", "B2b"], ["B3a", "B3b"]]
    HT_MAIN = (hT, ["hT0", "hT1"])
    HT_ALT = (oT, ["oTA", "oTB0", "oTB1", "oTB2", "oTB3"])

    def h_loads(seq, l, tiles, fo=0):
        for ti, (tile_idx, Rr) in enumerate(tiles):
            src, sres = x_src(seq, l, tile_idx, Rr)
            DMA("sp", Fp[fo + ti][0:Rr, :], src, [sres] if sres else [], FRES[fo + ti])

    def h_compute(tiles, fo=0, on_act=False, scale_eng=None):
        for ti, (tile_idx, Rr) in enumerate(tiles):
            xin, xb = Fp[fo + ti], Bp[fo + ti]
            ssc = C_SS0 + ti
            MEMSET("pool" if on_act else "dve", col(ssc, Rr), 0.0, ["small_init"], [f"ssh{ti}"])
            if on_act:
                ACTV(xb[0:Rr, :], xin[0:Rr, :], AF.Square, FRES[fo + ti] + [f"ssh{ti}"], BRES[fo + ti] + [f"ssh{ti}"], accum_out=col(ssc, Rr))
            else:
                STT("dve", xb[0:Rr, :], xin[0:Rr, :], 1.0, xin[0:Rr, :], ALU.mult, ALU.mult, FRES[fo + ti] + [f"ssh{ti}"],
                    BRES[fo + ti] + [f"ssh{ti}"], accum_out=col(ssc, Rr))
        for ti, (tile_idx, Rr) in enumerate(tiles):
            rstd_chain(C_SS0 + ti, C_LN0 + ti, C_RS0 + ti, Rr, 1.0 / D, f"h{ti}")
        for ti, (tile_idx, Rr) in enumerate(tiles):
            if on_act and scale_eng is None:
                ACTV(Bp[fo + ti][0:Rr, :], Fp[fo + ti][0:Rr, :], AF.Identity, FRES[fo + ti] + [f"rsh{ti}"], BRES[fo + ti],
                     scale=col(C_RS0 + ti, Rr))
            else:
                TS(scale_eng or "dve", Bp[fo + ti][0:Rr, :], Fp[fo + ti][0:Rr, :], col(C_RS0 + ti, Rr), ALU.mult,
                   FRES[fo + ti] + [f"rsh{ti}"], BRES[fo + ti])

    def h_compute_staged(tiles):
        for ti, (tile_idx, Rr) in enumerate(tiles):
            xin, xb = Fp[ti], Bp[ti]
            ssc = C_SS0 + ti
            MEMSET("dve", col(ssc, Rr), 0.0, ["small_init"], [f"ssh{ti}"])
            STT("dve", xb[0:Rr, :], xin[0:Rr, :], 1.0, xin[0:Rr, :], ALU.mult, ALU.mult, FRES[ti] + [f"ssh{ti}"],
                BRES[ti] + [f"ssh{ti}"], accum_out=col(ssc, Rr))

        def stage_b():
            for ti, (tile_idx, Rr) in enumerate(tiles):
                rstd_chain(C_SS0 + ti, C_LN0 + ti, C_RS0 + ti, Rr, 1.0 / D, f"h{ti}")

            def stage_c():
                for ti, (tile_idx, Rr) in enumerate(tiles):
                    TS("dve", Bp[ti][0:Rr, :], Fp[ti][0:Rr, :], col(C_RS0 + ti, Rr), ALU.mult, FRES[ti] + [f"rsh{ti}"], BRES[ti])
            return stage_c
        return stage_b

    def h_transposes(tiles, pbank, fo=0, hbuf=None, on_act=False):
        hb, hres = hbuf if hbuf is not None else HT_MAIN
        for ti, (tile_idx, Rr) in enumerate(tiles):
            xb = Bp[fo + ti]
            pT, pres = pbank[ti]
            for kc in range(8):
                TR(pT[:, kc, 0:Rr], xb[0:Rr, kc * 128:(kc + 1) * 128], ident[0:Rr, 0:Rr], BRES[fo + ti] + ["ident"], [pres])
            (ACOPY if on_act else VCOPY)(hb[:, :, ti * 128:ti * 128 + Rr], pT[:, :, 0:Rr], [pres], [hres[ti]] if hbuf is None else hres)

    pp = [0]
    ff = [0]

    def next_tmp(in_diff):
        k = ff[0]
        ff[0] = (ff[0] + 1) % 4
        Fb = Fp[k // 2]
        c0 = (k % 2) * 512
        rn = [f"F{k // 2}s{k % 2}"]
        return Fb[:, c0:c0 + 256], Fb[:, c0 + 256:c0 + 512], rn, rn

    def proj_fm(Wt, wname, col0, NQ, ntiles, hbuf=None):
        half = pp[0]; pp[0] ^= 1
        out = psP[:, half * 512:half * 512 + NQ]
        hb, hres = hbuf if hbuf is not None else (hT, [f"hT{t}" for t in range(ntiles)])
        for kc in range(8):
            MM(out, Wt[:, kc, col0:col0 + 128], hb[:, kc, 0:NQ], kc == 0, kc == 7, hres + [wname], [f"psP{half}"])
        return out, f"psP{half}"

    GBf = GB[:].rearrange("p a b c -> p (a b c)").bitcast(F32)
    p1s = [0]

    def p1_tmp():
        k = p1s[0]; p1s[0] ^= 1
        return (GBf[:, k * 512:k * 512 + 256], GBf[:, k * 512 + 256:k * 512 + 512], [f"GB{k}_0", f"GB{k}_1"], [f"GB{k}_2", f"GB{k}_3"])

    def rope_to(psrc, pres, NQ, pieces, dres, in_diff=False, tmp=None):
        t1, t2, r1, r2 = tmp if tmp is not None else next_tmp(in_diff)
        TT("dve", t1[:, 0:NQ], psrc, tabs[:, 0, 0:NQ], ALU.mult, [pres, "tabs"], r1)
        for (o0, i0) in ((0, 32), (32, 0), (64, 96), (96, 64)):
            TT("dve", t2[o0:o0 + 32, 0:NQ], psrc[i0:i0 + 32, :], tabs[i0:i0 + 32, 1, 0:NQ], ALU.mult, [pres, "tabs"], r2)
        for (p0, p1, dd) in pieces:
            TT("pool", dd, t1[p0:p1, 0:NQ], t2[p0:p1, 0:NQ], ALU.add, r1 + r2, [dres])

    def gate_tail(psrc, pres, NQ, dest, dres, in_diff):
        t1, t2, r1, r2 = next_tmp(in_diff)
        ee, rr = t1[:, 0:NQ], t2[:, 0:NQ]
        ACTV(ee, psrc, AF.Exp, [pres], r1, scale=-1.0)
        TS("dve", ee, ee, 1.0, ALU.add, r1, r1)
        RECIP(rr, ee, r1, r2)
        TT("dve", dest, psrc, rr, ALU.mult, [pres] + r2, [dres])

    def gate_to(col0, NQ, ntiles, dest, dres, in_diff=False):
        psrc, pres = proj_fm(Wq, "Wq", col0, NQ, ntiles)
        t1, t2, r1, r2 = next_tmp(in_diff)
        ee, rr = t1[:, 0:NQ], t2[:, 0:NQ]
        ACTV(ee, psrc, AF.Exp, [pres], r1, scale=-1.0)
        TS("dve", ee, ee, 1.0, ALU.add, r1, r1)
        RECIP(rr, ee, r1, r2)
        TT("dve", dest, psrc, rr, ALU.mult, [pres] + r2, [dres])

    def load_tabs(seq, col0, NQ):
        src = ropeP if seq["kind"] == "p" else ropeS
        DMA("sp", tabs[:, :, 0:NQ], src[:, :, col0:col0 + NQ], [], ["tabs"])

    sS = [0]
    wS = [0]
    bE = [0]

    def next_E():
        k = bE[0]; bE[0] = (bE[0] + 1) % 4
        return Bp[2 + k // 2][:, (k % 2) * 512:(k % 2) * 512 + 512], f"B{2 + k // 2}{'ab'[k % 2]}"

    PB_O = [(psTb[0], "psO0"), (psTb[1], "psO1")]
    psPT = [psP[:, 0:512].bitcast(BF16).rearrange("p (k r) -> p k r", k=8), psP[:, 512:1024].bitcast(BF16).rearrange("p (k r) -> p k r", k=8)]
    PB_P = [(psPT[0], "psP0"), (psPT[1], "psP1")]

    def phase1(seq, l):
        nb = seq["S"] // 128
        chunks = [[(t, 128), (t + 1, 128)] for t in range(0, nb, 2)] + [[(nb, NMETA)]]

        def pre_a(ci):
            fo = 2 * (ci % 2)
            h_loads(seq, l, chunks[ci], fo)
            h_compute(chunks[ci], fo, on_act=True)

        def pre_b(ci):
            fo = 2 * (ci % 2)
            h_transposes(chunks[ci], PB_O, fo, HT_MAIN_P1 if ci % 2 == 0 else HT_ALT, on_act=True)

        HT_MAIN_P1 = (hT, ["hT0", "hT1"])
        load_tabs(seq, chunks[0][0][0] * 128, sum(r for _, r in chunks[0]))
        pre_a(0)
        pre_b(0)
        for ci, tiles in enumerate(chunks):
            hbuf = HT_MAIN_P1 if ci % 2 == 0 else HT_ALT
            hb, hres = hbuf
            col0 = tiles[0][0] * 128
            NQ = sum(r for _, r in tiles)
            nt = len(tiles)
            cid = tiles[0][0]
            if ci + 1 < len(chunks):
                pre_a(ci + 1)
            for h in range(4):
                psrc, pres = proj_fm(Wkv, "Wkv", h * 128, NQ, nt, hbuf)
                rope_to(psrc, pres, NQ, [(0, 128, KTB[:, h, col0:col0 + NQ])], f"KTB{h}_{cid}", tmp=p1_tmp())
            psrc, pres = proj_fm(Wkv, "Wkv", 512, NQ, nt, hbuf)
            rope_to(psrc, pres, NQ, [(0, 128, KTA[:, col0:col0 + NQ])], f"KTA_{cid}", tmp=p1_tmp())
            if ci + 1 < len(chunks):
                pre_b(ci + 1)
            for ti, (tile_idx, Rr) in enumerate(tiles):
                half = pp[0]; pp[0] ^= 1
                out = psP[0:Rr, half * 512:(half + 1) * 512]
                for kc in range(8):
                    MM(out, hb[:, kc, ti * 128:ti * 128 + Rr], Wkv[:, kc, 640:1152], kc == 0, kc == 7, hres + ["Wkv"], [f"psP{half}"])
                ACOPY(VB[0:Rr, tile_idx, :], out, [f"psP{half}"], [f"VB_{tile_idx}"])
                half = pp[0]; pp[0] ^= 1
                out2 = psP[0:Rr, half * 512:half * 512 + 128]
                for kc in range(8):
                    MM(out2, hb[:, kc, ti * 128:ti * 128 + Rr], Wkv[:, kc, 1152:1280], kc == 0, kc == 7, hres + ["Wkv"], [f"psP{half}"])
                ACOPY(VA[0:Rr, tile_idx, :], out2, [f"psP{half}"], [f"VA_{tile_idx}"])
            if ci + 1 < len(chunks):
                nxt = chunks[ci + 1]
                load_tabs(seq, nxt[0][0] * 128, sum(r for _, r in nxt))

    def front_units(seq, l, tiles, slot, in_diff):
        col0 = tiles[0][0] * 128
        NQ = sum(r for _, r in tiles)
        nt = len(tiles)
        units = []
        units.append(lambda: (load_tabs(seq, col0, NQ), h_loads(seq, l, tiles)))
        if in_diff:
            units.append(lambda: h_compute_staged(tiles))
            units.append(lambda: None)
        else:
            units.append(lambda: h_compute(tiles, 0, on_act=True, scale_eng="dve"))
        units.append(lambda: h_transposes(tiles, PB_P if in_diff else PB_O))

        def qa(j):
            psrc, pres = proj_fm(Wq, "Wq", j * 128, NQ, nt)
            return lambda: rope_to(psrc, pres, NQ, [(0, 64, QTA[0:64, 0, j, 0:NQ]), (64, 128, QTA[64:128, 1, j, 0:NQ])],
                                   f"QTA{j}", in_diff)

        def qb(h):
            psrc, pres = proj_fm(Wq, "Wq", 512 + h * 128, NQ, nt)
            return lambda: rope_to(psrc, pres, NQ, [(0, 64, QTB[0:64, slot, h, 0, 0:NQ]), (64, 128, QTB[64:128, slot, h, 1, 0:NQ])],
                                   f"QTB{slot}_{h}", in_diff)

        def gt(col0, dest, dres):
            psrc, pres = proj_fm(Wq, "Wq", col0, NQ, nt)
            return lambda: gate_tail(psrc, pres, NQ, dest, dres, in_diff)
        for j in range(4):
            units.append(lambda j=j: qa(j))
        for j in range(4):
            units.append(lambda j=j: gt(1024 + j * 128, GA[:, j, 0:NQ], f"GA{j}"))
        for h in range(4):
            units.append(lambda h=h: qb(h))
        for h in range(4):
            units.append(lambda h=h: gt(1536 + h * 128, GB[:, slot, h, 0:NQ], f"GB{slot}_{h}"))
        return units

    def window_block(nb, S, qc0, QB, kl):
        W4 = 4 * QB
        items = [(g, ki) + tuple(kl[ki]) for g in range(2) for ki in range(len(kl))]
        st = {}

        def issue_S(i):
            g, ki, kt, M, kc0, mk = items[i]
            half = wS[0]; wS[0] ^= 1
            sout = psS[0:M, half * 512:half * 512 + W4]
            kres = f"KTA_{(kt // 2) * 2 if kt < nb else nb}"
            MM(sout, KTA[:, kc0:kc0 + M], QTA[:, g, :, qc0:qc0 + QB], True, mk is None, [kres] + [f"QTA{j}" for j in range(4)], [f"psS{half}"])
            if mk is not None:
                MM(sout, ident[0:M, 0:M], masks[0:M, mk:mk + QB].unsqueeze(1).broadcast_to([M, 4, QB]), False, True,
                   ["ident", "masks"], [f"psS{half}"])
            E, eres = next_E()
            ACTV(E[0:M, 0:W4], sout, AF.Exp, [f"psS{half}"], [eres], scale=SCALE)
            st[i] = (E, eres)

        def issue_AV(i):
            g, ki, kt, M, kc0, mk = items[i]
            E, eres = st.pop(i)
            first, last = ki == 0, ki == len(kl) - 1
            MM(psOb[g][:, 0:W4], VA[0:M, kt, :], E[0:M, 0:W4], first, last, [eres, f"VA_{kt}"], [f"psO{g}"])
            MM(psSb[0][:, 0:W4], onesg[0:M, g, :], E[0:M, 0:W4], first and g == 0, last and g == 1, [eres, "onesg"], ["psSum0"])

        n = len(items)
        issue_S(0)
        if n > 1:
            issue_S(1)
        for i in range(n):
            if i + 2 < n:
                issue_S(i + 2)
            issue_AV(i)
        def finalize():
            F3 = Fp[3]
            den = F3[:, 0:W4].rearrange("p (a b) -> p a b", a=4)
            TT("dve", den, psSb[0][:, 0:W4].rearrange("p (a b) -> p a b", a=4), sinkbc[:, :].unsqueeze(2).broadcast_to([128, 4, QB]),
               ALU.add, ["psSum0", "sinkbc"], RF3A)
            RECIP(F3[:, 0:W4], F3[:, 0:W4], RF3A, RF3A)
            for g in range(2):
                TT("dve", F3[g * 64:(g + 1) * 64, 512:512 + W4], psOb[g][g * 64:(g + 1) * 64, 0:W4], F3[g * 64:(g + 1) * 64, 0:W4],
                   ALU.mult, [f"psO{g}"] + RF3A, RF3B)
            TT("pool", oT[:, 0:4, qc0:qc0 + QB], F3[:, 512:512 + W4].rearrange("p (a b) -> p a b", a=4), GA[:, :, qc0:qc0 + QB], ALU.mult,
               RF3B + [f"GA{j}" for j in range(4)], ["oTA"])
        return finalize

    def window_blocks(seq, tiles):
        S = seq["S"]; nb = S // 128
        if tiles[0][0] == nb:
            return [lambda: window_block(nb, S, 0, NMETA, [(nb, NMETA, S, None), (0, 128, 0, 256)])]
        out = []
        for bi, (t, _) in enumerate(tiles):
            kl = []
            if t >= 1:
                kl.append((t - 1, 128, (t - 1) * 128, 0))
            kl.append((t, 128, t * 128, None))
            if t + 1 < nb:
                kl.append((t + 1, 128, (t + 1) * 128, 128))
            kl.append((nb, NMETA, S, None))
            out.append(lambda bi=bi, kl=kl: window_block(nb, S, bi * 128, 128, kl))
        return out

    def window_all(seq, tiles):
        for th in window_blocks(seq, tiles):
            th()()

    post2_done = [False]
    last_post2 = [None]

    def diff_all(nb, NQ, ktiles, slot, units, reload_x):
        W2 = 2 * NQ
        n = len(ktiles)
        pending = [None]
        tails = []
        reloaded = [False]
        next_unit = [1]
        units = list(units)
        total_steps = 4 * n
        stride = max(1, (total_steps - 2) // (len(units) + 1)) if units else 0
        step = [0]

        def post1(h):
            p = h % 2
            psO, psSum = psOb[p], psSb[p]
            F2, F3 = Fp[2], Fp[3]
            RECIP(F2[:, 0:W2], psSum[:, 0:W2], [f"psSum{p}"], RF2A)
            TT("dve", F2[:, 512:512 + W2], psO[:, 0:W2], F2[:, 0:W2], ALU.mult, [f"psO{p}"] + RF2A, RF2B)
            STT("dve", F3[:, 0:NQ], F2[:, 512 + NQ:512 + W2], small[:, C_NLAM:C_NLAM + 1], F2[:, 512:512 + NQ], ALU.mult, ALU.add,
                RF2B + ["nlam"], RF3A)
            TT("pool", F3[:, 256:256 + NQ], F3[:, 0:NQ], F3[:, 0:NQ], ALU.mult, RF3A, RF3A)

        def post2(h):
            p = h % 2
            psSum = psSb[p]
            F3 = Fp[3]
            post2_done[0] = True
            MM(psSum[:, 0:NQ], onesf[:, :], F3[:, 256:256 + NQ], True, True, RF3A + ["onesf"], [f"psSum{p}"])
            ACTV(F3[:, 512:512 + NQ], psSum[:, 0:NQ], AF.Ln, [f"psSum{p}", "small_init"], RF3B, scale=1.0 / 128,
                 bias=small[:, C_EPS:C_EPS + 1])
            ACTV(F3[:, 512:512 + NQ], F3[:, 512:512 + NQ], AF.Exp, RF3B, RF3B, scale=-0.5)
            STT("dve", F3[:, 768:768 + NQ], F3[:, 0:NQ], small[:, C_SGR:C_SGR + 1], F3[:, 512:512 + NQ], ALU.mult, ALU.mult,
                RF3A + RF3B + ["sg"], RF3B)
            TT("pool", oT[:, 4 + h, 0:NQ], F3[:, 768:768 + NQ], GB[:, slot, h, 0:NQ], ALU.mult, RF3B + [f"GB{slot}_{h}"], [f"oTB{h}"])

        K0 = min(max(5, n // 2), n - 1)
        steps = [(h, ki) for h in range(4) for ki in range(n)]
        st = {}

        def issue_S(i):
            h, ki = steps[i]
            kt, M, kc0 = ktiles[ki]
            half = sS[0]; sS[0] ^= 1
            sres = f"psS{half}"
            kres = f"KTB{h}_{(kt // 2) * 2 if kt < nb else nb}"
            MM(psS[0:M, half * 512:half * 512 + W2], KTB[:, h, kc0:kc0 + M], QTB[:, slot, h, :, 0:NQ], True, True,
               [kres, f"QTB{slot}_{h}"], [sres])
            E, eres = next_E()
            ACTV(E[0:M, 0:W2], psS[0:M, half * 512:half * 512 + W2], AF.Exp, [sres], [eres], scale=SCALE)
            st[i] = (E, eres)

        def issue_AV(i):
            h, ki = steps[i]
            p = h % 2
            kt, M, kc0 = ktiles[ki]
            E, eres = st.pop(i)
            first, last = ki == 0, ki == n - 1
            MM(psOb[p][:, 0:W2], VB[0:M, kt, h * 128:(h + 1) * 128], E[0:M, 0:W2], first, last, [eres, f"VB_{kt}"], [f"psO{p}"])
            MM(psSb[p][:, 0:W2], onesb[0:M, :], E[0:M, 0:W2], first, last, [eres, "onesb"], [f"psSum{p}"])

        ns = len(steps)
        issue_S(0)
        if ns > 1:
            issue_S(1)
        for i in range(ns):
            h, ki = steps[i]
            if i + 2 < ns:
                issue_S(i + 2)
            issue_AV(i)
            if ki == K0 and pending[0] is not None:
                post2(pending[0]); pending[0] = None
            step[0] += 1
            for tl in [t for t in tails if t[0] <= step[0]]:
                tails.remove(tl)
                nxt = tl[1]()
                if callable(nxt):
                    tails.append((step[0] + 2, nxt))
            if units and not tails and step[0] >= next_unit[0]:
                tail = units.pop(0)()
                next_unit[0] = step[0] + stride
                if callable(tail):
                    tails.append((step[0] + 2, tail))
            elif not units and not tails and not reloaded[0]:
                reload_x(); reloaded[0] = True
            if ki == n - 1:
                post1(h)
                pending[0] = h
        while tails:
            nxt = tails.pop(0)[1]()
            if callable(nxt):
                tails.insert(0, (0, nxt))
        while units:
            tail = units.pop(0)()
            while callable(tail):
                tail = tail()
        if not reloaded[0]:
            reload_x()
        last_post2[0] = (lambda hh=pending[0]: post2(hh))

    def outproj_mm(ti, Rr, f_list):
        c0 = ti * 128
        pt, pn = (psP, "psP") if ti == 0 else (psS, "psS")
        ores = ["oTA", "oTA", "oTA", "oTA", "oTB0", "oTB1", "oTB2", "oTB3"]
        for f in f_list:
            for hf in range(2):
                MM(pt[0:Rr, hf * 512:(hf + 1) * 512], oT[:, f, c0:c0 + Rr], Wo[:, f, hf * 512:(hf + 1) * 512], f == 0, f == 7,
                   [ores[f], "Wo"], [f"{pn}{hf}"])

    def outproj_post(seq, l, ti, tile_idx, Rr):
        pt, pn = (psP, "psP") if ti == 0 else (psS, "psS")
        pres = [f"{pn}0", f"{pn}1"]
        Fy = Fp[2 + ti]
        fy = (RF2A + RF2B) if ti == 0 else (RF3A + RF3B)
        ssc = C_SS2 + ti
        MEMSET("dve", col(ssc, Rr), 0.0, ["small_init"], [f"ssy{ti}"])
        ACTV(Fy[0:Rr, :], pt[0:Rr, :], AF.Square, pres + [f"ssy{ti}"], fy + [f"ssy{ti}"], accum_out=col(ssc, Rr))
        rstd_chain(ssc, C_LN2 + ti, C_RS2 + ti, Rr, 1.0 / D, f"y{ti}")
        STT("dve", Fy[0:Rr, :], pt[0:Rr, :], col(C_RS2 + ti, Rr), gpost[0:Rr, :], ALU.mult, ALU.mult,
            pres + [f"rsy{ti}", "gpost"], fy)
        TT("pool", Fy[0:Rr, :], Fy[0:Rr, :], Fp[ti][0:Rr, :], ALU.add, fy + FRES[ti], fy)
        dst, dres = x_dst(seq, l, tile_idx, Rr)
        DMA("pool", dst, Fy[0:Rr, :], fy, [dres] if dres else [])

    def phase2(seq, l):
        S = seq["S"]; nb = S // 128
        chunks = [[(t, 128), (t + 1, 128)] for t in range(0, nb, 2)]
        if l < NL - 1:
            chunks.append([(nb, NMETA)])
        ktiles = [(t, 128, t * 128) for t in range(nb)] + [(nb, NMETA, S)]
        for u in front_units(seq, l, chunks[0], 0, False):
            tail = u()
            while callable(tail):
                tail = tail()
        window_all(seq, chunks[0])
        for ci, tiles in enumerate(chunks):
            slot = ci % 2
            NQ = sum(r for _, r in tiles)
            units = []
            if ci + 1 < len(chunks):
                units = front_units(seq, l, chunks[ci + 1], 1 - slot, True)
            diff_all(nb, NQ, ktiles, slot, units, lambda tiles=tiles: h_loads(seq, l, tiles))
            if DBG and l == 0 and seq is seqs[0] and ci == 0:
                for nme, t, rs in (("oT", oT, ["oTA"] + [f"oTB{j}" for j in range(4)]),
                                   ("KTB", KTB, [f"KTB{hh}_{c}" for hh in range(4) for c in list(range(0, nb, 2)) + [nb]]),
                                   ("KTA", KTA, [f"KTA_{c}" for c in list(range(0, nb, 2)) + [nb]]),
                                   ("VB", VB, [f"VB_{c}" for c in range(nb + 1)]), ("VA", VA, [f"VA_{c}" for c in range(nb + 1)])):
                    DMA("pool", dbg[nme], t[:], rs, [])
            for ti, (tile_idx, Rr) in enumerate(tiles):
                outproj_mm(ti, Rr, range(7))
            last_post2[0]()
            for ti, (tile_idx, Rr) in enumerate(tiles):
                outproj_mm(ti, Rr, [7])
            for ti, (tile_idx, Rr) in enumerate(tiles):
                outproj_post(seq, l, ti, tile_idx, Rr)
            if ci + 1 < len(chunks):
                window_all(seq, chunks[ci + 1])

    for l in range(NL):
        if l == 0:
            load_wkv(0)
        load_wq_wo(l)
        load_params(l)
        for si, seq in enumerate(seqs):
            phase1(seq, l)
            if si == len(seqs) - 1 and l + 1 < NL:
                load_wkv(l + 1)
            phase2(seq, l)

    R.finalize()
    with ExitStack() as ss_:
        esems = {e: ss_.enter_context(nc.semaphore(f"sem_{e}")) for e in Rec.ENGS}
        dsems = {}
        for q, n in R.n_dma.items():
            for i in range(n):
                dsems[(q, i)] = ss_.enter_context(nc.semaphore(f"dsem_{q}{i}"))
        block = ss_.enter_context(nc.Block())

        @block.tensor
        def _(t):
            R.play("pe", t, esems, dsems)

        @block.scalar
        def _(s):
            R.play("act", s, esems, dsems)

        @block.vector
        def _(v):
            R.play("dve", v, esems, dsems)

        @block.gpsimd
        def _(g):
            R.play("pool", g, esems, dsems)

        @block.sync
        def _(sy):
            R.play("sp", sy, esems, dsems)
    es.close()
    return nc


def rope_table(S):
    LP = S + NMETA
    pos = np.concatenate([np.arange(S, dtype=np.float32) + NMETA, np.arange(NMETA, dtype=np.float32)])
    inv_freq = (1.0 / (ROPE_THETA ** (np.arange(0, 64, 2, dtype=np.float32) / 64.0))).astype(np.float32)
    ang = (pos[None, :] * inv_freq[:, None]).astype(np.float32)
    cos = np.cos(ang).astype(np.float32)
    sin = np.sin(ang).astype(np.float32)
    tab = np.zeros((128, 2, LP), np.float32)
    for p in range(128):
        f = p % 32
        tab[p, 0] = cos[f]
        tab[p, 1] = -sin[f] if (p % 64) >= 32 else sin[f]
    return tab


def const_tables():
    ident = np.eye(128, dtype=np.float32)
    k = np.arange(128)[:, None]
    q = np.arange(128)[None, :]
    NEG = -30000.0
    masks = np.zeros((128, 272), np.float32)
    masks[:, 0:128] = np.where(k >= q, 0.0, NEG)
    masks[:, 128:256] = np.where(k <= q, 0.0, NEG)
    u = np.arange(128)[:, None]
    m = np.arange(NMETA)[None, :]
    masks[:, 256:272] = np.where(u <= 112 + m, 0.0, NEG)
    return ident, masks


_CACHE = {}


def kernel(x_prompt, x_sample, meta_tokens, w_in, w_out, pre_norm_g, post_norm_g, sink_logits,
           lambda_q1, lambda_k1, lambda_q2, lambda_k2, subln_g):
    f = lambda a: np.ascontiguousarray(np.asarray(a, dtype=np.float32))
    x_prompt, x_sample = f(x_prompt), f(x_sample)
    NL = w_in.shape[0]
    BP, SPn = x_prompt.shape[0], x_prompt.shape[1]
    BS, SSn = x_sample.shape[0], x_sample.shape[1]
    nP, nS = BP // N_CORES, BS // N_CORES
    cfg = Cfg(n_layers=NL, n_prompt=nP, s_prompt=SPn, n_sample=nS, s_sample=SSn)
    nc = build_program(cfg)
    ident, masks = const_tables()
    common = dict(meta=f(meta_tokens), w_in=f(w_in), w_out=f(w_out), pre_g=f(pre_norm_g), post_g=f(post_norm_g),
                  sink=f(sink_logits), lq1=f(lambda_q1), lk1=f(lambda_k1), lq2=f(lambda_q2), lk2=f(lambda_k2),
                  subln=f(subln_g), ropeP=rope_table(SPn), ropeS=rope_table(SSn), cident=ident, cmasks=masks)
    in_maps = []
    for c in range(N_CORES):
        m = dict(common)
        m["xp"] = x_prompt[c * nP:(c + 1) * nP]
        m["xs"] = x_sample[c * nS:(c + 1) * nS]
        in_maps.append(m)
    res = run_bass_kernel_spmd(nc, in_maps, core_ids=list(range(N_CORES)))
    ypo = np.concatenate([np.asarray(r["yp"]) for r in res.results], axis=0).astype(np.float32)
    yso = np.concatenate([np.asarray(r["ys"]) for r in res.results], axis=0).astype(np.float32)
    return (ypo, yso)
```

```python
import math
from contextlib import ExitStack

import numpy as np
import concourse.bass as bass
import concourse.mybir as mybir
from concourse.bass_utils import run_bass_kernel_spmd

F32 = mybir.dt.float32
BF16 = mybir.dt.bfloat16
AF = mybir.ActivationFunctionType
ALU = mybir.AluOpType

D = 1024
NMETA = 16
EPS = 1e-6
SCALE = 0.125
ROPE_THETA = 10000.0
N_CORES = 8


class Op:
    __slots__ = ("eng", "fn", "deps", "sig", "cnt", "dma", "vsem", "vval")

    def __init__(self, eng, fn, dma):
        self.eng = eng
        self.fn = fn
        self.deps = ()
        self.sig = False
        self.cnt = 0
        self.dma = dma
        self.vsem = None
        self.vval = 0


class Rec:
    ENGS = ("pe", "act", "dve", "pool", "sp")

    def __init__(self, n_dma_sems):
        self.ops = {e: [] for e in self.ENGS}
        self.res_w = {}
        self.res_r = {}
        self.n_dma = n_dma_sems
        self.dma_rr = {e: 0 for e in n_dma_sems}
        self.dma_last = {}
        self.dma_val = {}

    def emit(self, eng, fn, reads=(), writes=(), dma=False):
        op = Op(eng, fn, dma)
        deps = []
        seen = set()
        res_w, res_r = self.res_w, self.res_r
        for r in reads:
            w = res_w.get(r)
            if w is not None and id(w) not in seen:
                seen.add(id(w)); deps.append(w)
        for r in writes:
            w = res_w.get(r)
            if w is not None and id(w) not in seen:
                seen.add(id(w)); deps.append(w)
            for rd in res_r.get(r, ()):
                if id(rd) not in seen:
                    seen.add(id(rd)); deps.append(rd)
        if dma:
            k = (eng, self.dma_rr[eng])
            self.dma_rr[eng] = (self.dma_rr[eng] + 1) % self.n_dma[eng]
            prev = self.dma_last.get(k)
            if prev is not None and id(prev) not in seen:
                seen.add(id(prev)); deps.append(prev)
            self.dma_last[k] = op
            self.dma_val[k] = self.dma_val.get(k, 0) + 16
            op.vsem = k
            op.vval = self.dma_val[k]
        keep = []
        for d in deps:
            if d.dma or dma:
                keep.append(d)
            elif d.eng == eng and eng == "pe":
                continue
            else:
                keep.append(d)
        op.deps = keep
        for r in reads:
            res_r.setdefault(r, []).append(op)
        for r in writes:
            res_w[r] = op
            res_r[r] = []
        self.ops[eng].append(op)
        return op

    def finalize(self):
        for e in self.ENGS:
            for op in self.ops[e]:
                for d in op.deps:
                    if not d.dma:
                        d.sig = True
        for e in self.ENGS:
            c = 0
            for op in self.ops[e]:
                if op.sig and not op.dma:
                    c += 1
                    op.cnt = c

    def play(self, eng, handle, esems, dsems):
        waited = {}
        for op in self.ops[eng]:
            for d in op.deps:
                if d.dma:
                    key = ("d",) + d.vsem
                    sem = dsems[d.vsem]
                    val = d.vval
                else:
                    key = ("e", d.eng)
                    sem = esems[d.eng]
                    val = d.cnt
                if waited.get(key, 0) < val:
                    handle.wait_ge(sem, val)
                    waited[key] = val
            ins = op.fn(handle)
            if op.dma:
                ins.then_inc(dsems[op.vsem], 16)
            elif op.sig:
                ins.then_inc(esems[eng], 1)
        if eng == "sp":
            for k, v in self.dma_val.items():
                handle.wait_ge(dsems[k], v)


class Cfg:
    def __init__(self, n_layers=4, n_prompt=2, s_prompt=4096, n_sample=2, s_sample=2048, lambda_layers=None):
        self.n_layers = n_layers
        self.n_prompt = n_prompt
        self.s_prompt = s_prompt
        self.n_sample = n_sample
        self.s_sample = s_sample


def build_program(cfg):
    nc = bass.Bass("TRN2", target_bir_lowering=False)
    NL = cfg.n_layers
    SP_, SS_ = cfg.s_prompt, cfg.s_sample
    nP, nS = cfg.n_prompt, cfg.n_sample
    LPP, LPS = SP_ + NMETA, SS_ + NMETA
    LPM = max(LPP, LPS)
    NTM = max(SP_, SS_) // 128 + 1

    def din(name, shape, dt=F32):
        return nc.dram_tensor(name, list(shape), dt, kind="ExternalInput").ap()

    xp = din("xp", [nP, SP_, D])
    xs = din("xs", [nS, SS_, D])
    meta = din("meta", [NMETA, D])
    w_in = din("w_in", [NL, D, 3328])
    w_out = din("w_out", [NL, D, D])
    pre_g = din("pre_g", [NL, D])
    post_g = din("post_g", [NL, D])
    sink = din("sink", [NL, 8])
    lq1 = din("lq1", [NL, 64]); lk1 = din("lk1", [NL, 64])
    lq2 = din("lq2", [NL, 64]); lk2 = din("lk2", [NL, 64])
    subln = din("subln", [NL, 128])
    ropeP = din("ropeP", [128, 2, LPP])
    ropeS = din("ropeS", [128, 2, LPS])
    cident = din("cident", [128, 128])
    cmasks = din("cmasks", [128, 272])
    DBG = getattr(cfg, "debug", False)
    if DBG:
        dbg = {n: nc.dram_tensor("dbg_" + n, sh, dt, kind="ExternalOutput").ap() for n, sh, dt in (
            ("KTB", [128, 4, LPM], BF16), ("KTA", [128, LPM], BF16), ("VB", [128, NTM, 512], BF16), ("VA", [128, NTM, 128], BF16),
            ("oT", [128, 8, 256], BF16))}
    yp = nc.dram_tensor("yp", [nP, SP_, D], F32, kind="ExternalOutput").ap()
    ys = nc.dram_tensor("ys", [nS, SS_, D], F32, kind="ExternalOutput").ap()

    seqs = []
    row = 0
    for i in range(max(nP, nS)):
        if i < nP:
            seqs.append(dict(kind="p", idx=i, S=SP_, base=row)); row += LPP
        if i < nS:
            seqs.append(dict(kind="s", idx=i, S=SS_, base=row)); row += LPS
    tot_rows = row
    xscr = [nc.dram_tensor("xscrA", [tot_rows, D], F32).ap(), nc.dram_tensor("xscrB", [tot_rows, D], F32).ap()]

    R = Rec({"sp": 12, "pool": 6})
    es = ExitStack()

    def sb(name, shape, dt):
        return es.enter_context(nc.sbuf_tensor(name, list(shape), dt))

    def ps(name, shape, dt):
        return es.enter_context(nc.psum_tensor(name, list(shape), dt))

    KTB = sb("KTB", [128, 4, LPM], BF16)
    VB = sb("VB", [128, NTM, 512], BF16)
    KTA = sb("KTA", [128, LPM], BF16)
    VA = sb("VA", [128, NTM, 128], BF16)
    Wkv = sb("Wkv", [128, 8, 1280], BF16)
    Wq = sb("Wq", [128, 8, 2048], BF16)
    Wo = sb("Wo", [128, 8, 1024], BF16)
    gcol = sb("gcol", [128, 8], F32)
    gpost = sb("gpost", [128, D], F32)
    ident = sb("ident", [128, 128], BF16)
    masks = sb("masks", [128, 272], BF16)
    onesg = sb("onesg", [128, 2, 128], BF16)
    onesb = sb("onesb", [128, 128], BF16)
    onesf = sb("onesf", [128, 128], F32)
    Fp = [sb(f"F{i}", [128, D], F32) for i in range(4)]
    Bp = [sb(f"B{i}", [128, D], BF16) for i in range(4)]
    hT = sb("hT", [128, 8, 256], BF16)
    tabs = sb("tabs", [128, 2, 256], F32)
    QTA = sb("QTA", [128, 2, 4, 256], BF16)
    QTB = sb("QTB", [128, 2, 4, 2, 256], BF16)
    GA = sb("GA", [128, 4, 256], BF16)
    GB = sb("GB", [128, 2, 4, 256], BF16)
    oT = sb("oT", [128, 8, 256], BF16)
    small = sb("small", [128, 48], F32)
    sinkbc = sb("sinkbc", [128, 4], F32)

    C_EPS, C_SS0, C_SS1, C_LN0, C_LN1, C_RS0, C_RS1 = 0, 1, 2, 3, 4, 5, 6
    C_SS2, C_LN2, C_RS2 = 8, 10, 12
    C_D1, C_D2, C_E1, C_E2, C_T, C_NLAM, C_SG, C_SGR, C_NEG1 = 20, 21, 22, 23, 24, 25, 26, 27, 28
    gcolkv = small[:, 32:40]

    psS = ps("psS", [128, 1024], F32)
    psOO = ps("psOO", [128, 1024], F32)
    psSS = ps("psSS", [128, 1024], F32)
    psP = ps("psP", [128, 1024], F32)
    psOb = [psOO[:, 0:512], psOO[:, 512:1024]]
    psSb = [psSS[:, 0:512], psSS[:, 512:1024]]
    psTb = [psOO[:, 0:512].bitcast(BF16).rearrange("p (k r) -> p k r", k=8),
            psOO[:, 512:1024].bitcast(BF16).rearrange("p (k r) -> p k r", k=8)]

    emit = R.emit

    def MM(out, lhsT, rhs, start, stop, reads, writes):
        emit("pe", lambda e: e.matmul(out, lhsT=lhsT, rhs=rhs, start=start, stop=stop), reads, writes)

    def TR(out, in_, idn, reads, writes):
        emit("pe", lambda e: e.transpose(out=out, in_=in_, identity=idn), reads, writes)

    def ACTV(out, in_, func, reads, writes, scale=1.0, bias=None, accum_out=None):
        kw = {}
        if bias is not None:
            kw["bias"] = bias
        if accum_out is not None:
            kw["accum_out"] = accum_out
        emit("act", lambda e: e.activation(out=out, in_=in_, func=func, scale=scale, **kw), reads, writes)

    def ACOPY(out, in_, reads, writes):
        emit("act", lambda e: e.copy(out=out, in_=in_), reads, writes)

    def TT(eng, out, in0, in1, op, reads, writes):
        emit(eng, lambda e: e.tensor_tensor(out=out, in0=in0, in1=in1, op=op), reads, writes)

    def STT(eng, out, in0, scalar, in1, op0, op1, reads, writes, accum_out=None):
        if accum_out is None:
            emit(eng, lambda e: e.scalar_tensor_tensor(out=out, in0=in0, scalar=scalar, in1=in1, op0=op0, op1=op1), reads, writes)
        else:
            emit(eng, lambda e: e.scalar_tensor_tensor(out=out, in0=in0, scalar=scalar, in1=in1, op0=op0, op1=op1,
                                                      accum_out=accum_out), reads, writes)

    def TS(eng, out, in0, s1, op0, reads, writes):
        emit(eng, lambda e: e.tensor_scalar(out=out, in0=in0, scalar1=s1, scalar2=None, op0=op0), reads, writes)

    def RECIP(out, in_, reads, writes):
        emit("dve", lambda e: e.reciprocal(out=out, in_=in_), reads, writes)

    def VCOPY(out, in_, reads, writes):
        emit("dve", lambda e: e.tensor_copy(out=out, in_=in_), reads, writes)

    def MEMSET(eng, ap, val, reads, writes):
        emit(eng, lambda e: e.memset(ap, val), reads, writes)

    def DMA(eng, out, in_, reads, writes):
        emit(eng, lambda e: e.dma_start(out=out, in_=in_), reads, writes, dma=True)

    def col(c, Rr=128):
        return small[0:Rr, c:c + 1]

    DMA("pool", ident[:], cident, [], ["ident"])
    DMA("pool", masks[:], cmasks, [], ["masks"])
    MEMSET("pool", onesb[:], 1.0, [], ["onesb"])
    MEMSET("pool", onesf[:], 1.0, [], ["onesf"])
    MEMSET("pool", small[:], 0.0, [], ["small"])
    MEMSET("pool", col(C_EPS), EPS, [], ["small"])
    MEMSET("pool", col(C_NEG1), -1.0, [], ["small", "small_init"])
    MEMSET("pool", QTA[:], 0.0, [], [f"QTA{j}" for j in range(4)])
    MEMSET("pool", QTB[:], 0.0, [], [f"QTB{sl}_{h}" for sl in range(2) for h in range(4)])
    MEMSET("pool", onesg[:], 0.0, [], ["onesg"])
    MEMSET("pool", onesg[:, 0, 0:64], 1.0, [], ["onesg"])
    MEMSET("pool", onesg[:, 1, 64:128], 1.0, [], ["onesg"])

    def load_wkv(l):
        src = w_in[l].rearrange("(c p) n -> p c n", p=128)
        for (d0, s0, n) in ((0, 1792, 512), (512, 512, 128), (640, 2304, 512), (1152, 640, 128)):
            DMA("pool", Wkv[:, :, d0:d0 + n], src[:, :, s0:s0 + n], [], ["Wkv"])
        for kc in range(8):
            DMA("pool", gcolkv[:, kc:kc + 1], pre_g[l:l + 1, kc * 128:(kc + 1) * 128].rearrange("o p -> p o"), ["small_init"], ["gcolkv"])
        for kc in range(8):
            TS("dve", Wkv[:, kc, :], Wkv[:, kc, :], gcolkv[:, kc:kc + 1], ALU.mult, ["Wkv", "gcolkv"], ["Wkv"])

    def load_wq_wo(l):
        src = w_in[l].rearrange("(c p) n -> p c n", p=128)
        for (dbase, sbase) in ((0, 0), (1024, 768)):
            for j in range(4):
                for g in range(2):
                    d0 = dbase + j * 128 + g * 64
                    s0 = sbase + (g * 4 + j) * 64
                    DMA("pool", Wq[:, :, d0:d0 + 64], src[:, :, s0:s0 + 64], [], ["Wq"])
        DMA("pool", Wq[:, :, 512:1024], src[:, :, 1280:1792], [], ["Wq"])
        DMA("pool", Wq[:, :, 1536:2048], src[:, :, 2816:3328], [], ["Wq"])
        for kc in range(8):
            DMA("pool", gcol[:, kc:kc + 1], pre_g[l:l + 1, kc * 128:(kc + 1) * 128].rearrange("o p -> p o"), [], ["gcol"])
        for kc in range(8):
            TS("dve", Wq[:, kc, :], Wq[:, kc, :], gcol[:, kc:kc + 1], ALU.mult, ["Wq", "gcol"], ["Wq"])
        wo = w_out[l]
        for j in range(4):
            for g in range(2):
                r0 = (g * 4 + j) * 64
                DMA("pool", Wo[g * 64:(g + 1) * 64, j, :], wo[r0:r0 + 64, :], [], ["Wo"])
        DMA("pool", Wo[:, 4:8, :], wo[512:1024, :].rearrange("(c p) n -> p c n", p=128), [], ["Wo"])

    def load_params(l):
        lambda_init = 0.8 - 0.6 * math.exp(-0.3 * l)
        DMA("sp", gpost[:], post_g[l:l + 1, :].broadcast_to([128, D]), [], ["gpost"])
        lamt = Fp[3][:, 512:768].rearrange("p (a b) -> p a b", a=4)
        for i, t in enumerate((lq1, lk1, lq2, lk2)):
            DMA("sp", lamt[:, i, :], t[l:l + 1, :].broadcast_to([128, 64]), [], ["F3b"])
        for g in range(2):
            DMA("sp", sinkbc[g * 64:(g + 1) * 64, :], sink[l:l + 1, g * 4:(g + 1) * 4].broadcast_to([64, 4]), [], ["sinkbc"])
        DMA("sp", col(C_SG), subln[l:l + 1, :].rearrange("o p -> p o"), ["small_init"], ["sg_raw"])
        junk = Fp[3]
        MEMSET("dve", small[:, C_D1:C_D1 + 2], 0.0, ["small_init"], ["d1", "d2"])
        STT("dve", junk[:, 0:64], lamt[:, 0, :], 1.0, lamt[:, 1, :], ALU.mult, ALU.mult, ["F3b", "d1"], ["F3a", "d1"],
            accum_out=col(C_D1))
        STT("dve", junk[:, 64:128], lamt[:, 2, :], 1.0, lamt[:, 3, :], ALU.mult, ALU.mult, ["F3b", "d2"], ["F3a", "d2"],
            accum_out=col(C_D2))
        ACTV(small[:, C_E1:C_E1 + 2], small[:, C_D1:C_D1 + 2], AF.Exp, ["d1", "d2"], ["e12"])
        TT("dve", col(C_T), col(C_E2), col(C_E1), ALU.subtract, ["e12"], ["lt"])
        TS("dve", col(C_NLAM), col(C_T), -lambda_init, ALU.add, ["lt"], ["nlam"])
        TS("dve", col(C_SGR), col(C_SG), 1.0 - lambda_init, ALU.mult, ["sg_raw"], ["sg"])
        ACTV(sinkbc[:], sinkbc[:], AF.Exp, ["sinkbc"], ["sinkbc"])

    def x_src(seq, l, tile_idx, Rr):
        S = seq["S"]
        nb = S // 128
        if l == 0:
            if tile_idx == nb:
                return meta[0:Rr, :], None
            src = xp if seq["kind"] == "p" else xs
            return src[seq["idx"], tile_idx * 128:tile_idx * 128 + Rr, :], None
        buf = (l - 1) % 2
        r0 = seq["base"] + tile_idx * 128
        return xscr[buf][r0:r0 + Rr, :], f"xs{buf}_{seq['base']}_{tile_idx}"

    def x_dst(seq, l, tile_idx, Rr):
        if l == NL - 1:
            dst = yp if seq["kind"] == "p" else ys
            return dst[seq["idx"], tile_idx * 128:tile_idx * 128 + Rr, :], None
        buf = l % 2
        r0 = seq["base"] + tile_idx * 128
        return xscr[buf][r0:r0 + Rr, :], f"xs{buf}_{seq['base']}_{tile_idx}"

    def rstd_chain(ss_col, ln_col, rs_col, Rr, scale, tag):
        ACTV(col(ln_col, Rr), col(ss_col, Rr), AF.Ln, [f"ss{tag}", "small_init"], [f"ln{tag}"], scale=scale, bias=col(C_EPS, Rr))
        ACTV(col(rs_col, Rr), col(ln_col, Rr), AF.Exp, [f"ln{tag}"], [f"rs{tag}"], scale=-0.5)

    RF2A, RF2B = ["F2a0", "F2a1"], ["F2b0", "F2b1"]
    RF3A, RF3B = ["F3a"], ["F3b"]

    def tmp_set(i):
        if i < 2:
            return Fp[2][:, i * 256:(i + 1) * 256], Fp[2][:, 512 + i * 256:512 + (i + 1) * 256], [f"F2a{i}"], [f"F2b{i}"]
        j = i - 2
        return Fp[3][:, j * 256:(j + 1) * 256], Fp[3][:, 512 + j * 256:512 + (j + 1) * 256], RF3A, RF3B

    FRES = [["F0s0", "F0s1"], ["F1s0", "F1s1"], ["F2a0", "F2a1", "F2b0", "F2b1"], ["F3a", "F3b"]]
    BRES = [["B0"], ["B1"], ["B2a", "B2b"], ["B3a", "B3b"]]
    HT_MAIN = (hT, ["hT0", "hT1"])
    HT_ALT = (oT, ["oTA", "oTB0", "oTB1", "oTB2", "oTB3"])

    def h_loads(seq, l, tiles, fo=0):
        for ti, (tile_idx, Rr) in enumerate(tiles):
            src, sres = x_src(seq, l, tile_idx, Rr)
            DMA("sp", Fp[fo + ti][0:Rr, :], src, [sres] if sres else [], FRES[fo + ti])

    def h_compute(tiles, fo=0, on_act=False, scale_eng=None):
        for ti, (tile_idx, Rr) in enumerate(tiles):
            xin, xb = Fp[fo + ti], Bp[fo + ti]
            ssc = C_SS0 + ti
            MEMSET("pool" if on_act else "dve", col(ssc, Rr), 0.0, ["small_init"], [f"ssh{ti}"])
            if on_act:
                ACTV(xb[0:Rr, :], xin[0:Rr, :], AF.Square, FRES[fo + ti] + [f"ssh{ti}"], BRES[fo + ti] + [f"ssh{ti}"], accum_out=col(ssc, Rr))
            else:
                STT("dve", xb[0:Rr, :], xin[0:Rr, :], 1.0, xin[0:Rr, :], ALU.mult, ALU.mult, FRES[fo + ti] + [f"ssh{ti}"],
                    BRES[fo + ti] + [f"ssh{ti}"], accum_out=col(ssc, Rr))
        for ti, (tile_idx, Rr) in enumerate(tiles):
            rstd_chain(C_SS0 + ti, C_LN0 + ti, C_RS0 + ti, Rr, 1.0 / D, f"h{ti}")
        for ti, (tile_idx, Rr) in enumerate(tiles):
            if on_act and scale_eng is None:
                ACTV(Bp[fo + ti][0:Rr, :], Fp[fo + ti][0:Rr, :], AF.Identity, FRES[fo + ti] + [f"rsh{ti}"], BRES[fo + ti],
                     scale=col(C_RS0 + ti, Rr))
            else:
                TS(scale_eng or "dve", Bp[fo + ti][0:Rr, :], Fp[fo + ti][0:Rr, :], col(C_RS0 + ti, Rr), ALU.mult,
                   FRES[fo + ti] + [f"rsh{ti}"], BRES[fo + ti])

    def h_compute_staged(tiles):
        for ti, (tile_idx, Rr) in enumerate(tiles):
            xin, xb = Fp[ti], Bp[ti]
            ssc = C_SS0 + ti
            MEMSET("dve", col(ssc, Rr), 0.0, ["small_init"], [f"ssh{ti}"])
            STT("dve", xb[0:Rr, :], xin[0:Rr, :], 1.0, xin[0:Rr, :], ALU.mult, ALU.mult, FRES[ti] + [f"ssh{ti}"],
                BRES[ti] + [f"ssh{ti}"], accum_out=col(ssc, Rr))

        def stage_b():
            for ti, (tile_idx, Rr) in enumerate(tiles):
                rstd_chain(C_SS0 + ti, C_LN0 + ti, C_RS0 + ti, Rr, 1.0 / D, f"h{ti}")

            def stage_c():
                for ti, (tile_idx, Rr) in enumerate(tiles):
                    TS("dve", Bp[ti][0:Rr, :], Fp[ti][0:Rr, :], col(C_RS0 + ti, Rr), ALU.mult, FRES[ti] + [f"rsh{ti}"], BRES[ti])
            return stage_c
        return stage_b

    def h_transposes(tiles, pbank, fo=0, hbuf=None, on_act=False):
        hb, hres = hbuf if hbuf is not None else HT_MAIN
        for ti, (tile_idx, Rr) in enumerate(tiles):
            xb = Bp[fo + ti]
            pT, pres = pbank[ti]
            for kc in range(8):
                TR(pT[:, kc, 0:Rr], xb[0:Rr, kc * 128:(kc + 1) * 128], ident[0:Rr, 0:Rr], BRES[fo + ti] + ["ident"], [pres])
            (ACOPY if on_act else VCOPY)(hb[:, :, ti * 128:ti * 128 + Rr], pT[:, :, 0:Rr], [pres], [hres[ti]] if hbuf is None else hres)

    pp = [0]
    ff = [0]

    def next_tmp(in_diff):
        k = ff[0]
        ff[0] = (ff[0] + 1) % 4
        Fb = Fp[k // 2]
        c0 = (k % 2) * 512
        rn = [f"F{k // 2}s{k % 2}"]
        return Fb[:, c0:c0 + 256], Fb[:, c0 + 256:c0 + 512], rn, rn

    def proj_fm(Wt, wname, col0, NQ, ntiles, hbuf=None):
        half = pp[0]; pp[0] ^= 1
        out = psP[:, half * 512:half * 512 + NQ]
        hb, hres = hbuf if hbuf is not None else (hT, [f"hT{t}" for t in range(ntiles)])
        for kc in range(8):
            MM(out, Wt[:, kc, col0:col0 + 128], hb[:, kc, 0:NQ], kc == 0, kc == 7, hres + [wname], [f"psP{half}"])
        return out, f"psP{half}"

    GBf = GB[:].rearrange("p a b c -> p (a b c)").bitcast(F32)
    p1s = [0]

    def p1_tmp():
        k = p1s[0]; p1s[0] ^= 1
        return (GBf[:, k * 512:k * 512 + 256], GBf[:, k * 512 + 256:k * 512 + 512], [f"GB{k}_0", f"GB{k}_1"], [f"GB{k}_2", f"GB{k}_3"])

    def rope_to(psrc, pres, NQ, pieces, dres, in_diff=False, tmp=None):
        t1, t2, r1, r2 = tmp if tmp is not None else next_tmp(in_diff)
        TT("dve", t1[:, 0:NQ], psrc, tabs[:, 0, 0:NQ], ALU.mult, [pres, "tabs"], r1)
        for (o0, i0) in ((0, 32), (32, 0), (64, 96), (96, 64)):
            TT("dve", t2[o0:o0 + 32, 0:NQ], psrc[i0:i0 + 32, :], tabs[i0:i0 + 32, 1, 0:NQ], ALU.mult, [pres, "tabs"], r2)
        for (p0, p1, dd) in pieces:
            TT("pool", dd, t1[p0:p1, 0:NQ], t2[p0:p1, 0:NQ], ALU.add, r1 + r2, [dres])

    def gate_tail(psrc, pres, NQ, dest, dres, in_diff):
        t1, t2, r1, r2 = next_tmp(in_diff)
        ee, rr = t1[:, 0:NQ], t2[:, 0:NQ]
        ACTV(ee, psrc, AF.Exp, [pres], r1, scale=-1.0)
        TS("dve", ee, ee, 1.0, ALU.add, r1, r1)
        RECIP(rr, ee, r1, r2)
        TT("dve", dest, psrc, rr, ALU.mult, [pres] + r2, [dres])

    def gate_to(col0, NQ, ntiles, dest, dres, in_diff=False):
        psrc, pres = proj_fm(Wq, "Wq", col0, NQ, ntiles)
        t1, t2, r1, r2 = next_tmp(in_diff)
        ee, rr = t1[:, 0:NQ], t2[:, 0:NQ]
        ACTV(ee, psrc, AF.Exp, [pres], r1, scale=-1.0)
        TS("dve", ee, ee, 1.0, ALU.add, r1, r1)
        RECIP(rr, ee, r1, r2)
        TT("dve", dest, psrc, rr, ALU.mult, [pres] + r2, [dres])

    def load_tabs(seq, col0, NQ):
        src = ropeP if seq["kind"] == "p" else ropeS
        DMA("sp", tabs[:, :, 0:NQ], src[:, :, col0:col0 + NQ], [], ["tabs"])

    sS = [0]
    wS = [0]
    bE = [0]

    def next_E():
        k = bE[0]; bE[0] = (bE[0] + 1) % 4
        return Bp[2 + k // 2][:, (k % 2) * 512:(k % 2) * 512 + 512], f"B{2 + k // 2}{'ab'[k % 2]}"

    PB_O = [(psTb[0], "psO0"), (psTb[1], "psO1")]
    psPT = [psP[:, 0:512].bitcast(BF16).rearrange("p (k r) -> p k r", k=8), psP[:, 512:1024].bitcast(BF16).rearrange("p (k r) -> p k r", k=8)]
    PB_P = [(psPT[0], "psP0"), (psPT[1], "psP1")]

    def phase1(seq, l):
        nb = seq["S"] // 128
        chunks = [[(t, 128), (t + 1, 128)] for t in range(0, nb, 2)] + [[(nb, NMETA)]]

        def pre_a(ci):
            fo = 2 * (ci % 2)
            h_loads(seq, l, chunks[ci], fo)
            h_compute(chunks[ci], fo, on_act=True)

        def pre_b(ci):
            fo = 2 * (ci % 2)
            h_transposes(chunks[ci], PB_O, fo, HT_MAIN_P1 if ci % 2 == 0 else HT_ALT, on_act=True)

        HT_MAIN_P1 = (hT, ["hT0", "hT1"])
        load_tabs(seq, chunks[0][0][0] * 128, sum(r for _, r in chunks[0]))
        pre_a(0)
        pre_b(0)
        for ci, tiles in enumerate(chunks):
            hbuf = HT_MAIN_P1 if ci % 2 == 0 else HT_ALT
            hb, hres = hbuf
            col0 = tiles[0][0] * 128
            NQ = sum(r for _, r in tiles)
            nt = len(tiles)
            cid = tiles[0][0]
            if ci + 1 < len(chunks):
                pre_a(ci + 1)
            for h in range(4):
                psrc, pres = proj_fm(Wkv, "Wkv", h * 128, NQ, nt, hbuf)
                rope_to(psrc, pres, NQ, [(0, 128, KTB[:, h, col0:col0 + NQ])], f"KTB{h}_{cid}", tmp=p1_tmp())
            psrc, pres = proj_fm(Wkv, "Wkv", 512, NQ, nt, hbuf)
            rope_to(psrc, pres, NQ, [(0, 128, KTA[:, col0:col0 + NQ])], f"KTA_{cid}", tmp=p1_tmp())
            if ci + 1 < len(chunks):
                pre_b(ci + 1)
            for ti, (tile_idx, Rr) in enumerate(tiles):
                half = pp[0]; pp[0] ^= 1
                out = psP[0:Rr, half * 512:(half + 1) * 512]
                for kc in range(8):
                    MM(out, hb[:, kc, ti * 128:ti * 128 + Rr], Wkv[:, kc, 640:1152], kc == 0, kc == 7, hres + ["Wkv"], [f"psP{half}"])
                ACOPY(VB[0:Rr, tile_idx, :], out, [f"psP{half}"], [f"VB_{tile_idx}"])
                half = pp[0]; pp[0] ^= 1
                out2 = psP[0:Rr, half * 512:half * 512 + 128]
                for kc in range(8):
                    MM(out2, hb[:, kc, ti * 128:ti * 128 + Rr], Wkv[:, kc, 1152:1280], kc == 0, kc == 7, hres + ["Wkv"], [f"psP{half}"])
                ACOPY(VA[0:Rr, tile_idx, :], out2, [f"psP{half}"], [f"VA_{tile_idx}"])
            if ci + 1 < len(chunks):
                nxt = chunks[ci + 1]
                load_tabs(seq, nxt[0][0] * 128, sum(r for _, r in nxt))

    def front_units(seq, l, tiles, slot, in_diff):
        col0 = tiles[0][0] * 128
        NQ = sum(r for _, r in tiles)
        nt = len(tiles)
        units = []
        units.append(lambda: (load_tabs(seq, col0, NQ), h_loads(seq, l, tiles)))
        if in_diff:
            units.append(lambda: h_compute_staged(tiles))
            units.append(lambda: None)
        else:
            units.append(lambda: h_compute(tiles, 0, on_act=True, scale_eng="dve"))
        units.append(lambda: h_transposes(tiles, PB_P if in_diff else PB_O))

        def qa(j):
            psrc, pres = proj_fm(Wq, "Wq", j * 128, NQ, nt)
            return lambda: rope_to(psrc, pres, NQ, [(0, 64, QTA[0:64, 0, j, 0:NQ]), (64, 128, QTA[64:128, 1, j, 0:NQ])],
                                   f"QTA{j}", in_diff)

        def qb(h):
            psrc, pres = proj_fm(Wq, "Wq", 512 + h * 128, NQ, nt)
            return lambda: rope_to(psrc, pres, NQ, [(0, 64, QTB[0:64, slot, h, 0, 0:NQ]), (64, 128, QTB[64:128, slot, h, 1, 0:NQ])],
                                   f"QTB{slot}_{h}", in_diff)

        def gt(col0, dest, dres):
            psrc, pres = proj_fm(Wq, "Wq", col0, NQ, nt)
            return lambda: gate_tail(psrc, pres, NQ, dest, dres, in_diff)
        for j in range(4):
            units.append(lambda j=j: qa(j))
        for j in range(4):
            units.append(lambda j=j: gt(1024 + j * 128, GA[:, j, 0:NQ], f"GA{j}"))
        for h in range(4):
            units.append(lambda h=h: qb(h))
        for h in range(4):
            units.append(lambda h=h: gt(1536 + h * 128, GB[:, slot, h, 0:NQ], f"GB{slot}_{h}"))
        return units

    def window_block(nb, S, qc0, QB, kl):
        W4 = 4 * QB
        items = [(g, ki) + tuple(kl[ki]) for g in range(2) for ki in range(len(kl))]
        st = {}

        def issue_S(i):
            g, ki, kt, M, kc0, mk = items[i]
            half = wS[0]; wS[0] ^= 1
            sout = psP[0:M, half * 512:half * 512 + W4]
            kres = f"KTA_{(kt // 2) * 2 if kt < nb else nb}"
            MM(sout, KTA[:, kc0:kc0 + M], QTA[:, g, :, qc0:qc0 + QB], True, mk is None, [kres] + [f"QTA{j}" for j in range(4)], [f"psP{half}"])
            if mk is not None:
                MM(sout, ident[0:M, 0:M], masks[0:M, mk:mk + QB].unsqueeze(1).broadcast_to([M, 4, QB]), False, True,
                   ["ident", "masks"], [f"psP{half}"])
            E, eres = next_E()
            ACTV(E[0:M, 0:W4], sout, AF.Exp, [f"psP{half}"], [eres], scale=SCALE)
            st[i] = (E, eres)

        def issue_AV(i):
            g, ki, kt, M, kc0, mk = items[i]
            E, eres = st.pop(i)
            first, last = ki == 0, ki == len(kl) - 1
            MM(psOb[g][:, 0:W4], VA[0:M, kt, :], E[0:M, 0:W4], first, last, [eres, f"VA_{kt}"], [f"psO{g}"])
            MM(psSb[0][:, 0:W4], onesg[0:M, g, :], E[0:M, 0:W4], first and g == 0, last and g == 1, [eres, "onesg"], ["psSum0"])

        n = len(items)
        issue_S(0)
        if n > 1:
            issue_S(1)
        for i in range(n):
            if i + 2 < n:
                issue_S(i + 2)
            issue_AV(i)
        def finalize():
            F3 = Fp[3]
            den = F3[:, 0:W4].rearrange("p (a b) -> p a b", a=4)
            TT("dve", den, psSb[0][:, 0:W4].rearrange("p (a b) -> p a b", a=4), sinkbc[:, :].unsqueeze(2).broadcast_to([128, 4, QB]),
               ALU.add, ["psSum0", "sinkbc"], RF3A)
            RECIP(F3[:, 0:W4], F3[:, 0:W4], RF3A, RF3A)
            for g in range(2):
                TT("dve", F3[g * 64:(g + 1) * 64, 512:512 + W4], psOb[g][g * 64:(g + 1) * 64, 0:W4], F3[g * 64:(g + 1) * 64, 0:W4],
                   ALU.mult, [f"psO{g}"] + RF3A, RF3B)
            TT("pool", oT[:, 0:4, qc0:qc0 + QB], F3[:, 512:512 + W4].rearrange("p (a b) -> p a b", a=4), GA[:, :, qc0:qc0 + QB], ALU.mult,
               RF3B + [f"GA{j}" for j in range(4)], ["oTA"])
        return finalize

    def window_blocks(seq, tiles):
        S = seq["S"]; nb = S // 128
        if tiles[0][0] == nb:
            return [lambda: window_block(nb, S, 0, NMETA, [(nb, NMETA, S, None), (0, 128, 0, 256)])]
        out = []
        for bi, (t, _) in enumerate(tiles):
            kl = []
            if t >= 1:
                kl.append((t - 1, 128, (t - 1) * 128, 0))
            kl.append((t, 128, t * 128, None))
            if t + 1 < nb:
                kl.append((t + 1, 128, (t + 1) * 128, 128))
            kl.append((nb, NMETA, S, None))
            out.append(lambda bi=bi, kl=kl: window_block(nb, S, bi * 128, 128, kl))
        return out

    def window_all(seq, tiles):
        for th in window_blocks(seq, tiles):
            th()()

    post2_done = [False]
    last_post2 = [None]

    def diff_all(nb, NQ, ktiles, slot, units, reload_x):
        W2 = 2 * NQ
        n = len(ktiles)
        pending = [None]
        tails = []
        reloaded = [False]
        next_unit = [1]
        units = list(units)
        total_steps = 4 * n
        stride = max(1, (total_steps - 2) // (len(units) + 1)) if units else 0
        step = [0]

        def post1(h):
            p = h % 2
            psO, psSum = psOb[p], psSb[p]
            F2, F3 = Fp[2], Fp[3]
            RECIP(F2[:, 0:W2], psSum[:, 0:W2], [f"psSum{p}"], RF2A)
            TT("dve", F2[:, 512:512 + W2], psO[:, 0:W2], F2[:, 0:W2], ALU.mult, [f"psO{p}"] + RF2A, RF2B)
            STT("dve", F3[:, 0:NQ], F2[:, 512 + NQ:512 + W2], small[:, C_NLAM:C_NLAM + 1], F2[:, 512:512 + NQ], ALU.mult, ALU.add,
                RF2B + ["nlam"], RF3A)
            TT("pool", F3[:, 256:256 + NQ], F3[:, 0:NQ], F3[:, 0:NQ], ALU.mult, RF3A, RF3A)

        def post2(h):
            p = h % 2
            psSum = psSb[p]
            F3 = Fp[3]
            post2_done[0] = True
            MM(psSum[:, 0:NQ], onesf[:, :], F3[:, 256:256 + NQ], True, True, RF3A + ["onesf"], [f"psSum{p}"])
            ACTV(F3[:, 512:512 + NQ], psSum[:, 0:NQ], AF.Ln, [f"psSum{p}", "small_init"], RF3B, scale=1.0 / 128,
                 bias=small[:, C_EPS:C_EPS + 1])
            ACTV(F3[:, 512:512 + NQ], F3[:, 512:512 + NQ], AF.Exp, RF3B, RF3B, scale=-0.5)
            STT("dve", F3[:, 768:768 + NQ], F3[:, 0:NQ], small[:, C_SGR:C_SGR + 1], F3[:, 512:512 + NQ], ALU.mult, ALU.mult,
                RF3A + RF3B + ["sg"], RF3B)
            TT("pool", oT[:, 4 + h, 0:NQ], F3[:, 768:768 + NQ], GB[:, slot, h, 0:NQ], ALU.mult, RF3B + [f"GB{slot}_{h}"], [f"oTB{h}"])

        K0 = min(max(5, n // 2), n - 1)
        steps = [(h, ki) for h in range(4) for ki in range(n)]
        st = {}

        def issue_S(i):
            h, ki = steps[i]
            kt, M, kc0 = ktiles[ki]
            half = sS[0]; sS[0] ^= 1
            sres = f"psS{half}"
            kres = f"KTB{h}_{(kt // 2) * 2 if kt < nb else nb}"
            MM(psS[0:M, half * 512:half * 512 + W2], KTB[:, h, kc0:kc0 + M], QTB[:, slot, h, :, 0:NQ], True, True,
               [kres, f"QTB{slot}_{h}"], [sres])
            E, eres = next_E()
            ACTV(E[0:M, 0:W2], psS[0:M, half * 512:half * 512 + W2], AF.Exp, [sres], [eres], scale=SCALE)
            st[i] = (E, eres)

        def issue_AV(i):
            h, ki = steps[i]
            p = h % 2
            kt, M, kc0 = ktiles[ki]
            E, eres = st.pop(i)
            first, last = ki == 0, ki == n - 1
            MM(psOb[p][:, 0:W2], VB[0:M, kt, h * 128:(h + 1) * 128], E[0:M, 0:W2], first, last, [eres, f"VB_{kt}"], [f"psO{p}"])
            MM(psSb[p][:, 0:W2], onesb[0:M, :], E[0:M, 0:W2], first, last, [eres, "onesb"], [f"psSum{p}"])

        ns = len(steps)
        issue_S(0)
        if ns > 1:
            issue_S(1)
        for i in range(ns):
            h, ki = steps[i]
            if i + 2 < ns:
                issue_S(i + 2)
            issue_AV(i)
            if ki == K0 and pending[0] is not None:
                post2(pending[0]); pending[0] = None
            step[0] += 1
            for tl in [t for t in tails if t[0] <= step[0]]:
                tails.remove(tl)
                nxt = tl[1]()
                if callable(nxt):
                    tails.append((step[0] + 2, nxt))
            if units and not tails and step[0] >= next_unit[0]:
                tail = units.pop(0)()
                next_unit[0] = step[0] + stride
                if callable(tail):
                    tails.append((step[0] + 2, tail))
            elif not units and not tails and not reloaded[0]:
                reload_x(); reloaded[0] = True
            if ki == n - 1:
                post1(h)
                pending[0] = h
        while tails:
            nxt = tails.pop(0)[1]()
            if callable(nxt):
                tails.insert(0, (0, nxt))
        while units:
            tail = units.pop(0)()
            while callable(tail):
                tail = tail()
        if not reloaded[0]:
            reload_x()
        last_post2[0] = (lambda hh=pending[0]: post2(hh))

    def outproj_mm(ti, Rr, f_list):
        c0 = ti * 128
        pt, pn = (psP, "psP") if ti == 0 else (psS, "psS")
        ores = ["oTA", "oTA", "oTA", "oTA", "oTB0", "oTB1", "oTB2", "oTB3"]
        for f in f_list:
            for hf in range(2):
                MM(pt[0:Rr, hf * 512:(hf + 1) * 512], oT[:, f, c0:c0 + Rr], Wo[:, f, hf * 512:(hf + 1) * 512], f == 0, f == 7,
                   [ores[f], "Wo"], [f"{pn}{hf}"])

    def outproj_post(seq, l, ti, tile_idx, Rr):
        pt, pn = (psP, "psP") if ti == 0 else (psS, "psS")
        pres = [f"{pn}0", f"{pn}1"]
        Fy = Fp[2 + ti]
        fy = (RF2A + RF2B) if ti == 0 else (RF3A + RF3B)
        ssc = C_SS2 + ti
        MEMSET("dve", col(ssc, Rr), 0.0, ["small_init"], [f"ssy{ti}"])
        ACTV(Fy[0:Rr, :], pt[0:Rr, :], AF.Square, pres + [f"ssy{ti}"], fy + [f"ssy{ti}"], accum_out=col(ssc, Rr))
        rstd_chain(ssc, C_LN2 + ti, C_RS2 + ti, Rr, 1.0 / D, f"y{ti}")
        STT("dve", Fy[0:Rr, :], pt[0:Rr, :], col(C_RS2 + ti, Rr), gpost[0:Rr, :], ALU.mult, ALU.mult,
            pres + [f"rsy{ti}", "gpost"], fy)
        TT("pool", Fy[0:Rr, :], Fy[0:Rr, :], Fp[ti][0:Rr, :], ALU.add, fy + FRES[ti], fy)
        dst, dres = x_dst(seq, l, tile_idx, Rr)
        DMA("pool", dst, Fy[0:Rr, :], fy, [dres] if dres else [])

    def phase2(seq, l):
        S = seq["S"]; nb = S // 128
        chunks = [[(t, 128), (t + 1, 128)] for t in range(0, nb, 2)]
        if l < NL - 1:
            chunks.append([(nb, NMETA)])
        ktiles = [(t, 128, t * 128) for t in range(nb)] + [(nb, NMETA, S)]
        for u in front_units(seq, l, chunks[0], 0, False):
            tail = u()
            while callable(tail):
                tail = tail()
        window_all(seq, chunks[0])
        for ci, tiles in enumerate(chunks):
            slot = ci % 2
            NQ = sum(r for _, r in tiles)
            units = []
            if ci + 1 < len(chunks):
                units = front_units(seq, l, chunks[ci + 1], 1 - slot, True)
            diff_all(nb, NQ, ktiles, slot, units, lambda tiles=tiles: h_loads(seq, l, tiles))
            if DBG and l == 0 and seq is seqs[0] and ci == 0:
                for nme, t, rs in (("oT", oT, ["oTA"] + [f"oTB{j}" for j in range(4)]),
                                   ("KTB", KTB, [f"KTB{hh}_{c}" for hh in range(4) for c in list(range(0, nb, 2)) + [nb]]),
                                   ("KTA", KTA, [f"KTA_{c}" for c in list(range(0, nb, 2)) + [nb]]),
                                   ("VB", VB, [f"VB_{c}" for c in range(nb + 1)]), ("VA", VA, [f"VA_{c}" for c in range(nb + 1)])):
                    DMA("pool", dbg[nme], t[:], rs, [])
            for ti, (tile_idx, Rr) in enumerate(tiles):
                outproj_mm(ti, Rr, range(7))
            last_post2[0]()
            for ti, (tile_idx, Rr) in enumerate(tiles):
                outproj_mm(ti, Rr, [7])
            for ti, (tile_idx, Rr) in enumerate(tiles):
                outproj_post(seq, l, ti, tile_idx, Rr)
            if ci + 1 < len(chunks):
                window_all(seq, chunks[ci + 1])

    for l in range(NL):
        if l == 0:
            load_wkv(0)
        load_wq_wo(l)
        load_params(l)
        for si, seq in enumerate(seqs):
            phase1(seq, l)
            if si == len(seqs) - 1 and l + 1 < NL:
                load_wkv(l + 1)
            phase2(seq, l)

    R.finalize()
    with ExitStack() as ss_:
        esems = {e: ss_.enter_context(nc.semaphore(f"sem_{e}")) for e in Rec.ENGS}
        dsems = {}
        for q, n in R.n_dma.items():
            for i in range(n):
                dsems[(q, i)] = ss_.enter_context(nc.semaphore(f"dsem_{q}{i}"))
        block = ss_.enter_context(nc.Block())

        @block.tensor
        def _(t):
            R.play("pe", t, esems, dsems)

        @block.scalar
        def _(s):
            R.play("act", s, esems, dsems)

        @block.vector
        def _(v):
            R.play("dve", v, esems, dsems)

        @block.gpsimd
        def _(g):
            R.play("pool", g, esems, dsems)

        @block.sync
        def _(sy):
            R.play("sp", sy, esems, dsems)
    es.close()
    return nc


def rope_table(S):
    LP = S + NMETA
    pos = np.concatenate([np.arange(S, dtype=np.float32) + NMETA, np.arange(NMETA, dtype=np.float32)])
    inv_freq = (1.0 / (ROPE_THETA ** (np.arange(0, 64, 2, dtype=np.float32) / 64.0))).astype(np.float32)
    ang = (pos[None, :] * inv_freq[:, None]).astype(np.float32)
    cos = np.cos(ang).astype(np.float32)
    sin = np.sin(ang).astype(np.float32)
    tab = np.zeros((128, 2, LP), np.float32)
    for p in range(128):
        f = p % 32
        tab[p, 0] = cos[f]
        tab[p, 1] = -sin[f] if (p % 64) >= 32 else sin[f]
    return tab


def const_tables():
    ident = np.eye(128, dtype=np.float32)
    k = np.arange(128)[:, None]
    q = np.arange(128)[None, :]
    NEG = -30000.0
    masks = np.zeros((128, 272), np.float32)
    masks[:, 0:128] = np.where(k >= q, 0.0, NEG)
    masks[:, 128:256] = np.where(k <= q, 0.0, NEG)
    u = np.arange(128)[:, None]
    m = np.arange(NMETA)[None, :]
    masks[:, 256:272] = np.where(u <= 112 + m, 0.0, NEG)
    return ident, masks


_CACHE = {}


def kernel(x_prompt, x_sample, meta_tokens, w_in, w_out, pre_norm_g, post_norm_g, sink_logits,
           lambda_q1, lambda_k1, lambda_q2, lambda_k2, subln_g):
    f = lambda a: np.ascontiguousarray(np.asarray(a, dtype=np.float32))
    x_prompt, x_sample = f(x_prompt), f(x_sample)
    NL = w_in.shape[0]
    BP, SPn = x_prompt.shape[0], x_prompt.shape[1]
    BS, SSn = x_sample.shape[0], x_sample.shape[1]
    nP, nS = BP // N_CORES, BS // N_CORES
    cfg = Cfg(n_layers=NL, n_prompt=nP, s_prompt=SPn, n_sample=nS, s_sample=SSn)
    nc = build_program(cfg)
    ident, masks = const_tables()
    common = dict(meta=f(meta_tokens), w_in=f(w_in), w_out=f(w_out), pre_g=f(pre_norm_g), post_g=f(post_norm_g),
                  sink=f(sink_logits), lq1=f(lambda_q1), lk1=f(lambda_k1), lq2=f(lambda_q2), lk2=f(lambda_k2),
                  subln=f(subln_g), ropeP=rope_table(SPn), ropeS=rope_table(SSn), cident=ident, cmasks=masks)
    in_maps = []
    for c in range(N_CORES):
        m = dict(common)
        m["xp"] = x_prompt[c * nP:(c + 1) * nP]
        m["xs"] = x_sample[c * nS:(c + 1) * nS]
        in_maps.append(m)
    res = run_bass_kernel_spmd(nc, in_maps, core_ids=list(range(N_CORES)))
    ypo = np.concatenate([np.asarray(r["yp"]) for r in res.results], axis=0).astype(np.float32)
    yso = np.concatenate([np.asarray(r["ys"]) for r in res.results], axis=0).astype(np.float32)
    return (ypo, yso)
```
